# Optimizing a Trainium2 kernel written in Bass

```python
import jax, jax.numpy as jnp
from jax import lax
import numpy as np

D_MODEL = 2048
BATCH = 2
SEQ = 4096
DEPTH = 4

HEAD_DIM = 128
MIX_WIDTH = D_MODEL
MOBA_HEADS = MIX_WIDTH // 2 // HEAD_DIM
MOBA_WIDTH = MOBA_HEADS * HEAD_DIM
MOBA_BLOCK = 256
MOBA_TOPK = 3
MOBA_Q_CHUNK = 32
DN_HEADS = MIX_WIDTH // 2 // HEAD_DIM
DN_DK = HEAD_DIM
DN_DV = HEAD_DIM
DN_QK_WIDTH = DN_HEADS * DN_DK
DN_WIDTH = DN_HEADS * DN_DV
DN_CONV_CH = 2 * DN_QK_WIDTH + DN_WIDTH
CONV_W = 4
DN_CHUNK = 64
AB_WIDTH = MOBA_WIDTH + DN_WIDTH
AB_IN = 3 * MOBA_WIDTH + DN_CONV_CH + DN_WIDTH + 2 * DN_HEADS
AB_SPLITS = (3 * MOBA_WIDTH, 3 * MOBA_WIDTH + DN_CONV_CH, 3 * MOBA_WIDTH + DN_CONV_CH + DN_WIDTH, 3 * MOBA_WIDTH + DN_CONV_CH + DN_WIDTH + DN_HEADS)
C_DK = 128
C_HEADS = MIX_WIDTH // C_DK
C_DV = MIX_WIDTH // C_HEADS
C_WIDTH = MIX_WIDTH
C_CHUNK = 64
MEM_TOKENS = 256
X_HEADS = 4
X_DIM = 128
X_WIDTH = X_HEADS * X_DIM
D_FF = 5504
N_EVEN = (DEPTH + 1) // 2
N_ODD = DEPTH // 2
EPS = 1e-6
F32 = jnp.float32

kernel_name = 'hybrid_moba_gdn_hgrn2_macaron'


def rmsnorm(x, g):
    xf = x.astype(F32)
    y = xf * lax.rsqrt(jnp.mean(xf * xf, axis=-1, keepdims=True) + EPS)
    return (y * g.astype(F32)).astype(x.dtype)


def l2norm(x):
    return x * lax.rsqrt(jnp.sum(x * x, axis=-1, keepdims=True) + EPS)


def swiglu(h, w_in, w_out):
    a, b = jnp.split(h @ w_in, 2, axis=-1)
    return (jax.nn.silu(a) * b) @ w_out


def causal_dwconv(x, w):
    width, ch = w.shape
    return lax.conv_general_dilated(x, w.astype(x.dtype)[:, None, :], window_strides=(1,), padding=[(width - 1, 0)], dimension_numbers=('NWC', 'WIO', 'NWC'), feature_group_count=ch)


def moba_attention(q, k, v):
    B, T, H, dh = q.shape
    nb = -(-T // MOBA_BLOCK)
    t_pad = nb * MOBA_BLOCK
    pad = ((0, 0), (0, t_pad - T), (0, 0), (0, 0))
    kh = jnp.pad(k, pad).transpose(0, 2, 1, 3).reshape(B, H, nb, MOBA_BLOCK, dh)
    vh = jnp.pad(v, pad).transpose(0, 2, 1, 3).reshape(B, H, nb, MOBA_BLOCK, dh)
    qh = q.transpose(0, 2, 1, 3)
    k_mean = jnp.mean(kh.astype(F32), axis=3)
    gate = jnp.einsum('bhtd,bhnd->bhtn', qh.astype(F32), k_mean)
    pos = jnp.arange(T)
    q_blk = pos // MOBA_BLOCK
    past = jnp.arange(nb)[None, :] < q_blk[:, None]
    gate = jnp.where(past, gate, -jnp.inf)
    kk = min(MOBA_TOPK, nb)
    _, top = lax.top_k(gate, kk)
    own = jnp.broadcast_to(q_blk[:, None], (B, H, T, 1)).astype(top.dtype)
    idx = jnp.concatenate([top, own], axis=-1)
    slot_ok = jnp.concatenate([top < q_blk[:, None], jnp.ones((B, H, T, 1), bool)], axis=-1)
    nq = T // MOBA_Q_CHUNK

    def to_q_chunks(a):
        a = a.reshape(B, H, nq, MOBA_Q_CHUNK, *a.shape[3:])
        return jnp.moveaxis(a, 2, 0)

    b_ix = jnp.arange(B)[:, None, None, None]
    h_ix = jnp.arange(H)[None, :, None, None]
    blk_off = jnp.arange(MOBA_BLOCK)
    scale = dh ** -0.5

    def attend(args):
        qc, ic, okc, pc = args
        ks = kh[b_ix, h_ix, ic]
        vs = vh[b_ix, h_ix, ic]
        s = jnp.einsum('bhqd,bhqnkd->bhqnk', qc, ks).astype(F32) * scale
        key_pos = ic[..., None] * MOBA_BLOCK + blk_off
        mask = okc[..., None] & (key_pos <= pc[:, None, None])
        s = jnp.where(mask, s, -jnp.inf)
        p = jax.nn.softmax(s.reshape(*s.shape[:3], -1), axis=-1).reshape(s.shape)
        return jnp.einsum('bhqnk,bhqnkd->bhqd', p.astype(vs.dtype), vs)

    out = lax.map(attend, (to_q_chunks(qh), to_q_chunks(idx), to_q_chunks(slot_ok), pos.reshape(nq, MOBA_Q_CHUNK)))
    out = jnp.moveaxis(out, 0, 2).reshape(B, H, T, dh).transpose(0, 2, 1, 3)
    return out.reshape(B, T, H * dh)


def gated_delta_rule(q, k, v, g, beta):
    B, T, H, dk = q.shape
    dv = v.shape[-1]
    C = DN_CHUNK
    n = T // C

    def chunks(a):
        a = a.reshape(B, n, C, H, *a.shape[3:])
        return jnp.moveaxis(a, 3, 1)

    q, k, v, g, beta = chunks(q), chunks(k), chunks(v), chunks(g), chunks(beta)
    G = jnp.cumsum(g, axis=-1)
    tril = jnp.tril(jnp.ones((C, C), bool))
    eye = jnp.eye(C, dtype=F32)
    gamma = jnp.exp(jnp.where(tril, G[..., :, None] - G[..., None, :], -jnp.inf))
    kb = k * beta[..., None]
    m = jnp.where(tril & (eye == 0), jnp.einsum('bhncd,bhnsd->bhncs', kb, k) * gamma, 0.0)
    t_inv = lax.linalg.triangular_solve(m + eye, jnp.broadcast_to(eye, m.shape), left_side=True, lower=True, unit_diagonal=True)
    u = t_inv @ (v * beta[..., None])
    w = t_inv @ (kb * jnp.exp(G)[..., None])
    a_qk = jnp.einsum('bhncd,bhnsd->bhncs', q, k) * gamma
    q_dec = q * jnp.exp(G)[..., None]
    k_dec = k * jnp.exp(G[..., -1:] - G)[..., None]
    g_last = jnp.exp(G[..., -1])
    xs = tuple(jnp.moveaxis(a, 2, 0) for a in (u, w, a_qk, q_dec, k_dec, g_last))

    def step(S, inp):
        u_c, w_c, a_c, qd, kd, gl = inp
        v_new = u_c - w_c @ S
        o = qd @ S + a_c @ v_new
        S = gl[..., None, None] * S + jnp.einsum('bhcd,bhcv->bhdv', kd, v_new)
        return S, o

    _, o = lax.scan(step, jnp.zeros((B, H, dk, dv), F32), xs)
    return o.transpose(1, 0, 3, 2, 4).reshape(B, T, H, dv)


def gla_chunked(q, k, v, log_f):
    B, T, H, dk = q.shape
    dv = v.shape[-1]
    C = C_CHUNK
    n = T // C

    def chunks(a):
        a = a.reshape(B, n, C, H, a.shape[-1])
        return a.transpose(1, 0, 3, 2, 4)

    qc, kc, vc = chunks(q), chunks(k), chunks(v)
    bc = jnp.cumsum(chunks(log_f), axis=3)
    tril = jnp.tril(jnp.ones((C, C), bool))[:, :, None]

    def step(S, inp):
        qi, ki, vi, bi = inp
        decay = jnp.exp(jnp.where(tril, bi[:, :, :, None, :] - bi[:, :, None, :, :], -jnp.inf))
        a = jnp.einsum('bhtd,bhsd,bhtsd->bhts', qi, ki, decay)
        o = a @ vi + (qi * jnp.exp(bi)) @ S
        b_last = bi[:, :, -1:, :]
        S = jnp.exp(b_last[:, :, 0, :])[..., None] * S + jnp.einsum('bhsd,bhsv->bhdv', ki * jnp.exp(b_last - bi), vi)
        return S, o

    _, o = lax.scan(step, jnp.zeros((B, H, dk, dv), F32), (qc, kc, vc, bc))
    return o.transpose(1, 0, 3, 2, 4).reshape(B, T, H, dv)


def moba_deltanet_mixer(h, w_in, conv_w, a_log, dt_bias, o_norm_g, w_out):
    B, T, _ = h.shape
    moba_qkv, dn_qkv, dn_gate, beta_raw, alpha_raw = jnp.split(h @ w_in, AB_SPLITS, axis=-1)
    mq, mk, mv = (a.reshape(B, T, MOBA_HEADS, HEAD_DIM) for a in jnp.split(moba_qkv, 3, axis=-1))
    y_a = moba_attention(mq, mk, mv)
    dn = jax.nn.silu(causal_dwconv(dn_qkv, conv_w)).astype(F32)
    dq, dk, dv = jnp.split(dn, (DN_QK_WIDTH, 2 * DN_QK_WIDTH), axis=-1)
    dq = l2norm(dq.reshape(B, T, DN_HEADS, DN_DK)) * DN_DK ** -0.5
    dk = l2norm(dk.reshape(B, T, DN_HEADS, DN_DK))
    dv = dv.reshape(B, T, DN_HEADS, DN_DV)
    beta = jax.nn.sigmoid(beta_raw.astype(F32))
    g = -jnp.exp(a_log.astype(F32)) * jax.nn.softplus(alpha_raw.astype(F32) + dt_bias.astype(F32))
    o = gated_delta_rule(dq, dk, dv, g, beta)
    y_b = rmsnorm(o, o_norm_g) * jax.nn.silu(dn_gate.astype(F32).reshape(B, T, DN_HEADS, DN_DV))
    y = jnp.concatenate([y_a, y_b.reshape(B, T, DN_WIDTH).astype(h.dtype)], axis=-1)
    return y @ w_out


def hgrn2_mixer(h, w_in, lb, o_norm_g, w_out):
    B, T, _ = h.shape
    q_raw, f_raw, i_raw, g_raw = jnp.split(h @ w_in, 4, axis=-1)
    f_raw = f_raw.astype(F32)
    lb = lb.astype(F32)
    log_f = jnp.log(lb + (1.0 - lb) * jax.nn.sigmoid(f_raw))
    k = (1.0 - lb) * jax.nn.sigmoid(-f_raw)
    q = jax.nn.silu(q_raw.astype(F32)) * C_DK ** -0.5
    heads = lambda a, d: a.reshape(B, T, C_HEADS, d)
    o = gla_chunked(heads(q, C_DK), heads(k, C_DK), heads(i_raw.astype(F32), C_DV), heads(log_f, C_DK))
    o = rmsnorm(o, o_norm_g) * jax.nn.silu(heads(g_raw.astype(F32), C_DV))
    return o.reshape(B, T, C_WIDTH).astype(h.dtype) @ w_out


def mem_cross_attention(h, mem_n, w_q, w_kv, w_o):
    B, T, _ = h.shape
    M = mem_n.shape[1]
    q = (h @ w_q).reshape(B, T, X_HEADS, X_DIM)
    k, v = jnp.split(mem_n @ w_kv, 2, axis=-1)
    k = k.reshape(B, M, X_HEADS, X_DIM)
    v = v.reshape(B, M, X_HEADS, X_DIM)
    s = jnp.einsum('bthd,bmhd->bhtm', q, k).astype(F32) * X_DIM ** -0.5
    p = jax.nn.softmax(s, axis=-1).astype(v.dtype)
    o = jnp.einsum('bhtm,bmhd->bthd', p, v).reshape(B, T, X_WIDTH)
    return o @ w_o


def setup_inputs(seed: int = 0) -> dict:
    key = jax.random.key(seed)
    ks = jax.random.split(key, 20)

    def nrm(k, shape, fan_in, mult=1.0):
        return jax.random.normal(k, shape, F32) * (mult * fan_in ** -0.5)

    def gain(k, shape):
        return 1.0 + 0.02 * jax.random.normal(k, shape, F32)

    dt = jnp.exp(jax.random.uniform(ks[9], (N_EVEN, DN_HEADS), F32, float(np.log(1e-3)), float(np.log(1e-1))))
    return {
        'x': jax.random.normal(ks[0], (BATCH, SEQ, D_MODEL), F32),
        'mem': jax.random.normal(ks[1], (BATCH, MEM_TOKENS, D_MODEL), F32),
        'norm_g': gain(ks[2], (DEPTH, 4, D_MODEL)),
        'mem_norm_g': gain(ks[3], (D_MODEL,)),
        'final_norm_g': gain(ks[4], (D_MODEL,)),
        'ffn_w_in': nrm(ks[5], (DEPTH, 2, D_MODEL, 2 * D_FF), D_MODEL),
        'ffn_w_out': nrm(ks[6], (DEPTH, 2, D_FF, D_MODEL), D_FF, 0.5),
        'ab_w_in': nrm(ks[7], (N_EVEN, D_MODEL, AB_IN), D_MODEL),
        'ab_conv_w': nrm(ks[8], (N_EVEN, CONV_W, DN_CONV_CH), CONV_W),
        'ab_a_log': jnp.log(jax.random.uniform(ks[10], (N_EVEN, DN_HEADS), F32, 1.0, 16.0)),
        'ab_dt_bias': dt + jnp.log(-jnp.expm1(-dt)),
        'ab_o_norm_g': gain(ks[11], (N_EVEN, DN_DV)),
        'ab_w_out': nrm(ks[12], (N_EVEN, AB_WIDTH, D_MODEL), AB_WIDTH, 0.5),
        'c_w_in': nrm(ks[13], (N_ODD, D_MODEL, 4 * C_WIDTH), D_MODEL),
        'c_lb_logits': 0.1 * jax.random.normal(ks[14], (DEPTH, C_WIDTH), F32),
        'c_o_norm_g': gain(ks[15], (N_ODD, C_DV)),
        'c_w_out': nrm(ks[16], (N_ODD, C_WIDTH, D_MODEL), C_WIDTH, 0.5),
        'x_w_q': nrm(ks[17], (DEPTH, D_MODEL, X_WIDTH), D_MODEL),
        'x_w_kv': nrm(ks[18], (DEPTH, D_MODEL, 2 * X_WIDTH), D_MODEL),
        'x_w_o': nrm(ks[19], (DEPTH, X_WIDTH, D_MODEL), X_WIDTH, 0.5),
    }


def reference(x, mem, norm_g, mem_norm_g, final_norm_g, ffn_w_in, ffn_w_out, ab_w_in, ab_conv_w, ab_a_log, ab_dt_bias, ab_o_norm_g, ab_w_out, c_w_in, c_lb_logits, c_o_norm_g, c_w_out, x_w_q, x_w_kv, x_w_o):
    p_lb = jax.nn.softmax(c_lb_logits.astype(F32), axis=0)
    lb_all = jnp.cumsum(p_lb, axis=0) - p_lb[0]
    mem_n = rmsnorm(mem, mem_norm_g)
    for l in range(DEPTH):
        x = x + 0.5 * swiglu(rmsnorm(x, norm_g[l, 0]), ffn_w_in[l, 0], ffn_w_out[l, 0])
        h = rmsnorm(x, norm_g[l, 1])
        if l % 2 == 0:
            e = l // 2
            x = x + moba_deltanet_mixer(h, ab_w_in[e], ab_conv_w[e], ab_a_log[e], ab_dt_bias[e], ab_o_norm_g[e], ab_w_out[e])
        else:
            o = l // 2
            x = x + hgrn2_mixer(h, c_w_in[o], lb_all[l], c_o_norm_g[o], c_w_out[o])
        x = x + mem_cross_attention(rmsnorm(x, norm_g[l, 2]), mem_n, x_w_q[l], x_w_kv[l], x_w_o[l])
        x = x + 0.5 * swiglu(rmsnorm(x, norm_g[l, 3]), ffn_w_in[l, 1], ffn_w_out[l, 1])
    return rmsnorm(x, final_norm_g)
```

```python
import numpy as np
from contextlib import ExitStack
import concourse.bass as bass
import concourse.mybir as mybir
from concourse.bass_utils import run_bass_kernel_spmd

F32 = mybir.dt.float32
BF16 = mybir.dt.bfloat16
AF = mybir.ActivationFunctionType
ALU = mybir.AluOpType
AX = mybir.AxisListType

D = 2048
DFF = 5504
T = 4096
B = 2
NCORE = 8
TPC = 1024
NTT = TPC // 128
EPS = 1e-6
AB_IN = 7184


class Tl:
    def __init__(self, h, name, nsub=1, psum=False):
        self.h = h
        self.name = name
        self.nsub = nsub
        self.psum = psum

    def __getitem__(self, k):
        return self.h[k]


class Ctx:
    SAME_ENG_SYNC = True

    def __init__(self, nc, es):
        self.nc = nc
        self.es = es
        self.eng = {"pe": nc.tensor, "act": nc.scalar, "dve": nc.vector, "pool": nc.gpsimd, "sp": nc.sync}
        self.sem = {}
        self.cnt = {}
        for e in ("pe", "act", "dve", "pool"):
            self.sem[e] = es.enter_context(nc.semaphore("s_" + e))
            self.cnt[e] = 0
        self.waited = {e: {} for e in self.eng}
        self.dq = {}
        for q in ("sp", "pool", "act"):
            n = 12 if q != "act" else 4
            self.dq[q] = {"sems": [es.enter_context(nc.semaphore("d_%s%d" % (q, i))) for i in range(n)],
                          "val": [0] * n, "i": 0}
        self.lastw = {}
        self.readers = {}
        self.psum_names = set()
        self.uid = 0
        self.ninstr = 0

    def sb(self, shape, dt, name=None, nsub=1):
        self.uid += 1
        name = (name or "t") + "_%d" % self.uid
        h = self.es.enter_context(self.nc.sbuf_tensor(name, list(shape), dt))
        return Tl(h, name, nsub)

    def ps(self, shape, dt, name=None):
        self.uid += 1
        name = (name or "p") + "_%d" % self.uid
        h = self.es.enter_context(self.nc.psum_tensor(name, list(shape), dt))
        self.psum_names.add(name)
        return Tl(h, name, 1, True)

    def dram(self, name, shape, dt, kind):
        h = self.nc.dram_tensor(name, list(shape), dt, kind=kind)
        return Tl(h.ap(), name, 1)

    def _keys(self, specs):
        ks = []
        for s in specs:
            if s is None:
                continue
            if isinstance(s, tuple):
                t, i = s
                ks.append((t.name, i))
            else:
                for i in range(s.nsub):
                    ks.append((s.name, i))
        return ks

    def _wait(self, e, tok):
        sem, val, te, sid = tok
        if te == e and (e in ("pe", "sp") or not self.SAME_ENG_SYNC):
            return
        w = self.waited[e]
        if w.get(sid, 0) >= val:
            return
        self.eng[e].wait_ge(sem, val)
        w[sid] = val

    def _deps(self, e, r, w):
        rk = self._keys(r)
        wk = self._keys(w)
        toks = {}
        def add(tok):
            if tok is None:
                return
            sid = tok[3]
            if sid not in toks or toks[sid][1] < tok[1]:
                toks[sid] = tok
        for k in rk:
            add(self.lastw.get(k))
            if k[0] in self.psum_names and e != "pe":
                for tok in self.readers.get(k, {}).values():
                    if tok[2] != "pe":
                        add(tok)
        for k in wk:
            add(self.lastw.get(k))
            for tok in self.readers.get(k, {}).values():
                add(tok)
        for tok in toks.values():
            self._wait(e, tok)
        return rk, wk

    def _commit(self, tok, rk, wk):
        for k in wk:
            self.lastw[k] = tok
            self.readers[k] = {}
        for k in rk:
            d = self.readers.setdefault(k, {})
            sid = tok[3]
            if sid not in d or d[sid][1] < tok[1]:
                d[sid] = tok

    def op(self, e, r, w, fn):
        rk, wk = self._deps(e, r, w)
        ins = fn()
        self.cnt[e] += 1
        ins.then_inc(self.sem[e], 1)
        self.ninstr += 1
        self._commit((self.sem[e], self.cnt[e], e, "c_" + e), rk, wk)
        return ins

    def pe(self, r, w, fn):
        return self.op("pe", r, w, fn)

    def act(self, r, w, fn):
        return self.op("act", r, w, fn)

    def dve(self, r, w, fn):
        return self.op("dve", r, w, fn)

    def pool(self, r, w, fn):
        return self.op("pool", r, w, fn)

    def dma(self, q, r, w, fn):
        rk, wk = self._deps(q, r, w)
        dq = self.dq[q]
        i = dq["i"]
        dq["i"] = (i + 1) % len(dq["sems"])
        sem = dq["sems"][i]
        sid = "d_%s%d" % (q, i)
        if dq["val"][i] > 0 and self.waited[q].get(sid, 0) < dq["val"][i]:
            self.eng[q].wait_ge(sem, dq["val"][i])
            self.waited[q][sid] = dq["val"][i]
        ins = fn(self.eng[q])
        ins.then_inc(sem, 16)
        dq["val"][i] += 16
        self.ninstr += 1
        self._commit((sem, dq["val"][i], "dma", sid), rk, wk)
        return ins

    def finish(self):
        for q, dq in self.dq.items():
            for i, sem in enumerate(dq["sems"]):
                if dq["val"][i] > 0:
                    self.nc.sync.wait_ge(sem, dq["val"][i])
        for e in ("pe", "act", "dve", "pool"):
            if self.cnt[e] > 0:
                self.nc.sync.wait_ge(self.sem[e], self.cnt[e])


class TokProg:
    def __init__(self, c):
        self.c = c
        nc = c.nc
        self.nc = nc
        self.ident_f = c.sb([128, 128], F32, "identf")
        self.ident_b = c.sb([128, 128], BF16, "identb")
        self.psb = [c.ps([128, 512], F32, "psb") for _ in range(6)]
        self.pst = [c.ps([128, 1024], BF16, "pst") for _ in range(2)]
        self.psi = 0
        self.pti = 0
        self.junk = c.sb([128, D], BF16, "junk")
        self.hn = [c.sb([128, D], BF16, "hn") for _ in range(2)]
        self.hni = 0
        self.gbc = c.sb([128, D], F32, "gbc")
        self.st = c.sb([128, 8], F32, "stat")
        self.junk2 = c.sb([128, D], F32, "junk2")
        self.stg = [c.sb([128, 512], F32, "stg") for _ in range(2)]
        self.stgi = 0
        self.wri = 0
        self.evi = 0

    def load_ident(self, ident_dram):
        c = self.c
        c.dma("sp", [], [self.ident_f], lambda e: e.dma_start(out=self.ident_f[:], in_=ident_dram[:, :]))
        c.dve([self.ident_f], [self.ident_b], lambda: self.nc.vector.tensor_copy(self.ident_b[:], self.ident_f[:]))

    def nps(self):
        p = self.psb[self.psi % len(self.psb)]
        self.psi += 1
        return p

    def npt(self):
        p = self.pst[self.pti % len(self.pst)]
        self.pti += 1
        return p

    def load_gain(self, g_ap):
        c = self.c
        c.dma("sp", [], [self.gbc], lambda e: e.dma_start(out=self.gbc[:], in_=g_ap.partition_broadcast(128)))

    def rmsnorm_rows(self, xt, out_ap_tile, out_ap, width=D, gbc=None):
        c, nc = self.c, self.nc
        gbc = gbc or self.gbc
        st = self.st
        c.act([xt], [self.junk, st], lambda: nc.scalar.activation(
            out=self.junk[:, :width], in_=xt[:, :width], func=AF.Square, accum_out=st[:, 0:1]))
        c.dve([st], [st], lambda: nc.vector.tensor_scalar(
            st[:, 1:2], st[:, 0:1], 1.0 / width, EPS, ALU.mult, ALU.add))
        c.act([st], [st], lambda: nc.scalar.sqrt(st[:, 3:4], st[:, 1:2]))
        c.dve([st], [st], lambda: nc.vector.reciprocal(st[:, 2:3], st[:, 3:4]))
        c.dve([xt, st, gbc], [out_ap_tile], lambda: nc.vector.scalar_tensor_tensor(
            out=out_ap, in0=xt[:, :width], scalar=st[:, 2:3], in1=gbc[:, :width], op0=ALU.mult, op1=ALU.mult))

    def norm_T(self, xt, hT, t):
        c, nc = self.c, self.nc
        hn = self.hn[self.hni % 2]
        self.hni += 1
        self.rmsnorm_rows(xt, hn, hn[:, :])
        for half in range(2):
            pt = self.npt()
            for kk in range(8):
                k = half * 8 + kk
                c.pe([hn, self.ident_b], [pt], lambda k=k, kk=kk: nc.tensor.transpose(
                    pt[:, kk * 128:(kk + 1) * 128], hn[:, k * 128:(k + 1) * 128], self.ident_b[:]))
            src = pt[:, :].rearrange("p (k n) -> p k n", k=8)
            dst = hT[:, half * 8:(half + 1) * 8, t * 128:(t + 1) * 128]
            if half == 0:
                c.act([pt], [(hT, t)], lambda: nc.scalar.copy(out=dst, in_=src))
            else:
                c.dve([pt], [(hT, t)], lambda: nc.vector.tensor_copy(dst, src))

    def ffn_alloc(self):
        c = self.c
        self.wa = [c.sb([128, 16, 256], BF16, "wa") for _ in range(2)]
        self.wb = [c.sb([128, 16, 256], BF16, "wb") for _ in range(2)]
        self.wo = c.sb([128, 4, D], BF16, "wo")
        self.actT = c.sb([128, 4, TPC], BF16, "actT")
        self.sA = [c.sb([128, 512], F32, "sA") for _ in range(2)]
        self.sAi = 0

    def ffn(self, xs, hT, w_in, w_out):
        c, nc = self.c, self.nc
        win = w_in.rearrange("(k p) n -> p k n", p=128)
        wout = w_out.rearrange("(j p) n -> p j n", p=128)
        NCH = DFF // 128
        hus = [(j, min(2, NCH - j)) for j in range(0, NCH, 2)]

        def load_hu(i):
            j0, n = hus[i]
            wa, wb = self.wa[i % 2], self.wb[i % 2]
            c.dma("pool", [], [wa], lambda e: e.dma_start(out=wa[:, :, :n * 128], in_=win[:, :, j0 * 128:(j0 + n) * 128]))
            c.dma("pool", [], [wb], lambda e: e.dma_start(out=wb[:, :, :n * 128], in_=win[:, :, DFF + j0 * 128:DFF + (j0 + n) * 128]))

        def load_wo(u):
            j0 = u * 4
            n = min(4, NCH - j0)
            c.dma("pool", [], [self.wo], lambda e: e.dma_start(out=self.wo[:, :n, :], in_=wout[:, j0:j0 + n, :]))

        load_hu(0)
        for i, (j0, n) in enumerate(hus):
            u = i // 2
            if i + 1 < len(hus):
                load_hu(i + 1)
            if i % 2 == 0:
                load_wo(u)
            wa, wb = self.wa[i % 2], self.wb[i % 2]
            for jj in range(n):
                j4 = (i % 2) * 2 + jj
                for th in range(2):
                    pA, pB = self.nps(), self.nps()
                    for (pp, ww) in ((pA, wa), (pB, wb)):
                        for k in range(16):
                            c.pe([ww, hT], [pp], lambda pp=pp, ww=ww, k=k: nc.tensor.matmul(
                                pp[:, :], ww[:, k, jj * 128:(jj + 1) * 128], hT[:, k, th * 512:(th + 1) * 512],
                                start=(k == 0), stop=(k == 15)))
                    sA = self.sA[self.sAi % 2]
                    self.sAi += 1
                    c.act([pA], [sA], lambda: nc.scalar.activation(out=sA[:, :], in_=pA[:, :], func=AF.Silu))
                    c.dve([sA, pB], [self.actT], lambda: nc.vector.tensor_tensor(
                        self.actT[:, j4, th * 512:(th + 1) * 512], sA[:, :], pB[:, :], ALU.mult))
            if i % 2 == 1 or i == len(hus) - 1:
                nj = min(4, NCH - u * 4)
                for cb in range(4):
                    for t in range(NTT):
                        pp = self.nps()
                        for j4 in range(nj):
                            c.pe([self.actT, self.wo], [pp], lambda j4=j4, pp=pp: nc.tensor.matmul(
                                pp[:, :], self.actT[:, j4, t * 128:(t + 1) * 128], self.wo[:, j4, cb * 512:(cb + 1) * 512],
                                start=(j4 == 0), stop=(j4 == nj - 1)))
                        xs_t = xs[t]
                        c.dve([pp, xs_t], [xs_t], lambda pp=pp, xs_t=xs_t: nc.vector.scalar_tensor_tensor(
                            out=xs_t[:, cb * 512:(cb + 1) * 512], in0=pp[:, :], scalar=0.5,
                            in1=xs_t[:, cb * 512:(cb + 1) * 512], op0=ALU.mult, op1=ALU.add))


    def wring(self):
        r = [self.wa[0], self.wb[0], self.wa[1], self.wb[1]]
        w = r[self.wri % 4]
        self.wri += 1
        return w

    def stage(self):
        s = self.stg[self.stgi % len(self.stg)]
        self.stgi += 1
        return s

    def evac(self, ps_ap, ps_t, dst_ap, dst_t):
        c, nc = self.c, self.nc
        self.evi += 1
        if self.evi % 2 == 0:
            c.act([ps_t], [dst_t], lambda: nc.scalar.copy(out=dst_ap, in_=ps_ap))
        else:
            c.dve([ps_t], [dst_t], lambda: nc.vector.tensor_copy(dst_ap, ps_ap))

    def plain_T(self, yt, hT, t):
        c, nc = self.c, self.nc
        hn = self.hn[self.hni % 2]
        self.hni += 1
        c.dve([yt], [hn], lambda: nc.vector.tensor_copy(hn[:, :], yt[:, :]))
        self._T16(hn, hT, t)

    def _T16(self, hn, hT, t, nk=16):
        c, nc = self.c, self.nc
        for half in range((nk + 7) // 8):
            pt = self.npt()
            n8 = min(8, nk - half * 8)
            for kk in range(n8):
                k = half * 8 + kk
                c.pe([hn, self.ident_b], [pt], lambda k=k, kk=kk: nc.tensor.transpose(
                    pt[:, kk * 128:(kk + 1) * 128], hn[:, k * 128:(k + 1) * 128], self.ident_b[:]))
            src = pt[:, :n8 * 128].rearrange("p (k n) -> p k n", k=n8)
            dst = hT[:, half * 8:half * 8 + n8, t * 128:(t + 1) * 128]
            self.evac(src, pt, dst, (hT, t))

    def inproj(self, hT, W, segs, pf, pt):
        c, nc = self.c, self.nc
        Wv = W.rearrange("(k p) n -> p k n", p=128)
        blocks = []
        for (c0, c1, mode, off) in segs:
            cc = c0
            while cc < c1:
                n = min(256, c1 - cc)
                blocks.append((cc, n, mode, off + cc - c0))
                cc += n

        def load(bi):
            cc, n, mode, off = blocks[bi]
            w = self.wring()
            c.dma("pool", [], [w], lambda e: e.dma_start(out=w[:, :, :n], in_=Wv[:, :, cc:cc + n]))
            return w
        wnext = load(0)
        for bi, (cc, n, mode, off) in enumerate(blocks):
            w = wnext
            if bi + 1 < len(blocks):
                wnext = load(bi + 1)
            if mode == "fm":
                for j in range(n // 128):
                    for th in range(2):
                        pp = self.nps()
                        for k in range(16):
                            c.pe([w, hT], [pp], lambda k=k, pp=pp: nc.tensor.matmul(
                                pp[:, :], w[:, k, j * 128:(j + 1) * 128], hT[:, k, th * 512:(th + 1) * 512],
                                start=(k == 0), stop=(k == 15)))
                        sg = self.stage()
                        self.evac(pp[:, :], pp, sg[:, :], sg)
                        c.dma("sp", [sg], [pf], lambda e, sg=sg: e.dma_start(
                            out=pf[off + j * 128:off + (j + 1) * 128, th * 512:(th + 1) * 512], in_=sg[:, :]))
            else:
                for t in range(NTT):
                    pp = self.nps()
                    for k in range(16):
                        c.pe([w, hT], [pp], lambda k=k, pp=pp: nc.tensor.matmul(
                            pp[:, :n], hT[:, k, t * 128:(t + 1) * 128], w[:, k, :n],
                            start=(k == 0), stop=(k == 15)))
                    sg = self.stage()
                    self.evac(pp[:, :n], pp, sg[:, :n], sg)
                    c.dma("sp", [sg], [pt], lambda e, sg=sg: e.dma_start(
                        out=pt[t * 128:(t + 1) * 128, off:off + n], in_=sg[:, :n]))

    def linear_res(self, xs, hT, W, N=D):
        c, nc = self.c, self.nc
        Wv = W.rearrange("(k p) n -> p k n", p=128)
        nb = N // 256

        def load(bi):
            w = self.wring()
            c.dma("pool", [], [w], lambda e: e.dma_start(out=w[:, :, :], in_=Wv[:, :, bi * 256:(bi + 1) * 256]))
            return w
        wnext = load(0)
        for bi in range(nb):
            w = wnext
            if bi + 1 < nb:
                wnext = load(bi + 1)
            for t in range(NTT):
                pp = self.nps()
                for k in range(16):
                    c.pe([w, hT], [pp], lambda k=k, pp=pp: nc.tensor.matmul(
                        pp[:, :256], hT[:, k, t * 128:(t + 1) * 128], w[:, k, :],
                        start=(k == 0), stop=(k == 15)))
                xt = xs[t]
                c.dve([pp, xt], [xt], lambda pp=pp, xt=xt: nc.vector.tensor_tensor(
                    xt[:, bi * 256:(bi + 1) * 256], pp[:, :256], xt[:, bi * 256:(bi + 1) * 256], ALU.add))

    def xattn(self, xs, hT, mem, mem_g, w_q, w_kv, w_o):
        c, nc = self.c, self.nc
        scale = 128 ** -0.5
        memT = c.sb([128, 16, 256], BF16, "memT", nsub=2)
        self.load_gain(mem_g)
        mt = self.junk2
        for t in range(2):
            c.dma("sp", [], [mt], lambda e, t=t: e.dma_start(out=mt[:, :], in_=mem[t * 128:(t + 1) * 128, :]))
            hn = self.hn[self.hni % 2]
            self.hni += 1
            self.rmsnorm_rows(mt, hn, hn[:, :])
            self._T16(hn, memT, t)
        kT = c.sb([128, 4, 256], BF16, "xkT")
        vv = c.sb([128, 2, 512], BF16, "xv")
        qT = self.actT
        Wkv = w_kv.rearrange("(k p) n -> p k n", p=128)
        Wq = w_q.rearrange("(k p) n -> p k n", p=128)
        for bi in range(2):
            w = self.wring()
            c.dma("pool", [], [w], lambda e, w=w: e.dma_start(out=w[:, :, :], in_=Wkv[:, :, bi * 256:(bi + 1) * 256]))
            for j in range(2):
                pp = self.nps()
                for k in range(16):
                    c.pe([w, memT], [pp], lambda k=k, pp=pp, w=w: nc.tensor.matmul(
                        pp[:, :256], w[:, k, j * 128:(j + 1) * 128], memT[:, k, :], start=(k == 0), stop=(k == 15)))
                self.evac(pp[:, :256], pp, kT[:, bi * 2 + j, :], kT)
        for bi in range(2):
            w = self.wring()
            c.dma("pool", [], [w], lambda e, w=w: e.dma_start(out=w[:, :, :], in_=Wkv[:, :, 512 + bi * 256:512 + (bi + 1) * 256]))
            for t in range(2):
                pp = self.nps()
                for k in range(16):
                    c.pe([w, memT], [pp], lambda k=k, pp=pp, w=w: nc.tensor.matmul(
                        pp[:, :256], memT[:, k, t * 128:(t + 1) * 128], w[:, k, :], start=(k == 0), stop=(k == 15)))
                self.evac(pp[:, :256], pp, vv[:, t, bi * 256:(bi + 1) * 256], vv)
        for bi in range(2):
            w = self.wring()
            c.dma("pool", [], [w], lambda e, w=w: e.dma_start(out=w[:, :, :], in_=Wq[:, :, bi * 256:(bi + 1) * 256]))
            for j in range(2):
                for th in range(2):
                    pp = self.nps()
                    for k in range(16):
                        c.pe([w, hT], [pp], lambda k=k, pp=pp, w=w: nc.tensor.matmul(
                            pp[:, :], w[:, k, j * 128:(j + 1) * 128], hT[:, k, th * 512:(th + 1) * 512],
                            start=(k == 0), stop=(k == 15)))
                    self.evac(pp[:, :], pp, qT[:, bi * 2 + j, th * 512:(th + 1) * 512], qT)
        Wo = w_o.rearrange("(k p) n -> p k n", p=128)
        c.dma("pool", [], [self.wo], lambda e: e.dma_start(out=self.wo[:, :, :], in_=Wo[:, :, :]))
        P = c.sb([128, 256], BF16, "xP")
        PT = c.sb([128, 2, 128], BF16, "xPT")
        osb = c.sb([128, 512], BF16, "xo")
        oT = c.sb([128, 4, 128], BF16, "xoT")
        st = c.sb([128, 8], F32, "xst")
        for t in range(NTT):
            for h in range(4):
                pp = self.nps()
                c.pe([qT, kT], [pp], lambda pp=pp: nc.tensor.matmul(
                    pp[:, :256], qT[:, h, t * 128:(t + 1) * 128], kT[:, h, :], start=True, stop=True))
                c.dve([pp], [st], lambda pp=pp: nc.vector.reduce_max(st[:, 0:1], pp[:, :256], axis=AX.X))
                c.dve([st], [st], lambda: nc.vector.tensor_scalar(st[:, 1:2], st[:, 0:1], -scale, None, ALU.mult))
                c.act([pp, st], [P, st], lambda pp=pp: nc.scalar.activation(
                    out=P[:, :], in_=pp[:, :256], func=AF.Exp, bias=st[:, 1:2], scale=scale, accum_out=st[:, 2:3]))
                c.dve([st], [st], lambda: nc.vector.reciprocal(st[:, 3:4], st[:, 2:3]))
                ptp = self.npt()
                for mc in range(2):
                    c.pe([P, self.ident_b], [ptp], lambda mc=mc, ptp=ptp: nc.tensor.transpose(
                        ptp[:, mc * 128:(mc + 1) * 128], P[:, mc * 128:(mc + 1) * 128], self.ident_b[:]))
                c.act([ptp], [PT], lambda ptp=ptp: nc.scalar.copy(
                    out=PT[:, :, :], in_=ptp[:, :256].rearrange("p (k n) -> p k n", k=2)))
                po = self.nps()
                for mc in range(2):
                    c.pe([PT, vv], [po], lambda mc=mc, po=po: nc.tensor.matmul(
                        po[:, :128], PT[:, mc, :], vv[:, mc, h * 128:(h + 1) * 128], start=(mc == 0), stop=(mc == 1)))
                c.dve([po, st], [osb], lambda po=po: nc.vector.tensor_scalar(
                    osb[:, h * 128:(h + 1) * 128], po[:, :128], st[:, 3:4], None, ALU.mult))
            ptp = self.npt()
            for h in range(4):
                c.pe([osb, self.ident_b], [ptp], lambda h=h, ptp=ptp: nc.tensor.transpose(
                    ptp[:, h * 128:(h + 1) * 128], osb[:, h * 128:(h + 1) * 128], self.ident_b[:]))
            c.act([ptp], [oT], lambda ptp=ptp: nc.scalar.copy(
                out=oT[:, :, :], in_=ptp[:, :512].rearrange("p (k n) -> p k n", k=4)))
            xt = xs[t]
            for cb in range(4):
                pp = self.nps()
                for h in range(4):
                    c.pe([oT, self.wo], [pp], lambda h=h, pp=pp: nc.tensor.matmul(
                        pp[:, :], oT[:, h, :], self.wo[:, h, cb * 512:(cb + 1) * 512], start=(h == 0), stop=(h == 3)))
                c.dve([pp, xt], [xt], lambda pp=pp: nc.vector.tensor_tensor(
                    xt[:, cb * 512:(cb + 1) * 512], pp[:, :], xt[:, cb * 512:(cb + 1) * 512], ALU.add))


def new_nc():
    return bass.Bass("TRN2", target_bir_lowering=False)


EVEN_SEGS = [(0, 2048, "fm", 0), (3072, 6144, "fm", 2048),
             (2048, 3072, "tm", 0), (6144, 7168, "tm", 1024), (7168, 7184, "tm", 2048)]
ODD_SEGS = [(0, 2048, "fm", 0), (2048, 8192, "tm", 0)]
NF = {"even": 5120, "odd": 2048}
NT = {"even": 2064, "odd": 6144}
NWIN = {"even": AB_IN, "odd": 8192}


def build_tok(mix, pre, final):
    nc = new_nc()
    es = ExitStack()
    with es:
        c = Ctx(nc, es)
        di = lambda n, sh: c.dram(n, sh, F32, "ExternalInput")
        x = di("x", [TPC, D])
        ident = di("ident", [128, 128])
        if mix:
            y = di("y", [TPC, D]); w_mo = di("w_mo", [D, D]); mem = di("mem", [256, D]); mem_g = di("mem_g", [D])
            w_q = di("w_q", [D, 512]); w_kv = di("w_kv", [D, 1024]); w_o = di("w_o", [512, D])
            g2 = di("g2", [D]); g3 = di("g3", [D]); f1_in = di("f1_in", [D, 2 * DFF]); f1_out = di("f1_out", [DFF, D])
        if pre:
            g0 = di("g0", [D]); g1 = di("g1", [D]); f0_in = di("f0_in", [D, 2 * DFF]); f0_out = di("f0_out", [DFF, D])
            w_in = di("w_in", [D, NWIN[pre]])
            xo = c.dram("xo", [TPC, D], F32, "ExternalOutput")
            pf = c.dram("pf", [NF[pre], TPC], F32, "ExternalOutput")
            pt = c.dram("pt", [TPC, NT[pre]], F32, "ExternalOutput")
        if final:
            gf = di("gf", [D])
            out = c.dram("out", [TPC, D], F32, "ExternalOutput")
        tp = TokProg(c)
        tp.load_ident(ident)
        tp.ffn_alloc()
        xs = [c.sb([128, D], F32, "x") for _ in range(NTT)]
        hT = c.sb([128, 16, TPC], BF16, "hT", nsub=NTT)
        for t in range(NTT):
            c.dma("sp", [], [xs[t]], lambda e, t=t: e.dma_start(out=xs[t][:, :], in_=x[t * 128:(t + 1) * 128, :]))
        if mix:
            for t in range(NTT):
                yt = tp.junk2
                c.dma("sp", [], [yt], lambda e, t=t: e.dma_start(out=yt[:, :], in_=y[t * 128:(t + 1) * 128, :]))
                tp.plain_T(yt, hT, t)
            tp.linear_res(xs, hT, w_mo.h)
            tp.load_gain(g2.h)
            for t in range(NTT):
                tp.norm_T(xs[t], hT, t)
            tp.xattn(xs, hT, mem.h, mem_g.h, w_q.h, w_kv.h, w_o.h)
            tp.load_gain(g3.h)
            for t in range(NTT):
                tp.norm_T(xs[t], hT, t)
            tp.ffn(xs, hT, f1_in.h, f1_out.h)
        if pre:
            tp.load_gain(g0.h)
            for t in range(NTT):
                tp.norm_T(xs[t], hT, t)
            tp.ffn(xs, hT, f0_in.h, f0_out.h)
            for t in range(NTT):
                c.dma("sp", [xs[t]], [xo], lambda e, t=t: e.dma_start(out=xo[t * 128:(t + 1) * 128, :], in_=xs[t][:, :]))
            tp.load_gain(g1.h)
            for t in range(NTT):
                tp.norm_T(xs[t], hT, t)
            tp.inproj(hT, w_in.h, EVEN_SEGS if pre == "even" else ODD_SEGS, pf, pt)
        if final:
            tp.load_gain(gf.h)
            for t in range(NTT):
                ot = tp.junk2
                tp.rmsnorm_rows(xs[t], ot, ot[:, :])
                c.dma("sp", [ot], [out], lambda e, t=t: e.dma_start(out=out[t * 128:(t + 1) * 128, :], in_=ot[:, :]))
        c.finish()
    return nc, c


def build_ffn_test():
    nc = new_nc()
    es = ExitStack()
    with es:
        c = Ctx(nc, es)
        x = c.dram("x", [TPC, D], F32, "ExternalInput")
        g = c.dram("g", [D], F32, "ExternalInput")
        w_in = c.dram("w_in", [D, 2 * DFF], F32, "ExternalInput")
        w_out = c.dram("w_out", [DFF, D], F32, "ExternalInput")
        ident = c.dram("ident", [128, 128], F32, "ExternalInput")
        y = c.dram("y", [TPC, D], F32, "ExternalOutput")
        tp = TokProg(c)
        tp.load_ident(ident)
        tp.ffn_alloc()
        xs = [c.sb([128, D], F32, "x") for _ in range(NTT)]
        hT = c.sb([128, 16, TPC], BF16, "hT", nsub=NTT)
        tp.load_gain(g.h)
        for t in range(NTT):
            c.dma("sp", [], [xs[t]], lambda e, t=t: e.dma_start(out=xs[t][:, :], in_=x[t * 128:(t + 1) * 128, :]))
        for t in range(NTT):
            tp.norm_T(xs[t], hT, t)
        tp.ffn(xs, hT, w_in.h, w_out.h)
        for t in range(NTT):
            c.dma("sp", [xs[t]], [y], lambda e, t=t: e.dma_start(out=y[t * 128:(t + 1) * 128, :], in_=xs[t][:, :]))
        c.finish()
    return nc, c


class MixProg:
    def __init__(self, c, cst):
        self.c = c
        nc = c.nc
        self.nc = nc
        self.K = c.sb([128, 5, 128], F32, "consts")
        c.dma("sp", [], [self.K], lambda e: e.dma_start(out=self.K[:, :, :], in_=cst.rearrange("k p n -> p k n")))
        self.identf = self.K[:, 0, :]
        self.U1 = self.K[:, 1, :]
        self.U2 = self.K[:, 2, :]
        self.ones = self.K[:, 3, :]
        self.Kb = c.sb([128, 5, 128], BF16, "constsb")
        c.dve([self.K], [self.Kb], lambda: nc.vector.tensor_copy(self.Kb[:, :, :], self.K[:, :, :]))
        self.identb = self.Kb[:, 0, :]
        self.triTb = self.Kb[:, 4, :]
        self.pb = [c.ps([128, 512], F32, "pm") for _ in range(8)]
        self.prings = {"all": [list(range(8)), 0]}
        self.rings = {}

    def nps(self, ring="all"):
        r = self.prings[ring]
        p = self.pb[r[0][r[1] % len(r[0])]]
        r[1] += 1
        return p

    def ring(self, key, shape, dt, n=2):
        if key not in self.rings:
            self.rings[key] = [[self.c.sb(shape, dt, key) for _ in range(n)], 0]
        r = self.rings[key]
        t = r[0][r[1] % n]
        r[1] += 1
        return t

    def rms_gate_out(self, o_ps, gbc, gt, y_dram, row0, col0, tag):
        for _ in self.rms_gate_out_g(o_ps, gbc, gt, y_dram, row0, col0, tag):
            pass

    def rms_gate_out_g(self, o_ps, gbc, gt, y_dram, row0, col0, tag):
        c, nc = self.c, self.nc
        st = self.ring(tag + "st", [128, 8], F32, 3)
        jk = self.ring(tag + "jk", [128, 128], F32, 2)
        c.act([o_ps], [jk, st], lambda: nc.scalar.activation(
            out=jk[:, :], in_=o_ps[:, :128], func=AF.Square, accum_out=st[:, 0:1]))
        c.dve([st], [st], lambda: nc.vector.tensor_scalar(st[:, 1:2], st[:, 0:1], 1.0 / 128, EPS, ALU.mult, ALU.add))
        yield
        c.act([st], [st], lambda: nc.scalar.sqrt(st[:, 3:4], st[:, 1:2]))
        c.dve([st], [st], lambda: nc.vector.reciprocal(st[:, 2:3], st[:, 3:4]))
        on = self.ring(tag + "on", [128, 128], F32, 2)
        c.dve([o_ps, st, gbc], [on], lambda: nc.vector.scalar_tensor_tensor(
            out=on[:, :], in0=o_ps[:, :128], scalar=st[:, 2:3], in1=gbc[:, :], op0=ALU.mult, op1=ALU.mult))
        yield
        sg = self.ring(tag + "sg", [128, 128], F32, 2)
        c.act([gt], [sg], lambda: nc.scalar.activation(out=sg[:, :], in_=gt[:, :], func=AF.Silu))
        yt = self.ring(tag + "yt", [128, 128], F32, 3)
        c.dve([on, sg], [yt], lambda: nc.vector.tensor_tensor(yt[:, :], on[:, :], sg[:, :], ALU.mult))
        c.dma("sp", [yt], [y_dram], lambda e: e.dma_start(out=y_dram[row0:row0 + 128, col0:col0 + 128], in_=yt[:, :]))

    def hgrn2(self, qT, f, iv, gg, lg, coef, ong, y, nheads=4):
        c, nc = self.c, self.nc
        scale = 128 ** -0.5
        NR = T // 128
        gbc = c.sb([128, 128], F32, "hg_gbc")
        c.dma("sp", [], [gbc], lambda e: e.dma_start(out=gbc[:, :], in_=ong.partition_broadcast(128)))
        cf = c.sb([128, 4], F32, "hg_coef")
        c.dma("sp", [], [cf], lambda e: e.dma_start(out=cf[:, :], in_=coef.partition_broadcast(128)))
        hs = []
        for h in range(nheads):
            d = {}
            lgt = c.sb([128, 4], F32, "lgt")
            c.dma("sp", [], [lgt], lambda e, h=h, lgt=lgt: e.dma_start(out=lgt[:, :], in_=lg[h]))
            st = c.sb([128, 8], F32, "lbst")
            c.dve([lgt], [st], lambda: nc.vector.reduce_max(st[:, 0:1], lgt[:, :], axis=AX.X))
            c.dve([st], [st], lambda: nc.vector.tensor_scalar(st[:, 1:2], st[:, 0:1], -1.0, None, ALU.mult))
            ee = c.sb([128, 4], F32, "lbe")
            c.act([lgt, st], [ee, st], lambda: nc.scalar.activation(
                out=ee[:, :], in_=lgt[:, :], func=AF.Exp, bias=st[:, 1:2], scale=1.0, accum_out=st[:, 2:3]))
            c.dve([st], [st], lambda: nc.vector.reciprocal(st[:, 3:4], st[:, 2:3]))
            c.dve([ee, cf], [ee], lambda: nc.vector.tensor_tensor(ee[:, :], ee[:, :], cf[:, :], ALU.mult))
            c.dve([ee], [st], lambda: nc.vector.reduce_sum(st[:, 4:5], ee[:, :], axis=AX.X))
            c.dve([st], [st], lambda: nc.vector.tensor_tensor(st[:, 5:6], st[:, 4:5], st[:, 3:4], ALU.mult))
            lbc = c.sb([128, 128], F32, "lbcB")
            lbB = c.sb([128, 128], F32, "lbB")
            omlB = c.sb([128, 128], F32, "omlB")
            S = c.sb([128, 128], F32, "S")
            Sb = c.sb([128, 128], BF16, "Sb")
            c.dve([st, self.K], [lbc], lambda: nc.vector.tensor_scalar(lbc[:, :], self.ones, st[:, 5:6], None, ALU.mult))
            pp = self.nps()
            c.pe([lbc, self.K], [pp], lambda: nc.tensor.matmul(pp[:, :128], lbc[:, :], self.identf, start=True, stop=True))
            c.act([pp], [lbB], lambda: nc.scalar.copy(out=lbB[:, :], in_=pp[:, :128]))
            c.dve([pp], [omlB], lambda: nc.vector.tensor_scalar(omlB[:, :], pp[:, :128], -1.0, 1.0, ALU.mult, ALU.add))
            c.dve([], [S], lambda: nc.vector.memset(S[:, :], 0.0))
            c.dve([], [Sb], lambda: nc.vector.memset(Sb[:, :], 0.0))
            d.update(lbB=lbB, omlB=omlB, S=S, Sb=Sb)
            hs.append(d)

        def prep(r, h, out):
            d = hs[h]
            tg = "h%d" % h
            rows = slice(r * 128, (r + 1) * 128)
            cols = slice(h * 128, (h + 1) * 128)
            ft = self.ring(tg + "ft", [128, 128], F32)
            vt = self.ring(tg + "vt", [128, 128], BF16)
            gt = self.ring(tg + "gt", [128, 128], F32)
            qt = self.ring(tg + "qt", [128, 128], F32)
            c.dma("sp", [], [ft], lambda e: e.dma_start(out=ft[:, :], in_=f[rows, cols]))
            c.dma("pool", [], [vt], lambda e: e.dma_start(out=vt[:, :], in_=iv[rows, cols]))
            c.dma("sp", [], [gt], lambda e: e.dma_start(out=gt[:, :], in_=gg[rows, cols]))
            c.dma("sp", [], [qt], lambda e: e.dma_start(out=qt[:, :], in_=qT[h][:, rows]))
            yield
            sg = self.ring(tg + "sig", [128, 128], F32)
            c.act([ft], [sg], lambda: nc.scalar.activation(out=sg[:, :], in_=ft[:, :], func=AF.Sigmoid))
            fg = self.ring(tg + "fg", [128, 128], F32)
            c.dve([sg, d["omlB"]], [fg], lambda: nc.vector.tensor_tensor(fg[:, :], sg[:, :], d["omlB"][:, :], ALU.mult))
            c.dve([fg, d["lbB"]], [fg], lambda: nc.vector.tensor_tensor(fg[:, :], fg[:, :], d["lbB"][:, :], ALU.add))
            yield
            sq = self.ring(tg + "sq", [128, 128], F32)
            c.act([qt], [sq], lambda: nc.scalar.activation(out=sq[:, :], in_=qt[:, :], func=AF.Silu))
            yield
            logf = self.ring(tg + "logf", [128, 128], F32)
            c.act([fg], [logf], lambda: nc.scalar.activation(out=logf[:, :], in_=fg[:, :], func=AF.Ln))
            kt = self.ring(tg + "kt", [128, 128], F32)
            c.dve([fg], [kt], lambda: nc.vector.tensor_scalar(kt[:, :], fg[:, :], -1.0, 1.0, ALU.mult, ALU.add))
            pbT, pblr, pkT = self.nps(), self.nps(), self.nps()
            c.pe([logf, self.K], [pbT], lambda: nc.tensor.matmul(pbT[:, :128], logf[:, :], self.U1, start=True, stop=True))
            c.pe([logf, self.K], [pblr], lambda: nc.tensor.matmul(pblr[:, :128], self.U2, logf[:, :], start=True, stop=True))
            c.pe([kt, self.K], [pkT], lambda: nc.tensor.transpose(pkT[:, :128], kt[:, :], self.identf))
            bm = self.ring(tg + "bm", [128, 2], F32, 3)
            c.dve([pbT], [bm], lambda: nc.vector.tensor_scalar(bm[:, 0:1], pbT[:, 63:64], -1.0, None, ALU.mult))
            c.dve([pbT], [bm], lambda: nc.vector.tensor_copy(bm[:, 1:2], pbT[:, 63:64]))
            e1 = self.ring(tg + "e1", [128, 128], F32)
            e2 = self.ring(tg + "e2", [128, 128], F32)
            e3 = self.ring(tg + "e3", [128, 128], F32, 3)
            c.act([pbT, bm], [e1], lambda: nc.scalar.activation(out=e1[:, :], in_=pbT[:, :128], func=AF.Exp, bias=bm[:, 0:1], scale=1.0))
            c.act([pbT, bm], [e2], lambda: nc.scalar.activation(out=e2[:, :], in_=pbT[:, :128], func=AF.Exp, bias=bm[:, 1:2], scale=-1.0))
            c.act([pbT], [e3], lambda: nc.scalar.activation(out=e3[:, :], in_=pbT[:, :128], func=AF.Exp))
            eb = self.ring(tg + "eblr", [128, 128], F32)
            c.act([pblr], [eb], lambda: nc.scalar.activation(out=eb[:, :], in_=pblr[:, :128], func=AF.Exp))
            qtil = self.ring(tg + "qtil", [128, 128], BF16)
            qhat = self.ring(tg + "qhat", [128, 128], BF16)
            ktil = self.ring(tg + "ktil", [128, 128], BF16)
            kdec = self.ring(tg + "kdec", [128, 128], BF16)
            c.dve([sq, e1], [qtil], lambda: nc.vector.scalar_tensor_tensor(
                out=qtil[:, :], in0=sq[:, :], scalar=scale, in1=e1[:, :], op0=ALU.mult, op1=ALU.mult))
            c.dve([sq, e3], [qhat], lambda: nc.vector.scalar_tensor_tensor(
                out=qhat[:, :], in0=sq[:, :], scalar=scale, in1=e3[:, :], op0=ALU.mult, op1=ALU.mult))
            c.dve([pkT, e2], [ktil], lambda: nc.vector.tensor_tensor(ktil[:, :], pkT[:, :128], e2[:, :], ALU.mult))
            c.dve([kt, eb], [kdec], lambda: nc.vector.tensor_tensor(kdec[:, :], kt[:, :], eb[:, :], ALU.mult))
            pa = self.nps()
            c.pe([ktil, qtil], [pa], lambda: nc.tensor.matmul(pa[:, :128], ktil[:, :], qtil[:, :], start=True, stop=True))
            aT = self.ring(tg + "aT", [128, 128], BF16)
            c.dve([pa, self.K], [aT], lambda: nc.vector.tensor_tensor(aT[:, :], pa[:, :128], self.U1, ALU.mult))
            out.update(vt=vt, gt=gt, qhat=qhat, aT=aT, kdec=kdec, e3=e3)

        def seq(r, h, p):
            d = hs[h]
            tg = "h%d" % h
            vt, gt, qhat, aT, kdec, e3 = p["vt"], p["gt"], p["qhat"], p["aT"], p["kdec"], p["e3"]
            po = self.nps()
            c.pe([aT, vt], [po], lambda: nc.tensor.matmul(po[:, :128], aT[:, :], vt[:, :], start=True, stop=False))
            c.pe([qhat, d["Sb"]], [po], lambda: nc.tensor.matmul(po[:, :128], qhat[:, :], d["Sb"][:, :], start=False, stop=True))
            pS = self.nps()
            c.pe([kdec, vt], [pS], lambda: nc.tensor.matmul(pS[:, :128], kdec[:, :], vt[:, :], start=True, stop=True))
            c.dve([pS, e3, d["S"]], [d["S"]], lambda: nc.vector.scalar_tensor_tensor(
                out=d["S"][:, :], in0=d["S"][:, :], scalar=e3[:, 127:128], in1=pS[:, :128], op0=ALU.mult, op1=ALU.add))
            c.pool([d["S"]], [d["Sb"]], lambda: nc.gpsimd.tensor_copy(d["Sb"][:, :], d["S"][:, :]))
            osb = self.ring(tg + "osb", [128, 128], F32)
            c.dve([po], [osb], lambda: nc.vector.tensor_copy(osb[:, :], po[:, :128]))
            yield
            yield from self.rms_gate_out_g(osb, gbc, gt, y, r * 128, h * 128, tg)

        P = {}

        def run_prep(r):
            for h in range(nheads):
                P[(r, h)] = {}
            lockstep([prep(r, h, P[(r, h)]) for h in range(nheads)])
        run_prep(0)
        for r in range(NR):
            if r + 1 < NR:
                run_prep(r + 1)
            lockstep([seq(r, h, P.pop((r, h))) for h in range(nheads)])


def lockstep(gens):
    gens = list(gens)
    while gens:
        nxt = []
        for g in gens:
            try:
                next(g)
                nxt.append(g)
            except StopIteration:
                pass
        gens = nxt


def mix_consts():
    i = np.arange(128)
    ident = np.eye(128, dtype=np.float32)
    U1 = (i[:, None] <= i[None, :]).astype(np.float32)
    U2 = (i[:, None] > i[None, :]).astype(np.float32)
    ones = np.ones((128, 128), np.float32)
    triT = np.where(i[:, None] <= i[None, :], 0.0, -30000.0).astype(np.float32)
    return np.stack([ident, U1, U2, ones, triT]).astype(np.float32)


def build_hgrn2(nheads=4):
    nc = new_nc()
    es = ExitStack()
    with es:
        c = Ctx(nc, es)
        di = lambda n, sh: c.dram(n, sh, F32, "ExternalInput")
        cst = di("cst", [5, 128, 128])
        qT = di("qT", [nheads, 128, T]); f = di("f", [T, nheads * 128]); iv = di("iv", [T, nheads * 128]); gg = di("gg", [T, nheads * 128])
        lg = di("lg", [nheads, 128, 4]); coef = di("coef", [4]); ong = di("ong", [128])
        y = c.dram("y", [T, nheads * 128], F32, "ExternalOutput")
        mp = MixProg(c, cst.h)
        mp.hgrn2(qT.h, f.h, iv.h, gg.h, lg.h, coef.h, ong.h, y, nheads)
        c.finish()
    return nc, c


def _gdn_g(self, qkvT, cw, gate, ba, sc, ong, y, nheads=2, col_off=0, bigs=None, pr="all"):
    c, nc = self.c, self.nc
    NR = T // 128
    I, U1, U2, ONES = self.identf, self.U1, self.U2, self.ones
    nps = lambda: self.nps(pr)
    gbc = c.sb([128, 128], F32, "dn_gbc")
    c.dma("sp", [], [gbc], lambda e: e.dma_start(out=gbc[:, :], in_=ong.partition_broadcast(128)))
    if bigs is None:
        bigs = [c.sb([128, T], F32, "big") for _ in range(5)]
    raw = [bigs[0], bigs[0]]
    acc = bigs[1]
    XS = bigs[2:5]
    R_ = lambda k, n=2: self.ring("dn_" + k, [128, 128], F32, n)
    for h in range(nheads):
        for xi in range(3):
            rw = raw[xi % 2]
            cwt = self.ring("dn_cw", [128, 4], F32, 3)
            c.dma("sp", [], [rw], lambda e: e.dma_start(out=rw[:, :], in_=qkvT[xi, h]))
            c.dma("sp", [], [cwt], lambda e: e.dma_start(out=cwt[:, :], in_=cw[xi, h]))
            c.dve([rw, cwt], [acc], lambda: nc.vector.tensor_scalar(acc[:, :], rw[:, :], cwt[:, 3:4], None, ALU.mult))
            for sh in (1, 2, 3):
                c.dve([rw, cwt, acc], [acc], lambda sh=sh: nc.vector.scalar_tensor_tensor(
                    out=acc[:, sh:], in0=rw[:, :T - sh], scalar=cwt[:, 3 - sh:4 - sh], in1=acc[:, sh:],
                    op0=ALU.mult, op1=ALU.add))
                yield
            X = XS[xi]
            c.act([acc], [X], lambda: nc.scalar.activation(out=X[:, :], in_=acc[:, :], func=AF.Silu))
            yield
            if xi < 2:
                for blk in range(T // 512):
                    cs = slice(blk * 512, (blk + 1) * 512)
                    sq = self.ring("dn_sq", [128, 512], F32)
                    c.act([X], [sq], lambda: nc.scalar.activation(out=sq[:, :], in_=X[:, cs], func=AF.Square))
                    pp = nps()
                    c.pe([sq, self.K], [pp], lambda: nc.tensor.matmul(pp[:, :], ONES, sq[:, :], start=True, stop=True))
                    rn = self.ring("dn_rn", [128, 512], F32)
                    c.dve([pp], [rn], lambda: nc.vector.tensor_scalar(rn[:, :], pp[:, :], EPS, None, ALU.add))
                    c.act([rn], [rn], lambda: nc.scalar.sqrt(rn[:, :], rn[:, :]))
                    c.dve([rn], [rn], lambda: nc.vector.reciprocal(rn[:, :], rn[:, :]))
                    sc_ = (128 ** -0.5) if xi == 0 else 1.0
                    c.dve([X, rn], [X], lambda: nc.vector.scalar_tensor_tensor(
                        out=X[:, cs], in0=X[:, cs], scalar=sc_, in1=rn[:, :], op0=ALU.mult, op1=ALU.mult))
                    yield
        qs, ks, vs = XS
        bat = self.ring("dn_bat", [128, 64], F32)
        sct = self.ring("dn_sct", [128, 2], F32)
        c.dma("sp", [], [bat], lambda e: e.dma_start(out=bat[:, :], in_=ba[h]))
        c.dma("sp", [], [sct], lambda e: e.dma_start(out=sct[:, :], in_=sc[h].partition_broadcast(128)))
        beta = self.ring("dn_beta", [128, 32], F32)
        nbeta = self.ring("dn_nbeta", [128, 32], F32)
        gg = self.ring("dn_g", [128, 32], F32)
        ea = self.ring("dn_ea", [128, 1], F32)
        c.act([bat], [beta], lambda: nc.scalar.activation(out=beta[:, :], in_=bat[:, 0:32], func=AF.Sigmoid))
        c.dve([beta], [nbeta], lambda: nc.vector.tensor_scalar(nbeta[:, :], beta[:, :], -1.0, None, ALU.mult))
        c.act([bat, sct], [gg], lambda: nc.scalar.activation(out=gg[:, :], in_=bat[:, 32:64], func=AF.Exp, bias=sct[:, 1:2], scale=1.0))
        c.dve([gg], [gg], lambda: nc.vector.tensor_scalar(gg[:, :], gg[:, :], 1.0, None, ALU.add))
        c.act([gg], [gg], lambda: nc.scalar.activation(out=gg[:, :], in_=gg[:, :], func=AF.Ln))
        c.act([sct], [ea], lambda: nc.scalar.activation(out=ea[:, :], in_=sct[:, 0:1], func=AF.Exp))
        c.dve([gg, ea], [gg], lambda: nc.vector.tensor_scalar(gg[:, :], gg[:, :], ea[:, 0:1], -1.0, ALU.mult, ALU.mult))
        S = self.ring("dn_S", [128, 128], F32)
        c.dve([], [S], lambda: nc.vector.memset(S[:, :], 0.0))
        yield

        def prep(r, out):
            rows = slice(r * 128, (r + 1) * 128)
            gcol = gg[:, r:r + 1]
            gt = self.ring("dn_gt", [128, 128], F32, 3)
            c.dma("sp", [], [gt], lambda e: e.dma_start(out=gt[:, :], in_=gate[rows, h * 128:(h + 1) * 128]))
            pk, pv = nps(), nps()
            c.pe([ks, self.K], [pk], lambda: nc.tensor.transpose(pk[:, :128], ks[:, rows], I))
            c.pe([vs, self.K], [pv], lambda: nc.tensor.transpose(pv[:, :128], vs[:, rows], I))
            ktm, Vb = R_("ktm"), R_("Vb")
            c.act([pk], [ktm], lambda: nc.scalar.copy(out=ktm[:, :], in_=pk[:, :128]))
            c.dve([pv, beta], [Vb], lambda: nc.vector.tensor_scalar(Vb[:, :], pv[:, :128], beta[:, r:r + 1], None, ALU.mult))
            gU2, gcB = R_("gU2"), R_("gcB")
            c.dve([gg, self.K], [gU2], lambda: nc.vector.tensor_scalar(gU2[:, :], U2, gcol, None, ALU.mult))
            c.dve([gg, self.K], [gcB], lambda: nc.vector.tensor_scalar(gcB[:, :], ONES, gcol, None, ALU.mult))
            yield
            pD, pDT, pGB, pcol = nps(), nps(), nps(), nps()
            c.pe([gU2, self.K], [pD], lambda: nc.tensor.matmul(pD[:, :128], U1, gU2[:, :], start=True, stop=True))
            c.pe([gU2, self.K], [pDT], lambda: nc.tensor.matmul(pDT[:, :128], gU2[:, :], U1, start=True, stop=True))
            c.pe([gcB, self.K], [pGB], lambda: nc.tensor.matmul(pGB[:, :128], gcB[:, :], U1, start=True, stop=True))
            c.pe([gcB, self.K], [pcol], lambda: nc.tensor.matmul(pcol[:, 0:128], U1, gcB[:, :], start=True, stop=True))
            c.pe([gcB, self.K], [pcol], lambda: nc.tensor.matmul(pcol[:, 128:256], U2, gcB[:, :], start=True, stop=True))
            c.pe([gcB, self.K], [pcol], lambda: nc.tensor.matmul(pcol[:, 256:384], ONES, gcB[:, :], start=True, stop=True))
            ecol = self.ring("dn_ecol", [128, 4], F32, 3)
            c.act([pcol], [ecol], lambda: nc.scalar.activation(
                out=ecol[:, 0:3], in_=pcol[:, 0:384].rearrange("p (k n) -> p k n", n=128)[:, :, 0], func=AF.Exp))
            expD, expDT, eGB = R_("expD"), R_("expDT"), R_("eGB")
            c.act([pD], [expD], lambda: nc.scalar.activation(out=expD[:, :], in_=pD[:, :128], func=AF.Exp))
            c.act([pDT], [expDT], lambda: nc.scalar.activation(out=expDT[:, :], in_=pDT[:, :128], func=AF.Exp))
            c.act([pGB], [eGB], lambda: nc.scalar.activation(out=eGB[:, :], in_=pGB[:, :128], func=AF.Exp))
            c.pool([expD, self.K], [expD], lambda: nc.gpsimd.tensor_tensor(expD[:, :], expD[:, :], U2, ALU.mult))
            c.pool([expDT, self.K], [expDT], lambda: nc.gpsimd.tensor_tensor(expDT[:, :], expDT[:, :], U1, ALU.mult))
            yield
            pKK, pQK = nps(), nps()
            c.pe([ks], [pKK], lambda: nc.tensor.matmul(pKK[:, :128], ks[:, rows], ks[:, rows], start=True, stop=True))
            c.pe([ks, qs], [pQK], lambda: nc.tensor.matmul(pQK[:, :128], ks[:, rows], qs[:, rows], start=True, stop=True))
            A = R_("A")
            c.dve([pKK, nbeta, expD], [A], lambda: nc.vector.scalar_tensor_tensor(
                out=A[:, :], in0=pKK[:, :128], scalar=nbeta[:, r:r + 1], in1=expD[:, :], op0=ALU.mult, op1=ALU.mult))
            aqkT = R_("aqkT", 3)
            c.dve([pQK, expDT], [aqkT], lambda: nc.vector.tensor_tensor(aqkT[:, :], pQK[:, :128], expDT[:, :], ALU.mult))
            qdT = R_("qdT", 3)
            c.pool([qs, eGB], [qdT], lambda: nc.gpsimd.tensor_tensor(qdT[:, :], qs[:, rows], eGB[:, :], ALU.mult))
            bcol = self.ring("dn_bcol", [128, 1], F32, 3)
            c.dve([beta, ecol], [bcol], lambda: nc.vector.tensor_tensor(bcol[:, :], beta[:, r:r + 1], ecol[:, 0:1], ALU.mult))
            kbg, kdec = R_("kbg"), R_("kdec", 3)
            c.dve([ktm, bcol], [kbg], lambda: nc.vector.tensor_scalar(kbg[:, :], ktm[:, :], bcol[:, 0:1], None, ALU.mult))
            c.pool([ktm, ecol], [kdec], lambda: nc.gpsimd.tensor_scalar(kdec[:, :], ktm[:, :], ecol[:, 1:2], None, ALU.mult))
            yield
            pB = nps()
            c.pe([A, self.K], [pB], lambda: nc.tensor.transpose(pB[:, :128], A[:, :], I))
            Bm = R_("B")
            c.act([pB], [Bm], lambda: nc.scalar.copy(out=Bm[:, :], in_=pB[:, :128]))
            Rm = R_("R", 3)
            c.dve([pB, self.K], [Rm], lambda: nc.vector.tensor_tensor(Rm[:, :], pB[:, :128], I, ALU.add))
            yield
            Aj, Bj = A, Bm
            for lvl in range(6):
                pA2 = nps()
                c.pe([Aj, Bj], [pA2], lambda Aj=Aj, Bj=Bj, pA2=pA2: nc.tensor.matmul(pA2[:, :128], Bj[:, :], Aj[:, :], start=True, stop=True))
                if lvl < 5:
                    pB2 = nps()
                    c.pe([Aj, Bj], [pB2], lambda Aj=Aj, Bj=Bj, pB2=pB2: nc.tensor.matmul(pB2[:, :128], Aj[:, :], Bj[:, :], start=True, stop=True))
                    A2, B2 = R_("A2", 3), R_("B2", 3)
                IA = R_("IA")
                c.dve([pA2, self.K], [IA], lambda IA=IA, pA2=pA2: nc.vector.tensor_tensor(IA[:, :], pA2[:, :128], I, ALU.add))
                if lvl < 5:
                    c.act([pA2], [A2], lambda A2=A2, pA2=pA2: nc.scalar.copy(out=A2[:, :], in_=pA2[:, :128]))
                    c.act([pB2], [B2], lambda B2=B2, pB2=pB2: nc.scalar.copy(out=B2[:, :], in_=pB2[:, :128]))
                pR = nps()
                c.pe([IA, Rm], [pR], lambda IA=IA, Rm=Rm, pR=pR: nc.tensor.matmul(pR[:, :128], IA[:, :], Rm[:, :], start=True, stop=True))
                Rn = R_("R", 3)
                c.dve([pR], [Rn], lambda Rn=Rn, pR=pR: nc.vector.tensor_copy(Rn[:, :], pR[:, :128]))
                Rm = Rn
                if lvl < 5:
                    Aj, Bj = A2, B2
                yield
            pu, pw = nps(), nps()
            c.pe([Rm, Vb], [pu], lambda: nc.tensor.matmul(pu[:, :128], Rm[:, :], Vb[:, :], start=True, stop=True))
            c.pe([Rm, kbg], [pw], lambda: nc.tensor.matmul(pw[:, :128], kbg[:, :], Rm[:, :], start=True, stop=True))
            u, wT = R_("u", 3), R_("wT", 3)
            c.act([pu], [u], lambda: nc.scalar.copy(out=u[:, :], in_=pu[:, :128]))
            c.dve([pw], [wT], lambda: nc.vector.tensor_copy(wT[:, :], pw[:, :128]))
            out.update(u=u, wT=wT, qdT=qdT, aqkT=aqkT, kdec=kdec, ecol=ecol, gt=gt)
            yield

        def seq(r, p):
            u, wT, qdT, aqkT, kdec, ecol, gt = (p[k] for k in ("u", "wT", "qdT", "aqkT", "kdec", "ecol", "gt"))
            pvn = nps()
            c.pe([wT, S], [pvn], lambda: nc.tensor.matmul(pvn[:, :128], wT[:, :], S[:, :], start=True, stop=True))
            vn = R_("vn")
            c.dve([u, pvn], [vn], lambda: nc.vector.tensor_tensor(vn[:, :], u[:, :], pvn[:, :128], ALU.subtract))
            po = nps()
            c.pe([qdT, S], [po], lambda: nc.tensor.matmul(po[:, :128], qdT[:, :], S[:, :], start=True, stop=False))
            c.pe([aqkT, vn], [po], lambda: nc.tensor.matmul(po[:, :128], aqkT[:, :], vn[:, :], start=False, stop=True))
            pS = nps()
            c.pe([kdec, vn], [pS], lambda: nc.tensor.matmul(pS[:, :128], kdec[:, :], vn[:, :], start=True, stop=True))
            c.dve([pS, ecol, S], [S], lambda: nc.vector.scalar_tensor_tensor(
                out=S[:, :], in0=S[:, :], scalar=ecol[:, 2:3], in1=pS[:, :128], op0=ALU.mult, op1=ALU.add))
            osb = R_("osb")
            c.act([po], [osb], lambda: nc.scalar.copy(out=osb[:, :], in_=po[:, :128]))
            yield
            yield from self.rms_gate_out_g(osb, gbc, gt, y, r * 128, col_off + h * 128, "dn")

        P = {0: {}}
        yield from prep(0, P[0])
        for r in range(NR):
            if r + 1 < NR:
                P[r + 1] = {}
                yield from prep(r + 1, P[r + 1])
            yield from seq(r, P.pop(r))


def _gdn(self, *a, **kw):
    for _ in _gdn_g(self, *a, **kw):
        pass


MixProg.gdn_g = _gdn_g
MixProg.gdn = _gdn
MixProg.gdn = _gdn


def build_gdn(nheads=2):
    nc = new_nc()
    es = ExitStack()
    with es:
        c = Ctx(nc, es)
        di = lambda n, sh: c.dram(n, sh, F32, "ExternalInput")
        cst = di("cst", [5, 128, 128])
        qkvT = di("qkvT", [3, nheads, 128, T]); cw = di("cw", [3, nheads, 128, 4]); gate = di("gate", [T, nheads * 128])
        ba = di("ba", [nheads, 128, 64]); sc = di("sc", [nheads, 2]); ong = di("ong", [128])
        y = c.dram("y", [T, nheads * 128], F32, "ExternalOutput")
        mp = MixProg(c, cst.h)
        mp.gdn(qkvT.h, cw.h, gate.h, ba.h, sc.h, ong.h, y, nheads)
        c.finish()
    return nc, c


def _moba_g(self, qT, kT, v, esel, pbias, triw, y, nheads=2, col_off=0, bigs=None):
    c, nc = self.c, self.nc
    scale = 128 ** -0.5
    NEG = -1e30
    Eb = c.sb([32, 32 * 128], BF16, "mo_esel")
    c.dma("pool", [], [Eb], lambda e: e.dma_start(out=Eb[:, :], in_=esel))
    pbt = c.sb([128, 256], F32, "mo_pb")
    c.dma("sp", [], [pbt], lambda e: e.dma_start(out=pbt[:, :], in_=pbias.partition_broadcast(128)))
    trb = c.sb([128, 4, 512], BF16, "mo_tri")
    c.dma("pool", [], [trb], lambda e: e.dma_start(out=trb[:, :, :], in_=triw.rearrange("j p n -> p j n")))
    zeros = c.sb([128, 32], F32, "mo_z")
    c.dve([], [zeros], lambda: nc.vector.memset(zeros[:, :], 0.0))
    zb = c.sb([128, 512], BF16, "mo_zb")
    c.dve([], [zb], lambda: nc.vector.memset(zb[:, :], 0.0))
    stg = [c.sb([128, 1024], F32, "mo_stg") for _ in range(2)]
    gall = c.sb([128, 32, 16], F32, "mo_gall")
    qb = c.sb([128, T], BF16, "mo_qb")
    kb = c.sb([128, T], BF16, "mo_kb")
    vaug = c.sb([128, 32, 132], BF16, "mo_v")
    kmT = c.sb([128, 16], F32, "mo_km")
    pobank = [self.pb[0], self.pb[0], self.pb[1], self.pb[1]]
    pooff = [0, 256, 0, 256]
    self.prings["mo"] = [[2, 3], 0]
    nps = lambda: self.nps("mo")
    for h in range(nheads):
        c.dma("pool", [], [vaug], lambda e: e.dma_start(
            out=vaug[:, :, 0:128], in_=v[:, h * 128:(h + 1) * 128].rearrange("(r p) d -> p r d", p=128)))
        c.dve([], [vaug], lambda: nc.vector.memset(vaug[:, :, 128:129], 1.0))
        for ch in range(4):
            sg = stg[ch % 2]
            cs = slice(ch * 1024, (ch + 1) * 1024)
            c.dma("sp", [], [sg], lambda e: e.dma_start(out=sg[:, :], in_=kT[h][:, cs]))
            c.dve([sg], [kb], lambda: nc.vector.tensor_copy(kb[:, cs], sg[:, :]))
            c.dve([sg], [kmT], lambda: nc.vector.tensor_reduce(
                out=kmT[:, ch * 4:(ch + 1) * 4], in_=sg[:, :].rearrange("p (n k) -> p n k", k=256), axis=AX.X, op=ALU.add))
            yield
        c.dve([kmT], [kmT], lambda: nc.vector.tensor_scalar(kmT[:, :], kmT[:, :], 1.0 / 256, None, ALU.mult))
        for ch in range(4):
            sg = stg[ch % 2]
            cs = slice(ch * 1024, (ch + 1) * 1024)
            c.dma("sp", [], [sg], lambda e: e.dma_start(out=sg[:, :], in_=qT[h][:, cs]))
            c.act([sg], [qb], lambda: nc.scalar.copy(out=qb[:, cs], in_=sg[:, :]))
            for j8 in range(8):
                qt_ = ch * 8 + j8
                pg = nps()
                c.pe([sg, kmT], [pg], lambda pg=pg: nc.tensor.matmul(
                    pg[:, :16], sg[:, j8 * 128:(j8 + 1) * 128], kmT[:, :], start=True, stop=True))
                c.dve([pg], [gall], lambda pg=pg: nc.vector.tensor_copy(gall[:, qt_, :], pg[:, :16]))
            yield
        for qc in range(8):
            biasT = self.ring("mo_biasT", [32, 512], BF16, 2)
            for j in range(4):
                qt = 4 * qc + j
                blk = qt // 2
                qcols = slice(qt * 128, (qt + 1) * 128)
                b32 = self.ring("mo_b32", [128, 32], F32, 2)
                mx = self.ring("mo_mx", [128, 16], F32, 2)
                nkc = qt // 4 + 1
                for kc in range(nkc):
                    pm = nps()
                    c.pe([qb, kb], [pm], lambda pm=pm, kc=kc: nc.tensor.matmul(
                        pm[:, :], qb[:, qcols], kb[:, kc * 512:(kc + 1) * 512], start=True, stop=True))
                    c.dve([pm], [mx], lambda pm=pm, kc=kc: nc.vector.reduce_max(mx[:, kc:kc + 1], pm[:, :], axis=AX.X))
                c.dve([mx], [mx], lambda: nc.vector.reduce_max(mx[:, 8:9], mx[:, 0:nkc], axis=AX.X))
                c.dve([], [b32], lambda: nc.vector.memset(b32[:, :], NEG))
                c.dve([mx, zeros], [b32], lambda: nc.vector.tensor_scalar(
                    b32[:, 0:qt + 1], zeros[:, 0:qt + 1], mx[:, 8:9], None, ALU.subtract))
                if blk > 0:
                    selb = self.ring("mo_selb", [128, 16], F32, 2)
                    if blk > 3:
                        gm = self.ring("mo_gm", [128, 16], F32, 2)
                        c.dve([gall, pbt], [gm], lambda: nc.vector.tensor_tensor(
                            gm[:, :], gall[:, qt, :], pbt[:, blk * 16:(blk + 1) * 16], ALU.add))
                        t8 = self.ring("mo_t8", [128, 8], F32, 2)
                        c.dve([gm], [t8], lambda: nc.vector.max(out=t8[:, :], in_=gm[:, :]))
                        c.dve([gm, t8], [selb], lambda: nc.vector.tensor_scalar(
                            selb[:, :], gm[:, :], t8[:, 2:3], None, ALU.is_ge))
                        c.dve([selb], [selb], lambda: nc.vector.tensor_scalar(
                            selb[:, :], selb[:, :], -NEG, NEG, ALU.mult, ALU.add))
                    else:
                        c.dve([], [selb], lambda: nc.vector.memset(selb[:, :], 0.0))
                    b32v = b32[:, 0:2 * blk].rearrange("p (n two) -> p n two", two=2)
                    for e2 in range(2):
                        c.dve([b32, selb], [b32], lambda e2=e2: nc.vector.tensor_tensor(
                            b32v[:, :, e2], b32v[:, :, e2], selb[:, 0:blk], ALU.add))
                pT = nps()
                c.pe([b32, self.K], [pT], lambda pT=pT: nc.tensor.transpose(pT[0:32, 0:128], b32[:, :], self.identf))
                c.act([pT], [biasT], lambda pT=pT: nc.scalar.copy(out=biasT[:, j * 128:(j + 1) * 128], in_=pT[0:32, 0:128]))
                yield
            qcs = slice(qc * 512, (qc + 1) * 512)
            ns = 4 * qc + 4
            for bk in (self.pb[0], self.pb[1]):
                c.pe([zb], [bk], lambda bk=bk: nc.tensor.matmul(bk[:, :], zb[:, 0:128], zb[:, :], start=True, stop=True))
            for s in range(ns):
                ps = nps()
                jd = s - 4 * qc
                c.pe([kb, qb], [ps], lambda ps=ps: nc.tensor.matmul(
                    ps[:, :], kb[:, s * 128:(s + 1) * 128], qb[:, qcs], start=True, stop=False))
                c.pe([Eb, biasT], [ps], lambda ps=ps: nc.tensor.matmul(
                    ps[:, :], Eb[:, s * 128:(s + 1) * 128], biasT[:, :], start=False, stop=(jd < 0)))
                if jd >= 0:
                    c.pe([self.Kb, trb], [ps], lambda ps=ps: nc.tensor.matmul(
                        ps[:, :], self.identb, trb[:, jd, :], start=False, stop=True))
                PT = self.ring("mo_PT", [128, 512], BF16, 3)
                c.act([ps], [PT], lambda ps=ps: nc.scalar.activation(out=PT[:, :], in_=ps[:, :], func=AF.Exp, scale=scale))
                for j in range(4):
                    qt = 4 * qc + j
                    if s <= qt:
                        c.pe([PT, vaug], [pobank[j]], lambda j=j: nc.tensor.matmul(
                            pobank[j][:, pooff[j]:pooff[j] + 129], PT[:, j * 128:(j + 1) * 128], vaug[:, s, 0:129],
                            start=False, stop=(s == qt), skip_group_check=True))
                yield
            for j in range(4):
                qt = 4 * qc + j
                rc = self.ring("mo_rc", [128, 1], F32, 3)
                ot = self.ring("mo_ot", [128, 128], F32, 3)
                c.dve([pobank[j]], [rc], lambda j=j: nc.vector.reciprocal(rc[:, :], pobank[j][:, pooff[j] + 128:pooff[j] + 129]))
                c.dve([pobank[j], rc], [ot], lambda j=j: nc.vector.tensor_scalar(
                    ot[:, :], pobank[j][:, pooff[j]:pooff[j] + 128], rc[:, 0:1], None, ALU.mult))
                c.dma("sp", [ot], [y], lambda e: e.dma_start(
                    out=y[qt * 128:(qt + 1) * 128, col_off + h * 128:col_off + (h + 1) * 128], in_=ot[:, :]))


def _moba(self, *a, **kw):
    for _ in _moba_g(self, *a, **kw):
        pass


MixProg.moba = _moba
MixProg.moba_g = _moba_g


def moba_consts():
    esel = np.zeros((32, 32, 128), np.float32)
    for s in range(32):
        esel[s, s, :] = 1.0
    pb = np.zeros((16, 16), np.float32)
    for blk in range(16):
        pb[blk, blk:] = -1e30
    i = np.arange(128)
    tri = np.where(i[:, None] <= i[None, :], 0.0, -30000.0).astype(np.float32)
    triw = np.zeros((4, 128, 512), np.float32)
    for j in range(4):
        triw[j, :, j * 128:(j + 1) * 128] = tri
    return esel.reshape(32, 32 * 128), pb.reshape(256), triw


def build_moba(nheads=2):
    nc = new_nc()
    es = ExitStack()
    with es:
        c = Ctx(nc, es)
        di = lambda n, sh: c.dram(n, sh, F32, "ExternalInput")
        cst = di("cst", [5, 128, 128])
        qT = di("mqT", [nheads, 128, T]); kT = di("mkT", [nheads, 128, T]); v = di("mv", [T, nheads * 128])
        esel = di("esel", [32, 32 * 128]); pbias = di("pbias", [256]); triw = di("triw", [4, 128, 512])
        y = c.dram("y", [T, nheads * 128], F32, "ExternalOutput")
        mp = MixProg(c, cst.h)
        mp.moba(qT.h, kT.h, v.h, esel.h, pbias.h, triw.h, y, nheads)
        c.finish()
    return nc, c


def build_mix_even():
    nc = new_nc()
    es = ExitStack()
    with es:
        c = Ctx(nc, es)
        di = lambda n, sh: c.dram(n, sh, F32, "ExternalInput")
        cst = di("cst", [5, 128, 128])
        qT = di("mqT", [2, 128, T]); kT = di("mkT", [2, 128, T]); v = di("mv", [T, 256])
        esel = di("esel", [32, 32 * 128]); pbias = di("pbias", [256]); triw = di("triw", [4, 128, 512])
        qkvT = di("qkvT", [3, 2, 128, T]); cw = di("cw", [3, 2, 128, 4]); gate = di("gate", [T, 256])
        ba = di("ba", [2, 128, 64]); sc = di("sc", [2, 2]); ong = di("ong", [128])
        y = c.dram("y", [T, 512], F32, "ExternalOutput")
        mp = MixProg(c, cst.h)
        bigs = [c.sb([128, T], F32, "big") for _ in range(5)]
        mp.prings["dn"] = [[4, 5, 6, 7], 0]
        lockstep([mp.moba_g(qT.h, kT.h, v.h, esel.h, pbias.h, triw.h, y, 2, 0, None),
                  mp.gdn_g(qkvT.h, cw.h, gate.h, ba.h, sc.h, ong.h, y, 2, 256, bigs, "dn")])
        c.finish()
    return nc, c


_PROGS = {}


def _prog(key, fn):
    if key not in _PROGS:
        _PROGS[key] = fn()[0]
    return _PROGS[key]


def _run(nc, maps):
    maps = [{k: np.ascontiguousarray(v, dtype=np.float32) for k, v in m.items()} for m in maps]
    return run_bass_kernel_spmd(nc, maps, core_ids=list(range(NCORE))).results


def kernel(x, mem, norm_g, mem_norm_g, final_norm_g, ffn_w_in, ffn_w_out, ab_w_in, ab_conv_w, ab_a_log,
           ab_dt_bias, ab_o_norm_g, ab_w_out, c_w_in, c_lb_logits, c_o_norm_g, c_w_out, x_w_q, x_w_kv, x_w_o):
    f = lambda a: np.asarray(a, dtype=np.float32)
    x, mem, norm_g = f(x), f(mem), f(norm_g)
    ident = np.eye(128, dtype=np.float32)
    cst = mix_consts()
    esel, pbias, triw = moba_consts()
    xs = x.reshape(NCORE, TPC, D)
    depth = 4

    def pre_inputs(l):
        kind = "even" if l % 2 == 0 else "odd"
        w_in = f(ab_w_in[l // 2]) if kind == "even" else f(c_w_in[l // 2])
        return kind, dict(g0=norm_g[l, 0], g1=norm_g[l, 1], f0_in=f(ffn_w_in[l, 0]), f0_out=f(ffn_w_out[l, 0]), w_in=w_in)

    def mix_inputs(l):
        w_mo = f(ab_w_out[l // 2]) if l % 2 == 0 else f(c_w_out[l // 2])
        return dict(w_mo=w_mo, mem_g=f(mem_norm_g), w_q=f(x_w_q[l]), w_kv=f(x_w_kv[l]), w_o=f(x_w_o[l]),
                    g2=norm_g[l, 2], g3=norm_g[l, 3], f1_in=f(ffn_w_in[l, 1]), f1_out=f(ffn_w_out[l, 1]))

    kind, pin = pre_inputs(0)
    res = _run(_prog(("tok", False, kind, False), lambda: build_tok(False, kind, False)),
               [dict(pin, x=xs[i], ident=ident) for i in range(NCORE)])
    out = None
    for l in range(depth):
        xcur = [r["xo"] for r in res]
        pf = [np.concatenate([res[b * 4 + q]["pf"] for q in range(4)], axis=1) for b in range(B)]
        pt = [np.concatenate([res[b * 4 + q]["pt"] for q in range(4)], axis=0) for b in range(B)]
        yfull = np.zeros((B, T, D), np.float32)
        if l % 2 == 0:
            e = l // 2
            convw = f(ab_conv_w[e])
            maps = []
            for i in range(NCORE):
                b, hp = i // 4, i % 4
                hds = [2 * hp, 2 * hp + 1]
                P = pf[b]
                m = dict(cst=cst, esel=esel, pbias=pbias, triw=triw)
                m["mqT"] = np.stack([P[h * 128:(h + 1) * 128] for h in hds])
                m["mkT"] = np.stack([P[1024 + h * 128:1024 + (h + 1) * 128] for h in hds])
                m["mv"] = pt[b][:, 2 * hp * 128:(2 * hp + 2) * 128]
                m["qkvT"] = np.stack([np.stack([P[2048 + xi * 1024 + h * 128:2048 + xi * 1024 + (h + 1) * 128] for h in hds])
                                      for xi in range(3)])
                m["cw"] = np.stack([np.stack([convw[:, xi * 1024 + h * 128:xi * 1024 + (h + 1) * 128].T for h in hds])
                                    for xi in range(3)])
                m["gate"] = pt[b][:, 1024 + 2 * hp * 128:1024 + (2 * hp + 2) * 128]
                ba = np.zeros((2, 128, 64), np.float32)
                for hh, h in enumerate(hds):
                    ba[hh, :, :32] = pt[b][:, 2048 + h].reshape(32, 128).T
                    ba[hh, :, 32:] = pt[b][:, 2056 + h].reshape(32, 128).T
                m["ba"] = ba
                m["sc"] = np.stack([f(ab_a_log[e])[hds], f(ab_dt_bias[e])[hds]], axis=1)
                m["ong"] = f(ab_o_norm_g[e])
                maps.append(m)
            rb = _run(_prog("mix_even", build_mix_even), maps)
            for i in range(NCORE):
                b, hp = i // 4, i % 4
                yfull[b][:, 2 * hp * 128:(2 * hp + 2) * 128] = rb[i]["y"][:, :256]
                yfull[b][:, 1024 + 2 * hp * 128:1024 + (2 * hp + 2) * 128] = rb[i]["y"][:, 256:]
        else:
            o = l // 2
            lgt = f(c_lb_logits)
            coef = np.zeros(4, np.float32)
            coef[1:l + 1] = 1.0
            maps = []
            for i in range(NCORE):
                b, hp = i // 4, i % 4
                cs = slice(hp * 512, (hp + 1) * 512)
                m = dict(cst=cst, coef=coef, ong=f(c_o_norm_g[o]))
                m["qT"] = pf[b][cs].reshape(4, 128, T)
                m["f"] = pt[b][:, cs]
                m["iv"] = pt[b][:, 2048 + hp * 512:2048 + (hp + 1) * 512]
                m["gg"] = pt[b][:, 4096 + hp * 512:4096 + (hp + 1) * 512]
                m["lg"] = lgt[:, cs].reshape(4, 4, 128).transpose(1, 2, 0)
                maps.append(m)
            rb = _run(_prog("mix_odd", lambda: build_hgrn2(4)), maps)
            for i in range(NCORE):
                b, hp = i // 4, i % 4
                yfull[b][:, hp * 512:(hp + 1) * 512] = rb[i]["y"]
        ys = yfull.reshape(NCORE, TPC, D)
        mems = [mem[i // 4] for i in range(NCORE)]
        min_ = mix_inputs(l)
        if l + 1 < depth:
            kind, pin = pre_inputs(l + 1)
            res = _run(_prog(("tok", True, kind, False), lambda: build_tok(True, kind, False)),
                       [dict(min_, **pin, x=xcur[i], y=ys[i], mem=mems[i], ident=ident) for i in range(NCORE)])
        else:
            res = _run(_prog(("tok", True, None, True), lambda: build_tok(True, None, True)),
                       [dict(min_, gf=f(final_norm_g), x=xcur[i], y=ys[i], mem=mems[i], ident=ident) for i in range(NCORE)])
            out = np.stack([r["out"] for r in res]).reshape(B, T, D)
    return out.astype(np.float32)
```

```python
import numpy as np
from contextlib import ExitStack
import concourse.bass as bass
import concourse.mybir as mybir
from concourse.bass_utils import run_bass_kernel_spmd

F32 = mybir.dt.float32
BF16 = mybir.dt.bfloat16
AF = mybir.ActivationFunctionType
ALU = mybir.AluOpType
AX = mybir.AxisListType

D = 2048
DFF = 5504
T = 4096
B = 2
NCORE = 8
TPC = 1024
NTT = TPC // 128
EPS = 1e-6
AB_IN = 7184


class Tl:
    def __init__(self, h, name, nsub=1, psum=False):
        self.h = h
        self.name = name
        self.nsub = nsub
        self.psum = psum

    def __getitem__(self, k):
        return self.h[k]


class Ctx:
    SAME_ENG_SYNC = True

    def __init__(self, nc, es):
        self.nc = nc
        self.es = es
        self.eng = {"pe": nc.tensor, "act": nc.scalar, "dve": nc.vector, "pool": nc.gpsimd, "sp": nc.sync}
        self.sem = {}
        self.cnt = {}
        for e in ("pe", "act", "dve", "pool"):
            self.sem[e] = es.enter_context(nc.semaphore("s_" + e))
            self.cnt[e] = 0
        self.waited = {e: {} for e in self.eng}
        self.dq = {}
        for q in ("sp", "pool", "act"):
            n = 12 if q != "act" else 4
            self.dq[q] = {"sems": [es.enter_context(nc.semaphore("d_%s%d" % (q, i))) for i in range(n)],
                          "val": [0] * n, "i": 0}
        self.lastw = {}
        self.readers = {}
        self.psum_names = set()
        self.uid = 0
        self.ninstr = 0

    def sb(self, shape, dt, name=None, nsub=1):
        self.uid += 1
        name = (name or "t") + "_%d" % self.uid
        h = self.es.enter_context(self.nc.sbuf_tensor(name, list(shape), dt))
        return Tl(h, name, nsub)

    def ps(self, shape, dt, name=None):
        self.uid += 1
        name = (name or "p") + "_%d" % self.uid
        h = self.es.enter_context(self.nc.psum_tensor(name, list(shape), dt))
        self.psum_names.add(name)
        return Tl(h, name, 1, True)

    def dram(self, name, shape, dt, kind):
        h = self.nc.dram_tensor(name, list(shape), dt, kind=kind)
        return Tl(h.ap(), name, 1)

    def _keys(self, specs):
        ks = []
        for s in specs:
            if s is None:
                continue
            if isinstance(s, tuple):
                t, i = s
                ks.append((t.name, i))
            else:
                for i in range(s.nsub):
                    ks.append((s.name, i))
        return ks

    def _wait(self, e, tok):
        sem, val, te, sid = tok
        if te == e and (e in ("pe", "sp") or not self.SAME_ENG_SYNC):
            return
        w = self.waited[e]
        if w.get(sid, 0) >= val:
            return
        self.eng[e].wait_ge(sem, val)
        w[sid] = val

    def _deps(self, e, r, w):
        rk = self._keys(r)
        wk = self._keys(w)
        toks = {}
        def add(tok):
            if tok is None:
                return
            sid = tok[3]
            if sid not in toks or toks[sid][1] < tok[1]:
                toks[sid] = tok
        for k in rk:
            add(self.lastw.get(k))
            if k[0] in self.psum_names and e != "pe":
                for tok in self.readers.get(k, {}).values():
                    if tok[2] != "pe":
                        add(tok)
        for k in wk:
            add(self.lastw.get(k))
            for tok in self.readers.get(k, {}).values():
                add(tok)
        for tok in toks.values():
            self._wait(e, tok)
        return rk, wk

    def _commit(self, tok, rk, wk):
        for k in wk:
            self.lastw[k] = tok
            self.readers[k] = {}
        for k in rk:
            d = self.readers.setdefault(k, {})
            sid = tok[3]
            if sid not in d or d[sid][1] < tok[1]:
                d[sid] = tok

    def op(self, e, r, w, fn):
        rk, wk = self._deps(e, r, w)
        ins = fn()
        self.cnt[e] += 1
        ins.then_inc(self.sem[e], 1)
        self.ninstr += 1
        self._commit((self.sem[e], self.cnt[e], e, "c_" + e), rk, wk)
        return ins

    def pe(self, r, w, fn):
        return self.op("pe", r, w, fn)

    def act(self, r, w, fn):
        return self.op("act", r, w, fn)

    def dve(self, r, w, fn):
        return self.op("dve", r, w, fn)

    def pool(self, r, w, fn):
        return self.op("pool", r, w, fn)

    def dma(self, q, r, w, fn):
        rk, wk = self._deps(q, r, w)
        dq = self.dq[q]
        i = dq["i"]
        dq["i"] = (i + 1) % len(dq["sems"])
        sem = dq["sems"][i]
        sid = "d_%s%d" % (q, i)
        if dq["val"][i] > 0 and self.waited[q].get(sid, 0) < dq["val"][i]:
            self.eng[q].wait_ge(sem, dq["val"][i])
            self.waited[q][sid] = dq["val"][i]
        ins = fn(self.eng[q])
        ins.then_inc(sem, 16)
        dq["val"][i] += 16
        self.ninstr += 1
        self._commit((sem, dq["val"][i], "dma", sid), rk, wk)
        return ins

    def finish(self):
        for q, dq in self.dq.items():
            for i, sem in enumerate(dq["sems"]):
                if dq["val"][i] > 0:
                    self.nc.sync.wait_ge(sem, dq["val"][i])
        for e in ("pe", "act", "dve", "pool"):
            if self.cnt[e] > 0:
                self.nc.sync.wait_ge(self.sem[e], self.cnt[e])


class TokProg:
    def __init__(self, c):
        self.c = c
        nc = c.nc
        self.nc = nc
        self.ident_f = c.sb([128, 128], F32, "identf")
        self.ident_b = c.sb([128, 128], BF16, "identb")
        self.psb = [c.ps([128, 512], F32, "psb") for _ in range(6)]
        self.pst = [c.ps([128, 1024], BF16, "pst") for _ in range(2)]
        self.psi = 0
        self.pti = 0
        self.junk = c.sb([128, D], BF16, "junk")
        self.hn = [c.sb([128, D], BF16, "hn") for _ in range(2)]
        self.hni = 0
        self.gbc = c.sb([128, D], F32, "gbc")
        self.st = c.sb([128, 8], F32, "stat")
        self.junk2 = c.sb([128, D], F32, "junk2")
        self.stg = [c.sb([128, 512], F32, "stg") for _ in range(2)]
        self.stgi = 0
        self.wri = 0
        self.evi = 0

    def load_ident(self, ident_dram):
        c = self.c
        c.dma("sp", [], [self.ident_f], lambda e: e.dma_start(out=self.ident_f[:], in_=ident_dram[:, :]))
        c.dve([self.ident_f], [self.ident_b], lambda: self.nc.vector.tensor_copy(self.ident_b[:], self.ident_f[:]))

    def nps(self):
        p = self.psb[self.psi % len(self.psb)]
        self.psi += 1
        return p

    def npt(self):
        p = self.pst[self.pti % len(self.pst)]
        self.pti += 1
        return p

    def load_gain(self, g_ap):
        c = self.c
        c.dma("sp", [], [self.gbc], lambda e: e.dma_start(out=self.gbc[:], in_=g_ap.partition_broadcast(128)))

    def rmsnorm_rows(self, xt, out_ap_tile, out_ap, width=D, gbc=None):
        c, nc = self.c, self.nc
        gbc = gbc or self.gbc
        st = self.st
        c.act([xt], [self.junk, st], lambda: nc.scalar.activation(
            out=self.junk[:, :width], in_=xt[:, :width], func=AF.Square, accum_out=st[:, 0:1]))
        c.dve([st], [st], lambda: nc.vector.tensor_scalar(
            st[:, 1:2], st[:, 0:1], 1.0 / width, EPS, ALU.mult, ALU.add))
        c.act([st], [st], lambda: nc.scalar.sqrt(st[:, 3:4], st[:, 1:2]))
        c.dve([st], [st], lambda: nc.vector.reciprocal(st[:, 2:3], st[:, 3:4]))
        c.dve([xt, st, gbc], [out_ap_tile], lambda: nc.vector.scalar_tensor_tensor(
            out=out_ap, in0=xt[:, :width], scalar=st[:, 2:3], in1=gbc[:, :width], op0=ALU.mult, op1=ALU.mult))

    def norm_T(self, xt, hT, t):
        c, nc = self.c, self.nc
        hn = self.hn[self.hni % 2]
        self.hni += 1
        self.rmsnorm_rows(xt, hn, hn[:, :])
        for half in range(2):
            pt = self.npt()
            for kk in range(8):
                k = half * 8 + kk
                c.pe([hn, self.ident_b], [pt], lambda k=k, kk=kk: nc.tensor.transpose(
                    pt[:, kk * 128:(kk + 1) * 128], hn[:, k * 128:(k + 1) * 128], self.ident_b[:]))
            src = pt[:, :].rearrange("p (k n) -> p k n", k=8)
            dst = hT[:, half * 8:(half + 1) * 8, t * 128:(t + 1) * 128]
            if half == 0:
                c.act([pt], [(hT, t)], lambda: nc.scalar.copy(out=dst, in_=src))
            else:
                c.dve([pt], [(hT, t)], lambda: nc.vector.tensor_copy(dst, src))

    def ffn_alloc(self):
        c = self.c
        self.wa = [c.sb([128, 16, 256], BF16, "wa") for _ in range(2)]
        self.wb = [c.sb([128, 16, 256], BF16, "wb") for _ in range(2)]
        self.wo = c.sb([128, 4, D], BF16, "wo")
        self.actT = c.sb([128, 4, TPC], BF16, "actT")
        self.sA = [c.sb([128, 512], F32, "sA") for _ in range(2)]
        self.sAi = 0

    def ffn(self, xs, hT, w_in, w_out):
        c, nc = self.c, self.nc
        win = w_in.rearrange("(k p) n -> p k n", p=128)
        wout = w_out.rearrange("(j p) n -> p j n", p=128)
        NCH = DFF // 128
        hus = [(j, min(2, NCH - j)) for j in range(0, NCH, 2)]

        def load_hu(i):
            j0, n = hus[i]
            wa, wb = self.wa[i % 2], self.wb[i % 2]
            c.dma("pool", [], [wa], lambda e: e.dma_start(out=wa[:, :, :n * 128], in_=win[:, :, j0 * 128:(j0 + n) * 128]))
            c.dma("pool", [], [wb], lambda e: e.dma_start(out=wb[:, :, :n * 128], in_=win[:, :, DFF + j0 * 128:DFF + (j0 + n) * 128]))

        def load_wo(u):
            j0 = u * 4
            n = min(4, NCH - j0)
            c.dma("pool", [], [self.wo], lambda e: e.dma_start(out=self.wo[:, :n, :], in_=wout[:, j0:j0 + n, :]))

        load_hu(0)
        for i, (j0, n) in enumerate(hus):
            u = i // 2
            if i + 1 < len(hus):
                load_hu(i + 1)
            if i % 2 == 0:
                load_wo(u)
            wa, wb = self.wa[i % 2], self.wb[i % 2]
            for jj in range(n):
                j4 = (i % 2) * 2 + jj
                for th in range(2):
                    pA, pB = self.nps(), self.nps()
                    for (pp, ww) in ((pA, wa), (pB, wb)):
                        for k in range(16):
                            c.pe([ww, hT], [pp], lambda pp=pp, ww=ww, k=k: nc.tensor.matmul(
                                pp[:, :], ww[:, k, jj * 128:(jj + 1) * 128], hT[:, k, th * 512:(th + 1) * 512],
                                start=(k == 0), stop=(k == 15)))
                    sA = self.sA[self.sAi % 2]
                    self.sAi += 1
                    c.act([pA], [sA], lambda: nc.scalar.activation(out=sA[:, :], in_=pA[:, :], func=AF.Silu))
                    c.dve([sA, pB], [self.actT], lambda: nc.vector.tensor_tensor(
                        self.actT[:, j4, th * 512:(th + 1) * 512], sA[:, :], pB[:, :], ALU.mult))
            if i % 2 == 1 or i == len(hus) - 1:
                nj = min(4, NCH - u * 4)
                for cb in range(4):
                    for t in range(NTT):
                        pp = self.nps()
                        for j4 in range(nj):
                            c.pe([self.actT, self.wo], [pp], lambda j4=j4, pp=pp: nc.tensor.matmul(
                                pp[:, :], self.actT[:, j4, t * 128:(t + 1) * 128], self.wo[:, j4, cb * 512:(cb + 1) * 512],
                                start=(j4 == 0), stop=(j4 == nj - 1)))
                        xs_t = xs[t]
                        c.dve([pp, xs_t], [xs_t], lambda pp=pp, xs_t=xs_t: nc.vector.scalar_tensor_tensor(
                            out=xs_t[:, cb * 512:(cb + 1) * 512], in0=pp[:, :], scalar=0.5,
                            in1=xs_t[:, cb * 512:(cb + 1) * 512], op0=ALU.mult, op1=ALU.add))


    def wring(self):
        r = [self.wa[0], self.wb[0], self.wa[1], self.wb[1]]
        w = r[self.wri % 4]
        self.wri += 1
        return w

    def stage(self):
        s = self.stg[self.stgi % len(self.stg)]
        self.stgi += 1
        return s

    def evac(self, ps_ap, ps_t, dst_ap, dst_t):
        c, nc = self.c, self.nc
        self.evi += 1
        if self.evi % 2 == 0:
            c.act([ps_t], [dst_t], lambda: nc.scalar.copy(out=dst_ap, in_=ps_ap))
        else:
            c.dve([ps_t], [dst_t], lambda: nc.vector.tensor_copy(dst_ap, ps_ap))

    def plain_T(self, yt, hT, t):
        c, nc = self.c, self.nc
        hn = self.hn[self.hni % 2]
        self.hni += 1
        c.dve([yt], [hn], lambda: nc.vector.tensor_copy(hn[:, :], yt[:, :]))
        self._T16(hn, hT, t)

    def _T16(self, hn, hT, t, nk=16):
        c, nc = self.c, self.nc
        for half in range((nk + 7) // 8):
            pt = self.npt()
            n8 = min(8, nk - half * 8)
            for kk in range(n8):
                k = half * 8 + kk
                c.pe([hn, self.ident_b], [pt], lambda k=k, kk=kk: nc.tensor.transpose(
                    pt[:, kk * 128:(kk + 1) * 128], hn[:, k * 128:(k + 1) * 128], self.ident_b[:]))
            src = pt[:, :n8 * 128].rearrange("p (k n) -> p k n", k=n8)
            dst = hT[:, half * 8:half * 8 + n8, t * 128:(t + 1) * 128]
            self.evac(src, pt, dst, (hT, t))

    def inproj(self, hT, W, segs, pf, pt):
        c, nc = self.c, self.nc
        Wv = W.rearrange("(k p) n -> p k n", p=128)
        blocks = []
        for (c0, c1, mode, off) in segs:
            cc = c0
            while cc < c1:
                n = min(256, c1 - cc)
                blocks.append((cc, n, mode, off + cc - c0))
                cc += n

        def load(bi):
            cc, n, mode, off = blocks[bi]
            w = self.wring()
            c.dma("pool", [], [w], lambda e: e.dma_start(out=w[:, :, :n], in_=Wv[:, :, cc:cc + n]))
            return w
        wnext = load(0)
        for bi, (cc, n, mode, off) in enumerate(blocks):
            w = wnext
            if bi + 1 < len(blocks):
                wnext = load(bi + 1)
            if mode == "fm":
                for j in range(n // 128):
                    for th in range(2):
                        pp = self.nps()
                        for k in range(16):
                            c.pe([w, hT], [pp], lambda k=k, pp=pp: nc.tensor.matmul(
                                pp[:, :], w[:, k, j * 128:(j + 1) * 128], hT[:, k, th * 512:(th + 1) * 512],
                                start=(k == 0), stop=(k == 15)))
                        sg = self.stage()
                        self.evac(pp[:, :], pp, sg[:, :], sg)
                        c.dma("sp", [sg], [pf], lambda e, sg=sg: e.dma_start(
                            out=pf[off + j * 128:off + (j + 1) * 128, th * 512:(th + 1) * 512], in_=sg[:, :]))
            else:
                for t in range(NTT):
                    pp = self.nps()
                    for k in range(16):
                        c.pe([w, hT], [pp], lambda k=k, pp=pp: nc.tensor.matmul(
                            pp[:, :n], hT[:, k, t * 128:(t + 1) * 128], w[:, k, :n],
                            start=(k == 0), stop=(k == 15)))
                    sg = self.stage()
                    self.evac(pp[:, :n], pp, sg[:, :n], sg)
                    c.dma("sp", [sg], [pt], lambda e, sg=sg: e.dma_start(
                        out=pt[t * 128:(t + 1) * 128, off:off + n], in_=sg[:, :n]))

    def linear_res(self, xs, hT, W, N=D):
        c, nc = self.c, self.nc
        Wv = W.rearrange("(k p) n -> p k n", p=128)
        nb = N // 256

        def load(bi):
            w = self.wring()
            c.dma("pool", [], [w], lambda e: e.dma_start(out=w[:, :, :], in_=Wv[:, :, bi * 256:(bi + 1) * 256]))
            return w
        wnext = load(0)
        for bi in range(nb):
            w = wnext
            if bi + 1 < nb:
                wnext = load(bi + 1)
            for t in range(NTT):
                pp = self.nps()
                for k in range(16):
                    c.pe([w, hT], [pp], lambda k=k, pp=pp: nc.tensor.matmul(
                        pp[:, :256], hT[:, k, t * 128:(t + 1) * 128], w[:, k, :],
                        start=(k == 0), stop=(k == 15)))
                xt = xs[t]
                c.dve([pp, xt], [xt], lambda pp=pp, xt=xt: nc.vector.tensor_tensor(
                    xt[:, bi * 256:(bi + 1) * 256], pp[:, :256], xt[:, bi * 256:(bi + 1) * 256], ALU.add))

    def xattn(self, xs, hT, mem, mem_g, w_q, w_kv, w_o):
        c, nc = self.c, self.nc
        scale = 128 ** -0.5
        memT = c.sb([128, 16, 256], BF16, "memT", nsub=2)
        self.load_gain(mem_g)
        mt = self.junk2
        for t in range(2):
            c.dma("sp", [], [mt], lambda e, t=t: e.dma_start(out=mt[:, :], in_=mem[t * 128:(t + 1) * 128, :]))
            hn = self.hn[self.hni % 2]
            self.hni += 1
            self.rmsnorm_rows(mt, hn, hn[:, :])
            self._T16(hn, memT, t)
        kT = c.sb([128, 4, 256], BF16, "xkT")
        vv = c.sb([128, 2, 512], BF16, "xv")
        qT = self.actT
        Wkv = w_kv.rearrange("(k p) n -> p k n", p=128)
        Wq = w_q.rearrange("(k p) n -> p k n", p=128)
        for bi in range(2):
            w = self.wring()
            c.dma("pool", [], [w], lambda e, w=w: e.dma_start(out=w[:, :, :], in_=Wkv[:, :, bi * 256:(bi + 1) * 256]))
            for j in range(2):
                pp = self.nps()
                for k in range(16):
                    c.pe([w, memT], [pp], lambda k=k, pp=pp, w=w: nc.tensor.matmul(
                        pp[:, :256], w[:, k, j * 128:(j + 1) * 128], memT[:, k, :], start=(k == 0), stop=(k == 15)))
                self.evac(pp[:, :256], pp, kT[:, bi * 2 + j, :], kT)
        for bi in range(2):
            w = self.wring()
            c.dma("pool", [], [w], lambda e, w=w: e.dma_start(out=w[:, :, :], in_=Wkv[:, :, 512 + bi * 256:512 + (bi + 1) * 256]))
            for t in range(2):
                pp = self.nps()
                for k in range(16):
                    c.pe([w, memT], [pp], lambda k=k, pp=pp, w=w: nc.tensor.matmul(
                        pp[:, :256], memT[:, k, t * 128:(t + 1) * 128], w[:, k, :], start=(k == 0), stop=(k == 15)))
                self.evac(pp[:, :256], pp, vv[:, t, bi * 256:(bi + 1) * 256], vv)
        for bi in range(2):
            w = self.wring()
            c.dma("pool", [], [w], lambda e, w=w: e.dma_start(out=w[:, :, :], in_=Wq[:, :, bi * 256:(bi + 1) * 256]))
            for j in range(2):
                for th in range(2):
                    pp = self.nps()
                    for k in range(16):
                        c.pe([w, hT], [pp], lambda k=k, pp=pp, w=w: nc.tensor.matmul(
                            pp[:, :], w[:, k, j * 128:(j + 1) * 128], hT[:, k, th * 512:(th + 1) * 512],
                            start=(k == 0), stop=(k == 15)))
                    self.evac(pp[:, :], pp, qT[:, bi * 2 + j, th * 512:(th + 1) * 512], qT)
        Wo = w_o.rearrange("(k p) n -> p k n", p=128)
        c.dma("pool", [], [self.wo], lambda e: e.dma_start(out=self.wo[:, :, :], in_=Wo[:, :, :]))
        Ps = [[c.sb([128, 256], BF16, "xP") for _ in range(4)]] * 2
        PTs = [c.sb([128, 4, 256], BF16, "xPT")] * 2
        osbs = [c.sb([128, 512], BF16, "xo")] * 2
        oTs = [c.sb([128, 4, 128], BF16, "xoT")] * 2
        sts = [[c.sb([128, 8], F32, "xst") for _ in range(4)] for _ in range(2)]
        for t in range(NTT):
            P, PT, osb, oT, st = Ps[t % 2], PTs[t % 2], osbs[t % 2], oTs[t % 2], sts[t % 2]
            pps = [self.nps() for _ in range(4)]
            for h in range(4):
                c.pe([qT, kT], [pps[h]], lambda h=h: nc.tensor.matmul(
                    pps[h][:, :256], qT[:, h, t * 128:(t + 1) * 128], kT[:, h, :], start=True, stop=True))
            for h in range(4):
                c.dve([pps[h]], [st[h]], lambda h=h: nc.vector.reduce_max(st[h][:, 0:1], pps[h][:, :256], axis=AX.X))
                c.dve([st[h]], [st[h]], lambda h=h: nc.vector.tensor_scalar(st[h][:, 1:2], st[h][:, 0:1], -scale, None, ALU.mult))
            for h in range(4):
                c.act([pps[h], st[h]], [P[h], st[h]], lambda h=h: nc.scalar.activation(
                    out=P[h][:, :], in_=pps[h][:, :256], func=AF.Exp, bias=st[h][:, 1:2], scale=scale, accum_out=st[h][:, 2:3]))
            for h in range(4):
                c.dve([st[h]], [st[h]], lambda h=h: nc.vector.reciprocal(st[h][:, 3:4], st[h][:, 2:3]))
            ptp = self.npt()
            for h in range(4):
                for mc in range(2):
                    c.pe([P[h], self.ident_b], [ptp], lambda mc=mc, h=h: nc.tensor.transpose(
                        ptp[:, h * 256 + mc * 128:h * 256 + (mc + 1) * 128], P[h][:, mc * 128:(mc + 1) * 128], self.ident_b[:]))
            c.act([ptp], [PT], lambda: nc.scalar.copy(
                out=PT[:, :, :], in_=ptp[:, :1024].rearrange("p (k n) -> p k n", k=4)))
            po = self.nps()
            for h in range(4):
                for mc in range(2):
                    c.pe([PT, vv], [po], lambda mc=mc, h=h: nc.tensor.matmul(
                        po[:, h * 128:(h + 1) * 128], PT[:, h, mc * 128:(mc + 1) * 128], vv[:, mc, h * 128:(h + 1) * 128],
                        start=(mc == 0), stop=(mc == 1)))
            for h in range(4):
                c.dve([po, st[h]], [osb], lambda h=h: nc.vector.tensor_scalar(
                    osb[:, h * 128:(h + 1) * 128], po[:, h * 128:(h + 1) * 128], st[h][:, 3:4], None, ALU.mult))
            ptp2 = self.npt()
            for h in range(4):
                c.pe([osb, self.ident_b], [ptp2], lambda h=h: nc.tensor.transpose(
                    ptp2[:, h * 128:(h + 1) * 128], osb[:, h * 128:(h + 1) * 128], self.ident_b[:]))
            c.act([ptp2], [oT], lambda: nc.scalar.copy(
                out=oT[:, :, :], in_=ptp2[:, :512].rearrange("p (k n) -> p k n", k=4)))
            xt = xs[t]
            for cb in range(4):
                pp = self.nps()
                for h in range(4):
                    c.pe([oT, self.wo], [pp], lambda h=h, pp=pp: nc.tensor.matmul(
                        pp[:, :], oT[:, h, :], self.wo[:, h, cb * 512:(cb + 1) * 512], start=(h == 0), stop=(h == 3)))
                c.dve([pp, xt], [xt], lambda pp=pp: nc.vector.tensor_tensor(
                    xt[:, cb * 512:(cb + 1) * 512], pp[:, :], xt[:, cb * 512:(cb + 1) * 512], ALU.add))


def new_nc():
    return bass.Bass("TRN2", target_bir_lowering=False)


EVEN_SEGS = [(0, 2048, "fm", 0), (3072, 6144, "fm", 2048),
             (2048, 3072, "tm", 0), (6144, 7168, "tm", 1024), (7168, 7184, "tm", 2048)]
ODD_SEGS = [(0, 2048, "fm", 0), (2048, 8192, "tm", 0)]
NF = {"even": 5120, "odd": 2048}
NT = {"even": 2064, "odd": 6144}
NWIN = {"even": AB_IN, "odd": 8192}


def build_tok(mix, pre, final):
    nc = new_nc()
    es = ExitStack()
    with es:
        c = Ctx(nc, es)
        di = lambda n, sh: c.dram(n, sh, F32, "ExternalInput")
        x = di("x", [TPC, D])
        ident = di("ident", [128, 128])
        if mix:
            y = di("y", [TPC, D]); w_mo = di("w_mo", [D, D]); mem = di("mem", [256, D]); mem_g = di("mem_g", [D])
            w_q = di("w_q", [D, 512]); w_kv = di("w_kv", [D, 1024]); w_o = di("w_o", [512, D])
            g2 = di("g2", [D]); g3 = di("g3", [D]); f1_in = di("f1_in", [D, 2 * DFF]); f1_out = di("f1_out", [DFF, D])
        if pre:
            g0 = di("g0", [D]); g1 = di("g1", [D]); f0_in = di("f0_in", [D, 2 * DFF]); f0_out = di("f0_out", [DFF, D])
            w_in = di("w_in", [D, NWIN[pre]])
            xo = c.dram("xo", [TPC, D], F32, "ExternalOutput")
            pf = c.dram("pf", [NF[pre], TPC], F32, "ExternalOutput")
            pt = c.dram("pt", [TPC, NT[pre]], F32, "ExternalOutput")
        if final:
            gf = di("gf", [D])
            out = c.dram("out", [TPC, D], F32, "ExternalOutput")
        tp = TokProg(c)
        tp.load_ident(ident)
        tp.ffn_alloc()
        xs = [c.sb([128, D], F32, "x") for _ in range(NTT)]
        hT = c.sb([128, 16, TPC], BF16, "hT", nsub=NTT)
        for t in range(NTT):
            c.dma("sp", [], [xs[t]], lambda e, t=t: e.dma_start(out=xs[t][:, :], in_=x[t * 128:(t + 1) * 128, :]))
        if mix:
            for t in range(NTT):
                yt = tp.junk2
                c.dma("sp", [], [yt], lambda e, t=t: e.dma_start(out=yt[:, :], in_=y[t * 128:(t + 1) * 128, :]))
                tp.plain_T(yt, hT, t)
            tp.linear_res(xs, hT, w_mo.h)
            tp.load_gain(g2.h)
            for t in range(NTT):
                tp.norm_T(xs[t], hT, t)
            tp.xattn(xs, hT, mem.h, mem_g.h, w_q.h, w_kv.h, w_o.h)
            tp.load_gain(g3.h)
            for t in range(NTT):
                tp.norm_T(xs[t], hT, t)
            tp.ffn(xs, hT, f1_in.h, f1_out.h)
        if pre:
            tp.load_gain(g0.h)
            for t in range(NTT):
                tp.norm_T(xs[t], hT, t)
            tp.ffn(xs, hT, f0_in.h, f0_out.h)
            for t in range(NTT):
                c.dma("sp", [xs[t]], [xo], lambda e, t=t: e.dma_start(out=xo[t * 128:(t + 1) * 128, :], in_=xs[t][:, :]))
            tp.load_gain(g1.h)
            for t in range(NTT):
                tp.norm_T(xs[t], hT, t)
            tp.inproj(hT, w_in.h, EVEN_SEGS if pre == "even" else ODD_SEGS, pf, pt)
        if final:
            tp.load_gain(gf.h)
            for t in range(NTT):
                ot = tp.junk2
                tp.rmsnorm_rows(xs[t], ot, ot[:, :])
                c.dma("sp", [ot], [out], lambda e, t=t: e.dma_start(out=out[t * 128:(t + 1) * 128, :], in_=ot[:, :]))
        c.finish()
    return nc, c


def build_ffn_test():
    nc = new_nc()
    es = ExitStack()
    with es:
        c = Ctx(nc, es)
        x = c.dram("x", [TPC, D], F32, "ExternalInput")
        g = c.dram("g", [D], F32, "ExternalInput")
        w_in = c.dram("w_in", [D, 2 * DFF], F32, "ExternalInput")
        w_out = c.dram("w_out", [DFF, D], F32, "ExternalInput")
        ident = c.dram("ident", [128, 128], F32, "ExternalInput")
        y = c.dram("y", [TPC, D], F32, "ExternalOutput")
        tp = TokProg(c)
        tp.load_ident(ident)
        tp.ffn_alloc()
        xs = [c.sb([128, D], F32, "x") for _ in range(NTT)]
        hT = c.sb([128, 16, TPC], BF16, "hT", nsub=NTT)
        tp.load_gain(g.h)
        for t in range(NTT):
            c.dma("sp", [], [xs[t]], lambda e, t=t: e.dma_start(out=xs[t][:, :], in_=x[t * 128:(t + 1) * 128, :]))
        for t in range(NTT):
            tp.norm_T(xs[t], hT, t)
        tp.ffn(xs, hT, w_in.h, w_out.h)
        for t in range(NTT):
            c.dma("sp", [xs[t]], [y], lambda e, t=t: e.dma_start(out=y[t * 128:(t + 1) * 128, :], in_=xs[t][:, :]))
        c.finish()
    return nc, c


class MixProg:
    def __init__(self, c, cst):
        self.c = c
        nc = c.nc
        self.nc = nc
        self.K = c.sb([128, 5, 128], F32, "consts")
        c.dma("sp", [], [self.K], lambda e: e.dma_start(out=self.K[:, :, :], in_=cst.rearrange("k p n -> p k n")))
        self.identf = self.K[:, 0, :]
        self.U1 = self.K[:, 1, :]
        self.U2 = self.K[:, 2, :]
        self.ones = self.K[:, 3, :]
        self.Kb = c.sb([128, 5, 128], BF16, "constsb")
        c.dve([self.K], [self.Kb], lambda: nc.vector.tensor_copy(self.Kb[:, :, :], self.K[:, :, :]))
        self.identb = self.Kb[:, 0, :]
        self.triTb = self.Kb[:, 4, :]
        self.pb = [c.ps([128, 512], F32, "pm") for _ in range(8)]
        self.prings = {"all": [list(range(8)), 0]}
        self.rings = {}

    def nps(self, ring="all"):
        r = self.prings[ring]
        p = self.pb[r[0][r[1] % len(r[0])]]
        r[1] += 1
        return p

    def ring(self, key, shape, dt, n=2):
        if key not in self.rings:
            self.rings[key] = [[self.c.sb(shape, dt, key) for _ in range(n)], 0]
        r = self.rings[key]
        t = r[0][r[1] % n]
        r[1] += 1
        return t

    def rms_gate_out(self, o_ps, gbc, gt, y_dram, row0, col0, tag):
        for _ in self.rms_gate_out_g(o_ps, gbc, gt, y_dram, row0, col0, tag):
            pass

    def rms_gate_out_g(self, o_ps, gbc, gt, y_dram, row0, col0, tag):
        c, nc = self.c, self.nc
        st = self.ring(tag + "st", [128, 8], F32, 3)
        jk = self.ring(tag + "jk", [128, 128], F32, 2)
        c.act([o_ps], [jk, st], lambda: nc.scalar.activation(
            out=jk[:, :], in_=o_ps[:, :128], func=AF.Square, accum_out=st[:, 0:1]))
        c.dve([st], [st], lambda: nc.vector.tensor_scalar(st[:, 1:2], st[:, 0:1], 1.0 / 128, EPS, ALU.mult, ALU.add))
        yield
        c.act([st], [st], lambda: nc.scalar.sqrt(st[:, 3:4], st[:, 1:2]))
        c.dve([st], [st], lambda: nc.vector.reciprocal(st[:, 2:3], st[:, 3:4]))
        on = self.ring(tag + "on", [128, 128], F32, 2)
        c.dve([o_ps, st, gbc], [on], lambda: nc.vector.scalar_tensor_tensor(
            out=on[:, :], in0=o_ps[:, :128], scalar=st[:, 2:3], in1=gbc[:, :], op0=ALU.mult, op1=ALU.mult))
        yield
        sg = self.ring(tag + "sg", [128, 128], F32, 2)
        c.act([gt], [sg], lambda: nc.scalar.activation(out=sg[:, :], in_=gt[:, :], func=AF.Silu))
        yt = self.ring(tag + "yt", [128, 128], F32, 3)
        c.dve([on, sg], [yt], lambda: nc.vector.tensor_tensor(yt[:, :], on[:, :], sg[:, :], ALU.mult))
        c.dma("sp", [yt], [y_dram], lambda e: e.dma_start(out=y_dram[row0:row0 + 128, col0:col0 + 128], in_=yt[:, :]))

    def hgrn2(self, qT, f, iv, gg, lg, coef, ong, y, nheads=4):
        c, nc = self.c, self.nc
        assert nheads == 4
        scale = 128 ** -0.5
        NR = T // 128
        W = 512
        v3 = lambda ap: ap.rearrange("p (h n) -> p h n", h=4)
        gbc = c.sb([128, 128], F32, "hg_gbc")
        c.dma("sp", [], [gbc], lambda e: e.dma_start(out=gbc[:, :], in_=ong.partition_broadcast(128)))
        cf = c.sb([128, 4], F32, "hg_coef")
        c.dma("sp", [], [cf], lambda e: e.dma_start(out=cf[:, :], in_=coef.partition_broadcast(128)))
        lbB = c.sb([128, W], F32, "lbB")
        omlB = c.sb([128, W], F32, "omlB")
        S = c.sb([128, W], F32, "S")
        Sb = c.sb([128, W], BF16, "Sb")
        pp = self.nps()
        for h in range(4):
            lgt = c.sb([128, 4], F32, "lgt")
            c.dma("sp", [], [lgt], lambda e, h=h, lgt=lgt: e.dma_start(out=lgt[:, :], in_=lg[h]))
            st = c.sb([128, 8], F32, "lbst")
            c.dve([lgt], [st], lambda: nc.vector.reduce_max(st[:, 0:1], lgt[:, :], axis=AX.X))
            c.dve([st], [st], lambda: nc.vector.tensor_scalar(st[:, 1:2], st[:, 0:1], -1.0, None, ALU.mult))
            ee = c.sb([128, 4], F32, "lbe")
            c.act([lgt, st], [ee, st], lambda: nc.scalar.activation(
                out=ee[:, :], in_=lgt[:, :], func=AF.Exp, bias=st[:, 1:2], scale=1.0, accum_out=st[:, 2:3]))
            c.dve([st], [st], lambda: nc.vector.reciprocal(st[:, 3:4], st[:, 2:3]))
            c.dve([ee, cf], [ee], lambda: nc.vector.tensor_tensor(ee[:, :], ee[:, :], cf[:, :], ALU.mult))
            c.dve([ee], [st], lambda: nc.vector.reduce_sum(st[:, 4:5], ee[:, :], axis=AX.X))
            c.dve([st], [st], lambda: nc.vector.tensor_tensor(st[:, 5:6], st[:, 4:5], st[:, 3:4], ALU.mult))
            lbc = c.sb([128, 128], F32, "lbcB")
            c.dve([st, self.K], [lbc], lambda: nc.vector.tensor_scalar(lbc[:, :], self.ones, st[:, 5:6], None, ALU.mult))
            c.pe([lbc, self.K], [pp], lambda h=h, lbc=lbc: nc.tensor.matmul(
                pp[:, h * 128:(h + 1) * 128], lbc[:, :], self.identf, start=True, stop=True))
        c.act([pp], [lbB], lambda: nc.scalar.copy(out=lbB[:, :], in_=pp[:, :]))
        c.dve([pp], [omlB], lambda: nc.vector.tensor_scalar(omlB[:, :], pp[:, :], -1.0, 1.0, ALU.mult, ALU.add))
        c.dve([], [S], lambda: nc.vector.memset(S[:, :], 0.0))
        c.dve([], [Sb], lambda: nc.vector.memset(Sb[:, :], 0.0))
        qTv = qT.rearrange("h c t -> c h t")
        RF = lambda k, n=2: self.ring("hg_" + k, [128, W], F32, n)
        RB = lambda k, n=2: self.ring("hg_" + k, [128, W], BF16, n)

        def prep(r, out):
            rows = slice(r * 128, (r + 1) * 128)
            ft, gt, qt = RF("ft"), RF("gt", 3), RF("qt")
            vt = RB("vt", 3)
            c.dma("sp", [], [ft], lambda e: e.dma_start(out=ft[:, :], in_=f[rows, :]))
            c.dma("pool", [], [vt], lambda e: e.dma_start(out=vt[:, :], in_=iv[rows, :]))
            c.dma("sp", [], [gt], lambda e: e.dma_start(out=gt[:, :], in_=gg[rows, :]))
            c.dma("sp", [], [qt], lambda e: e.dma_start(out=v3(qt[:, :]), in_=qTv[:, :, rows]))
            sg = RF("sig")
            c.act([ft], [sg], lambda: nc.scalar.activation(out=sg[:, :], in_=ft[:, :], func=AF.Sigmoid))
            fg = RF("fg")
            c.dve([sg, omlB], [fg], lambda: nc.vector.tensor_tensor(fg[:, :], sg[:, :], omlB[:, :], ALU.mult))
            c.dve([fg, lbB], [fg], lambda: nc.vector.tensor_tensor(fg[:, :], fg[:, :], lbB[:, :], ALU.add))
            sq = RF("sq")
            c.act([qt], [sq], lambda: nc.scalar.activation(out=sq[:, :], in_=qt[:, :], func=AF.Silu))
            sgt = RF("sgt", 3)
            c.act([gt], [sgt], lambda: nc.scalar.activation(out=sgt[:, :], in_=gt[:, :], func=AF.Silu))
            logf = RF("logf")
            c.act([fg], [logf], lambda: nc.scalar.activation(out=logf[:, :], in_=fg[:, :], func=AF.Ln))
            kt = RF("kt")
            c.pool([fg], [kt], lambda: nc.gpsimd.tensor_scalar(kt[:, :], fg[:, :], -1.0, 1.0, ALU.mult, ALU.add))
            pbT, pblr, pkT = self.nps(), self.nps(), self.nps()
            for h in range(4):
                hs_ = slice(h * 128, (h + 1) * 128)
                c.pe([logf, self.K], [pbT], lambda hs_=hs_: nc.tensor.matmul(pbT[:, hs_], logf[:, hs_], self.U1, start=True, stop=True))
            c.pe([logf, self.K], [pblr], lambda: nc.tensor.matmul(pblr[:, :], self.U2, logf[:, :], start=True, stop=True))
            for h in range(4):
                hs_ = slice(h * 128, (h + 1) * 128)
                c.pe([kt, self.K], [pkT], lambda hs_=hs_: nc.tensor.transpose(pkT[:, hs_], kt[:, hs_], self.identf))
            bm = self.ring("hg_bm", [128, 4], F32, 3)
            c.dve([pbT], [bm], lambda: nc.vector.tensor_copy(bm[:, :], v3(pbT[:, :])[:, :, 63]))
            D1 = RF("D1")
            c.dve([pbT, bm], [D1], lambda: nc.vector.tensor_tensor(
                v3(D1[:, :]), v3(pbT[:, :]), bm[:, :].unsqueeze(2).to_broadcast([128, 4, 128]), ALU.subtract))
            e1, e2, e3, eb = RF("e1"), RF("e2"), RF("e3"), RF("eb")
            c.act([D1], [e1], lambda: nc.scalar.activation(out=e1[:, :], in_=D1[:, :], func=AF.Exp))
            c.act([D1], [e2], lambda: nc.scalar.activation(out=e2[:, :], in_=D1[:, :], func=AF.Exp, scale=-1.0))
            c.act([pbT], [e3], lambda: nc.scalar.activation(out=e3[:, :], in_=pbT[:, :], func=AF.Exp))
            c.act([pblr], [eb], lambda: nc.scalar.activation(out=eb[:, :], in_=pblr[:, :], func=AF.Exp))
            qtil, qhat, ktil, kdec = RB("qtil"), RB("qhat", 3), RB("ktil"), RB("kdec", 3)
            c.dve([sq, e1], [qtil], lambda: nc.vector.scalar_tensor_tensor(
                out=qtil[:, :], in0=sq[:, :], scalar=scale, in1=e1[:, :], op0=ALU.mult, op1=ALU.mult))
            c.dve([sq, e3], [qhat], lambda: nc.vector.scalar_tensor_tensor(
                out=qhat[:, :], in0=sq[:, :], scalar=scale, in1=e3[:, :], op0=ALU.mult, op1=ALU.mult))
            c.dve([pkT, e2], [ktil], lambda: nc.vector.tensor_tensor(ktil[:, :], pkT[:, :], e2[:, :], ALU.mult))
            c.pool([kt, eb], [kdec], lambda: nc.gpsimd.tensor_tensor(kdec[:, :], kt[:, :], eb[:, :], ALU.mult))
            el = self.ring("hg_el", [128, 4], F32, 3)
            c.dve([e3], [el], lambda: nc.vector.tensor_copy(el[:, :], v3(e3[:, :])[:, :, 127]))
            pa = self.nps()
            for h in range(4):
                hs_ = slice(h * 128, (h + 1) * 128)
                c.pe([ktil, qtil], [pa], lambda hs_=hs_: nc.tensor.matmul(pa[:, hs_], ktil[:, hs_], qtil[:, hs_], start=True, stop=True))
            aT = RB("aT", 3)
            c.dve([pa, self.K], [aT], lambda: nc.vector.tensor_tensor(
                v3(aT[:, :]), v3(pa[:, :]), self.U1.unsqueeze(1).to_broadcast([128, 4, 128]), ALU.mult))
            out.update(vt=vt, gt=sgt, qhat=qhat, aT=aT, kdec=kdec, el=el)

        def seq(r, p):
            vt, sgt, qhat, aT, kdec, el = (p[k] for k in ("vt", "gt", "qhat", "aT", "kdec", "el"))
            po, pS = self.nps(), self.nps()
            for h in range(4):
                hs_ = slice(h * 128, (h + 1) * 128)
                c.pe([aT, vt], [po], lambda hs_=hs_: nc.tensor.matmul(po[:, hs_], aT[:, hs_], vt[:, hs_], start=True, stop=False))
                c.pe([qhat, Sb], [po], lambda hs_=hs_: nc.tensor.matmul(po[:, hs_], qhat[:, hs_], Sb[:, hs_], start=False, stop=True))
            for h in range(4):
                hs_ = slice(h * 128, (h + 1) * 128)
                c.pe([kdec, vt], [pS], lambda hs_=hs_: nc.tensor.matmul(pS[:, hs_], kdec[:, hs_], vt[:, hs_], start=True, stop=True))
            c.dve([S, el], [S], lambda: nc.vector.tensor_tensor(
                v3(S[:, :]), v3(S[:, :]), el[:, :].unsqueeze(2).to_broadcast([128, 4, 128]), ALU.mult))
            c.dve([S, pS], [S], lambda: nc.vector.tensor_tensor(S[:, :], S[:, :], pS[:, :], ALU.add))
            c.pool([S], [Sb], lambda: nc.gpsimd.tensor_copy(Sb[:, :], S[:, :]))
            o2 = RF("o2")
            c.act([po], [o2], lambda: nc.scalar.activation(out=o2[:, :], in_=po[:, :], func=AF.Square))
            st = self.ring("hg_st", [128, 16], F32, 3)
            c.dve([o2], [st], lambda: nc.vector.tensor_reduce(out=st[:, 0:4], in_=v3(o2[:, :]), axis=AX.X, op=ALU.add))
            c.dve([st], [st], lambda: nc.vector.tensor_scalar(st[:, 4:8], st[:, 0:4], 1.0 / 128, EPS, ALU.mult, ALU.add))
            c.act([st], [st], lambda: nc.scalar.activation(out=st[:, 8:12], in_=st[:, 4:8], func=AF.Ln))
            c.act([st], [st], lambda: nc.scalar.activation(out=st[:, 12:16], in_=st[:, 8:12], func=AF.Exp, scale=-0.5))
            on = RF("on")
            c.dve([po, st], [on], lambda: nc.vector.tensor_tensor(
                v3(on[:, :]), v3(po[:, :]), st[:, 12:16].unsqueeze(2).to_broadcast([128, 4, 128]), ALU.mult))
            c.pool([on, gbc], [on], lambda: nc.gpsimd.tensor_tensor(
                v3(on[:, :]), v3(on[:, :]), gbc[:, :].unsqueeze(1).to_broadcast([128, 4, 128]), ALU.mult))
            yt = RF("yt", 3)
            c.dve([on, sgt], [yt], lambda: nc.vector.tensor_tensor(yt[:, :], on[:, :], sgt[:, :], ALU.mult))
            c.dma("sp", [yt], [y], lambda e: e.dma_start(out=y[r * 128:(r + 1) * 128, :], in_=yt[:, :]))

        P = {0: {}}
        prep(0, P[0])
        for r in range(NR):
            if r + 1 < NR:
                P[r + 1] = {}
                prep(r + 1, P[r + 1])
            seq(r, P.pop(r))


def lockstep(gens):
    gens = list(gens)
    while gens:
        nxt = []
        for g in gens:
            try:
                next(g)
                nxt.append(g)
            except StopIteration:
                pass
        gens = nxt


def mix_consts():
    i = np.arange(128)
    ident = np.eye(128, dtype=np.float32)
    U1 = (i[:, None] <= i[None, :]).astype(np.float32)
    U2 = (i[:, None] > i[None, :]).astype(np.float32)
    ones = np.ones((128, 128), np.float32)
    triT = np.where(i[:, None] <= i[None, :], 0.0, -30000.0).astype(np.float32)
    return np.stack([ident, U1, U2, ones, triT]).astype(np.float32)


def build_hgrn2(nheads=4):
    nc = new_nc()
    es = ExitStack()
    with es:
        c = Ctx(nc, es)
        di = lambda n, sh: c.dram(n, sh, F32, "ExternalInput")
        cst = di("cst", [5, 128, 128])
        qT = di("qT", [nheads, 128, T]); f = di("f", [T, nheads * 128]); iv = di("iv", [T, nheads * 128]); gg = di("gg", [T, nheads * 128])
        lg = di("lg", [nheads, 128, 4]); coef = di("coef", [4]); ong = di("ong", [128])
        y = c.dram("y", [T, nheads * 128], F32, "ExternalOutput")
        mp = MixProg(c, cst.h)
        mp.hgrn2(qT.h, f.h, iv.h, gg.h, lg.h, coef.h, ong.h, y, nheads)
        c.finish()
    return nc, c


def lockstep_g(gens):
    gens = list(gens)
    while gens:
        nxt = []
        for g in gens:
            try:
                next(g)
                nxt.append(g)
            except StopIteration:
                pass
        gens = nxt
        yield


def _gdn_g(self, qkvT, cw, gate, ba, sc, ong, y, nheads=2, col_off=0, bigs=None, pr="all"):
    c, nc = self.c, self.nc
    NR = T // 128
    I, U1, U2, ONES = self.identf, self.U1, self.U2, self.ones
    nps = lambda: self.nps(pr)
    gbc = c.sb([128, 128], F32, "dn_gbc")
    c.dma("sp", [], [gbc], lambda e: e.dma_start(out=gbc[:, :], in_=ong.partition_broadcast(128)))
    HS = []
    for h in range(nheads):
        d = {}
        d["cw"] = [c.sb([128, 4], F32, "dn_cw") for _ in range(3)]
        for xi in range(3):
            c.dma("sp", [], [d["cw"][xi]], lambda e, xi=xi: e.dma_start(out=d["cw"][xi][:, :], in_=cw[xi, h]))
        bat = c.sb([128, 64], F32, "dn_bat")
        sct = c.sb([128, 2], F32, "dn_sct")
        c.dma("sp", [], [bat], lambda e: e.dma_start(out=bat[:, :], in_=ba[h]))
        c.dma("sp", [], [sct], lambda e: e.dma_start(out=sct[:, :], in_=sc[h].partition_broadcast(128)))
        beta = c.sb([128, 32], F32, "dn_beta")
        nbeta = c.sb([128, 32], F32, "dn_nbeta")
        gg = c.sb([128, 32], F32, "dn_g")
        ea = c.sb([128, 1], F32, "dn_ea")
        c.act([bat], [beta], lambda: nc.scalar.activation(out=beta[:, :], in_=bat[:, 0:32], func=AF.Sigmoid))
        c.dve([beta], [nbeta], lambda: nc.vector.tensor_scalar(nbeta[:, :], beta[:, :], -1.0, None, ALU.mult))
        c.act([bat, sct], [gg], lambda: nc.scalar.activation(out=gg[:, :], in_=bat[:, 32:64], func=AF.Exp, bias=sct[:, 1:2], scale=1.0))
        c.dve([gg], [gg], lambda: nc.vector.tensor_scalar(gg[:, :], gg[:, :], 1.0, None, ALU.add))
        c.act([gg], [gg], lambda: nc.scalar.activation(out=gg[:, :], in_=gg[:, :], func=AF.Ln))
        c.act([sct], [ea], lambda: nc.scalar.activation(out=ea[:, :], in_=sct[:, 0:1], func=AF.Exp))
        c.dve([gg, ea], [gg], lambda: nc.vector.tensor_scalar(gg[:, :], gg[:, :], ea[:, 0:1], -1.0, ALU.mult, ALU.mult))
        S = c.sb([128, 128], F32, "dn_S")
        c.dve([], [S], lambda: nc.vector.memset(S[:, :], 0.0))
        d.update(beta=beta, nbeta=nbeta, gg=gg, S=S)
        HS.append(d)
    yield

    def prep(r, h, out):
        d = HS[h]
        beta, nbeta, gg = d["beta"], d["nbeta"], d["gg"]
        tg = "dn%d_" % h
        R_ = lambda k, n=2: self.ring(tg + k, [128, 128], F32, n)
        rows = slice(r * 128, (r + 1) * 128)
        gcol = gg[:, r:r + 1]
        gt = self.ring(tg + "gt", [128, 128], F32, 3)
        c.dma("sp", [], [gt], lambda e: e.dma_start(out=gt[:, :], in_=gate[rows, h * 128:(h + 1) * 128]))
        X = []
        for xi in range(3):
            rw = self.ring(tg + "raw%d" % xi, [128, 131], F32, 2)
            if r == 0:
                c.dve([], [rw], lambda: nc.vector.memset(rw[:, 0:3], 0.0))
                c.dma("sp", [], [rw], lambda e: e.dma_start(out=rw[:, 3:131], in_=qkvT[xi, h][:, 0:128]))
            else:
                c.dma("sp", [], [rw], lambda e: e.dma_start(out=rw[:, :], in_=qkvT[xi, h][:, r * 128 - 3:(r + 1) * 128]))
            cwt = d["cw"][xi]
            acc = R_("acc%d" % xi)
            eng = c.dve
            ee = nc.vector
            eng([rw, cwt], [acc], lambda: ee.tensor_scalar(acc[:, :], rw[:, 3:131], cwt[:, 3:4], None, ALU.mult))
            for sh in (1, 2, 3):
                eng([rw, cwt, acc], [acc], lambda sh=sh: ee.scalar_tensor_tensor(
                    out=acc[:, :], in0=rw[:, 3 - sh:131 - sh], scalar=cwt[:, 3 - sh:4 - sh], in1=acc[:, :],
                    op0=ALU.mult, op1=ALU.add))
            X.append(acc)
        yield
        XS = []
        for xi in range(3):
            xs_ = R_("x%d" % xi)
            c.act([X[xi]], [xs_], lambda xi=xi, xs_=xs_: nc.scalar.activation(out=xs_[:, :], in_=X[xi][:, :], func=AF.Silu))
            XS.append(xs_)
        qs, ks, vs = XS
        yield
        for xi in range(2):
            Xt = XS[xi]
            sq = R_("sq%d" % xi)
            c.act([Xt], [sq], lambda: nc.scalar.activation(out=sq[:, :], in_=Xt[:, :], func=AF.Square))
            pp = nps()
            c.pe([sq, self.K], [pp], lambda: nc.tensor.matmul(pp[:, :128], ONES, sq[:, :], start=True, stop=True))
            rn = R_("rn%d" % xi)
            c.dve([pp], [rn], lambda: nc.vector.tensor_scalar(rn[:, :], pp[:, :128], EPS, None, ALU.add))
            XS.append(rn)
        yield
        for xi in range(2):
            rn = XS[3 + xi]
            c.act([rn], [rn], lambda: nc.scalar.sqrt(rn[:, :], rn[:, :]))
            c.dve([rn], [rn], lambda: nc.vector.reciprocal(rn[:, :], rn[:, :]))
            Xt = XS[xi]
            sc_ = (128 ** -0.5) if xi == 0 else 1.0
            c.dve([Xt, rn], [Xt], lambda: nc.vector.scalar_tensor_tensor(
                out=Xt[:, :], in0=Xt[:, :], scalar=sc_, in1=rn[:, :], op0=ALU.mult, op1=ALU.mult))
        yield
        pk, pv = nps(), nps()
        c.pe([ks, self.K], [pk], lambda: nc.tensor.transpose(pk[:, :128], ks[:, :], I))
        c.pe([vs, self.K], [pv], lambda: nc.tensor.transpose(pv[:, :128], vs[:, :], I))
        ktm, Vb = R_("ktm"), R_("Vb")
        c.act([pk], [ktm], lambda: nc.scalar.copy(out=ktm[:, :], in_=pk[:, :128]))
        c.dve([pv, beta], [Vb], lambda: nc.vector.tensor_scalar(Vb[:, :], pv[:, :128], beta[:, r:r + 1], None, ALU.mult))
        gU2, gcB = R_("gU2"), R_("gcB")
        c.pool([gg, self.K], [gU2], lambda: nc.gpsimd.tensor_scalar(gU2[:, :], U2, gcol, None, ALU.mult))
        c.pool([gg, self.K], [gcB], lambda: nc.gpsimd.tensor_scalar(gcB[:, :], ONES, gcol, None, ALU.mult))
        yield
        pD, pDT, pGB, pcol = nps(), nps(), nps(), nps()
        c.pe([gU2, self.K], [pD], lambda: nc.tensor.matmul(pD[:, :128], U1, gU2[:, :], start=True, stop=True))
        c.pe([gU2, self.K], [pDT], lambda: nc.tensor.matmul(pDT[:, :128], gU2[:, :], U1, start=True, stop=True))
        c.pe([gcB, self.K], [pGB], lambda: nc.tensor.matmul(pGB[:, :128], gcB[:, :], U1, start=True, stop=True))
        c.pe([gcB, self.K], [pcol], lambda: nc.tensor.matmul(pcol[:, 0:128], U1, gcB[:, :], start=True, stop=True))
        c.pe([gcB, self.K], [pcol], lambda: nc.tensor.matmul(pcol[:, 128:256], U2, gcB[:, :], start=True, stop=True))
        c.pe([gcB, self.K], [pcol], lambda: nc.tensor.matmul(pcol[:, 256:384], ONES, gcB[:, :], start=True, stop=True))
        ecol = self.ring(tg + "ecol", [128, 4], F32, 3)
        c.act([pcol], [ecol], lambda: nc.scalar.activation(
            out=ecol[:, 0:3], in_=pcol[:, 0:384].rearrange("p (k n) -> p k n", n=128)[:, :, 0], func=AF.Exp))
        expD, expDT, eGB = R_("expD"), R_("expDT"), R_("eGB")
        c.act([pD], [expD], lambda: nc.scalar.activation(out=expD[:, :], in_=pD[:, :128], func=AF.Exp))
        c.act([pDT], [expDT], lambda: nc.scalar.activation(out=expDT[:, :], in_=pDT[:, :128], func=AF.Exp))
        c.act([pGB], [eGB], lambda: nc.scalar.activation(out=eGB[:, :], in_=pGB[:, :128], func=AF.Exp))
        c.pool([expD, self.K], [expD], lambda: nc.gpsimd.tensor_tensor(expD[:, :], expD[:, :], U2, ALU.mult))
        c.pool([expDT, self.K], [expDT], lambda: nc.gpsimd.tensor_tensor(expDT[:, :], expDT[:, :], U1, ALU.mult))
        yield
        pKK, pQK = nps(), nps()
        c.pe([ks], [pKK], lambda: nc.tensor.matmul(pKK[:, :128], ks[:, :], ks[:, :], start=True, stop=True))
        c.pe([ks, qs], [pQK], lambda: nc.tensor.matmul(pQK[:, :128], ks[:, :], qs[:, :], start=True, stop=True))
        A = R_("A")
        c.dve([pKK, nbeta, expD], [A], lambda: nc.vector.scalar_tensor_tensor(
            out=A[:, :], in0=pKK[:, :128], scalar=nbeta[:, r:r + 1], in1=expD[:, :], op0=ALU.mult, op1=ALU.mult))
        aqkT = R_("aqkT", 3)
        c.dve([pQK, expDT], [aqkT], lambda: nc.vector.tensor_tensor(aqkT[:, :], pQK[:, :128], expDT[:, :], ALU.mult))
        qdT = R_("qdT", 3)
        c.pool([qs, eGB], [qdT], lambda: nc.gpsimd.tensor_tensor(qdT[:, :], qs[:, :], eGB[:, :], ALU.mult))
        bcol = self.ring(tg + "bcol", [128, 1], F32, 3)
        c.dve([beta, ecol], [bcol], lambda: nc.vector.tensor_tensor(bcol[:, :], beta[:, r:r + 1], ecol[:, 0:1], ALU.mult))
        kbg, kdec = R_("kbg"), R_("kdec", 3)
        c.dve([ktm, bcol], [kbg], lambda: nc.vector.tensor_scalar(kbg[:, :], ktm[:, :], bcol[:, 0:1], None, ALU.mult))
        c.pool([ktm, ecol], [kdec], lambda: nc.gpsimd.tensor_scalar(kdec[:, :], ktm[:, :], ecol[:, 1:2], None, ALU.mult))
        yield
        pB = nps()
        c.pe([A, self.K], [pB], lambda: nc.tensor.transpose(pB[:, :128], A[:, :], I))
        Bm = R_("B")
        c.act([pB], [Bm], lambda: nc.scalar.copy(out=Bm[:, :], in_=pB[:, :128]))
        Rm = R_("R", 3)
        c.dve([pB, self.K], [Rm], lambda: nc.vector.tensor_tensor(Rm[:, :], pB[:, :128], I, ALU.add))
        yield
        Aj, Bj = A, Bm
        for lvl in range(6):
            pA2 = nps()
            c.pe([Aj, Bj], [pA2], lambda Aj=Aj, Bj=Bj, pA2=pA2: nc.tensor.matmul(pA2[:, :128], Bj[:, :], Aj[:, :], start=True, stop=True))
            if lvl < 5:
                pB2 = nps()
                c.pe([Aj, Bj], [pB2], lambda Aj=Aj, Bj=Bj, pB2=pB2: nc.tensor.matmul(pB2[:, :128], Aj[:, :], Bj[:, :], start=True, stop=True))
                A2, B2 = R_("A2", 3), R_("B2", 3)
            IA = R_("IA")
            c.dve([pA2, self.K], [IA], lambda IA=IA, pA2=pA2: nc.vector.tensor_tensor(IA[:, :], pA2[:, :128], I, ALU.add))
            if lvl < 5:
                c.act([pA2], [A2], lambda A2=A2, pA2=pA2: nc.scalar.copy(out=A2[:, :], in_=pA2[:, :128]))
                c.act([pB2], [B2], lambda B2=B2, pB2=pB2: nc.scalar.copy(out=B2[:, :], in_=pB2[:, :128]))
            pR = nps()
            c.pe([IA, Rm], [pR], lambda IA=IA, Rm=Rm, pR=pR: nc.tensor.matmul(pR[:, :128], IA[:, :], Rm[:, :], start=True, stop=True))
            Rn = R_("R", 3)
            c.dve([pR], [Rn], lambda Rn=Rn, pR=pR: nc.vector.tensor_copy(Rn[:, :], pR[:, :128]))
            Rm = Rn
            if lvl < 5:
                Aj, Bj = A2, B2
            yield
        pu, pw = nps(), nps()
        c.pe([Rm, Vb], [pu], lambda: nc.tensor.matmul(pu[:, :128], Rm[:, :], Vb[:, :], start=True, stop=True))
        c.pe([Rm, kbg], [pw], lambda: nc.tensor.matmul(pw[:, :128], kbg[:, :], Rm[:, :], start=True, stop=True))
        u, wT = R_("u", 3), R_("wT", 3)
        c.act([pu], [u], lambda: nc.scalar.copy(out=u[:, :], in_=pu[:, :128]))
        c.dve([pw], [wT], lambda: nc.vector.tensor_copy(wT[:, :], pw[:, :128]))
        out.update(u=u, wT=wT, qdT=qdT, aqkT=aqkT, kdec=kdec, ecol=ecol, gt=gt)
        yield

    def seq(r, h, p):
        S = HS[h]["S"]
        tg = "dn%d_" % h
        R_ = lambda k, n=2: self.ring(tg + k, [128, 128], F32, n)
        u, wT, qdT, aqkT, kdec, ecol, gt = (p[k] for k in ("u", "wT", "qdT", "aqkT", "kdec", "ecol", "gt"))
        pvn = nps()
        c.pe([wT, S], [pvn], lambda: nc.tensor.matmul(pvn[:, :128], wT[:, :], S[:, :], start=True, stop=True))
        vn = R_("vn")
        c.dve([u, pvn], [vn], lambda: nc.vector.tensor_tensor(vn[:, :], u[:, :], pvn[:, :128], ALU.subtract))
        po = nps()
        c.pe([qdT, S], [po], lambda: nc.tensor.matmul(po[:, :128], qdT[:, :], S[:, :], start=True, stop=False))
        c.pe([aqkT, vn], [po], lambda: nc.tensor.matmul(po[:, :128], aqkT[:, :], vn[:, :], start=False, stop=True))
        pS = nps()
        c.pe([kdec, vn], [pS], lambda: nc.tensor.matmul(pS[:, :128], kdec[:, :], vn[:, :], start=True, stop=True))
        c.dve([pS, ecol, S], [S], lambda: nc.vector.scalar_tensor_tensor(
            out=S[:, :], in0=S[:, :], scalar=ecol[:, 2:3], in1=pS[:, :128], op0=ALU.mult, op1=ALU.add))
        osb = R_("osb")
        c.act([po], [osb], lambda: nc.scalar.copy(out=osb[:, :], in_=po[:, :128]))
        yield
        yield from self.rms_gate_out_g(osb, gbc, gt, y, r * 128, col_off + h * 128, tg)

    P = {}

    def run_prep(r):
        for h in range(nheads):
            P[(r, h)] = {}
        yield from lockstep_g([prep(r, h, P[(r, h)]) for h in range(nheads)])
    yield from run_prep(0)
    for r in range(NR):
        if r + 1 < NR:
            yield from run_prep(r + 1)
        yield from lockstep_g([seq(r, h, P.pop((r, h))) for h in range(nheads)])


def _gdn(self, *a, **kw):
    for _ in _gdn_g(self, *a, **kw):
        pass


MixProg.gdn_g = _gdn_g
MixProg.gdn = _gdn
MixProg.gdn = _gdn


def build_gdn(nheads=2):
    nc = new_nc()
    es = ExitStack()
    with es:
        c = Ctx(nc, es)
        di = lambda n, sh: c.dram(n, sh, F32, "ExternalInput")
        cst = di("cst", [5, 128, 128])
        qkvT = di("qkvT", [3, nheads, 128, T]); cw = di("cw", [3, nheads, 128, 4]); gate = di("gate", [T, nheads * 128])
        ba = di("ba", [nheads, 128, 64]); sc = di("sc", [nheads, 2]); ong = di("ong", [128])
        y = c.dram("y", [T, nheads * 128], F32, "ExternalOutput")
        mp = MixProg(c, cst.h)
        mp.gdn(qkvT.h, cw.h, gate.h, ba.h, sc.h, ong.h, y, nheads)
        c.finish()
    return nc, c


def _moba_g(self, qT, kT, v, esel, pbias, triw, y, nheads=2, col_off=0, bigs=None):
    c, nc = self.c, self.nc
    scale = 128 ** -0.5
    NEG = -1e30
    Eb = c.sb([32, 32 * 128], BF16, "mo_esel")
    c.dma("pool", [], [Eb], lambda e: e.dma_start(out=Eb[:, :], in_=esel))
    pbt = c.sb([128, 256], F32, "mo_pb")
    c.dma("sp", [], [pbt], lambda e: e.dma_start(out=pbt[:, :], in_=pbias.partition_broadcast(128)))
    trb = c.sb([128, 4, 512], BF16, "mo_tri")
    c.dma("pool", [], [trb], lambda e: e.dma_start(out=trb[:, :, :], in_=triw.rearrange("j p n -> p j n")))
    zeros = c.sb([128, 32], F32, "mo_z")
    c.dve([], [zeros], lambda: nc.vector.memset(zeros[:, :], 0.0))
    zb = c.sb([128, 512], BF16, "mo_zb")
    c.dve([], [zb], lambda: nc.vector.memset(zb[:, :], 0.0))
    stg = [c.sb([128, 1024], F32, "mo_stg") for _ in range(2)]
    gall = c.sb([128, 32, 16], F32, "mo_gall")
    qb = c.sb([128, T], BF16, "mo_qb")
    kb = c.sb([128, T], BF16, "mo_kb")
    vaug = c.sb([128, 32, 132], BF16, "mo_v")
    kmT = c.sb([128, 16], F32, "mo_km")
    pobank = [self.pb[0], self.pb[0], self.pb[1], self.pb[1]]
    pooff = [0, 256, 0, 256]
    self.prings["mo"] = [[2, 3], 0]
    nps = lambda: self.nps("mo")
    for h in range(nheads):
        c.dma("pool", [], [vaug], lambda e: e.dma_start(
            out=vaug[:, :, 0:128], in_=v[:, h * 128:(h + 1) * 128].rearrange("(r p) d -> p r d", p=128)))
        c.dve([], [vaug], lambda: nc.vector.memset(vaug[:, :, 128:129], 1.0))
        for ch in range(4):
            sg = stg[ch % 2]
            cs = slice(ch * 1024, (ch + 1) * 1024)
            c.dma("sp", [], [sg], lambda e: e.dma_start(out=sg[:, :], in_=kT[h][:, cs]))
            c.dve([sg], [kb], lambda: nc.vector.tensor_copy(kb[:, cs], sg[:, :]))
            c.dve([sg], [kmT], lambda: nc.vector.tensor_reduce(
                out=kmT[:, ch * 4:(ch + 1) * 4], in_=sg[:, :].rearrange("p (n k) -> p n k", k=256), axis=AX.X, op=ALU.add))
            yield
        c.dve([kmT], [kmT], lambda: nc.vector.tensor_scalar(kmT[:, :], kmT[:, :], 1.0 / 256, None, ALU.mult))
        for ch in range(4):
            sg = stg[ch % 2]
            cs = slice(ch * 1024, (ch + 1) * 1024)
            c.dma("sp", [], [sg], lambda e: e.dma_start(out=sg[:, :], in_=qT[h][:, cs]))
            c.act([sg], [qb], lambda: nc.scalar.copy(out=qb[:, cs], in_=sg[:, :]))
            for j8 in range(8):
                qt_ = ch * 8 + j8
                pg = nps()
                c.pe([sg, kmT], [pg], lambda pg=pg: nc.tensor.matmul(
                    pg[:, :16], sg[:, j8 * 128:(j8 + 1) * 128], kmT[:, :], start=True, stop=True))
                c.dve([pg], [gall], lambda pg=pg: nc.vector.tensor_copy(gall[:, qt_, :], pg[:, :16]))
            yield
        for qc in range(8):
            biasT = self.ring("mo_biasT", [32, 512], BF16, 2)
            for j in range(4):
                qt = 4 * qc + j
                blk = qt // 2
                qcols = slice(qt * 128, (qt + 1) * 128)
                b32 = self.ring("mo_b32", [128, 32], F32, 2)
                mx = self.ring("mo_mx", [128, 16], F32, 2)
                nkc = qt // 4 + 1
                for kc in range(nkc):
                    pm = nps()
                    c.pe([qb, kb], [pm], lambda pm=pm, kc=kc: nc.tensor.matmul(
                        pm[:, :], qb[:, qcols], kb[:, kc * 512:(kc + 1) * 512], start=True, stop=True))
                    c.dve([pm], [mx], lambda pm=pm, kc=kc: nc.vector.reduce_max(mx[:, kc:kc + 1], pm[:, :], axis=AX.X))
                c.dve([mx], [mx], lambda: nc.vector.reduce_max(mx[:, 8:9], mx[:, 0:nkc], axis=AX.X))
                c.dve([], [b32], lambda: nc.vector.memset(b32[:, :], NEG))
                c.dve([mx, zeros], [b32], lambda: nc.vector.tensor_scalar(
                    b32[:, 0:qt + 1], zeros[:, 0:qt + 1], mx[:, 8:9], None, ALU.subtract))
                if blk > 0:
                    selb = self.ring("mo_selb", [128, 16], F32, 2)
                    if blk > 3:
                        gm = self.ring("mo_gm", [128, 16], F32, 2)
                        c.dve([gall, pbt], [gm], lambda: nc.vector.tensor_tensor(
                            gm[:, :], gall[:, qt, :], pbt[:, blk * 16:(blk + 1) * 16], ALU.add))
                        t8 = self.ring("mo_t8", [128, 8], F32, 2)
                        c.dve([gm], [t8], lambda: nc.vector.max(out=t8[:, :], in_=gm[:, :]))
                        c.dve([gm, t8], [selb], lambda: nc.vector.tensor_scalar(
                            selb[:, :], gm[:, :], t8[:, 2:3], None, ALU.is_ge))
                        c.dve([selb], [selb], lambda: nc.vector.tensor_scalar(
                            selb[:, :], selb[:, :], -NEG, NEG, ALU.mult, ALU.add))
                    else:
                        c.dve([], [selb], lambda: nc.vector.memset(selb[:, :], 0.0))
                    b32v = b32[:, 0:2 * blk].rearrange("p (n two) -> p n two", two=2)
                    for e2 in range(2):
                        c.dve([b32, selb], [b32], lambda e2=e2: nc.vector.tensor_tensor(
                            b32v[:, :, e2], b32v[:, :, e2], selb[:, 0:blk], ALU.add))
                pT = nps()
                c.pe([b32, self.K], [pT], lambda pT=pT: nc.tensor.transpose(pT[0:32, 0:128], b32[:, :], self.identf))
                c.act([pT], [biasT], lambda pT=pT: nc.scalar.copy(out=biasT[:, j * 128:(j + 1) * 128], in_=pT[0:32, 0:128]))
                yield
            qcs = slice(qc * 512, (qc + 1) * 512)
            ns = 4 * qc + 4
            for bk in (self.pb[0], self.pb[1]):
                c.pe([zb], [bk], lambda bk=bk: nc.tensor.matmul(bk[:, :], zb[:, 0:128], zb[:, :], start=True, stop=True))
            for s in range(ns):
                ps = nps()
                jd = s - 4 * qc
                c.pe([kb, qb], [ps], lambda ps=ps: nc.tensor.matmul(
                    ps[:, :], kb[:, s * 128:(s + 1) * 128], qb[:, qcs], start=True, stop=False))
                c.pe([Eb, biasT], [ps], lambda ps=ps: nc.tensor.matmul(
                    ps[:, :], Eb[:, s * 128:(s + 1) * 128], biasT[:, :], start=False, stop=(jd < 0)))
                if jd >= 0:
                    c.pe([self.Kb, trb], [ps], lambda ps=ps: nc.tensor.matmul(
                        ps[:, :], self.identb, trb[:, jd, :], start=False, stop=True))
                PT = self.ring("mo_PT", [128, 512], BF16, 3)
                c.act([ps], [PT], lambda ps=ps: nc.scalar.activation(out=PT[:, :], in_=ps[:, :], func=AF.Exp, scale=scale))
                for j in range(4):
                    qt = 4 * qc + j
                    if s <= qt:
                        c.pe([PT, vaug], [pobank[j]], lambda j=j: nc.tensor.matmul(
                            pobank[j][:, pooff[j]:pooff[j] + 129], PT[:, j * 128:(j + 1) * 128], vaug[:, s, 0:129],
                            start=False, stop=(s == qt), skip_group_check=True))
                yield
            for j in range(4):
                qt = 4 * qc + j
                rc = self.ring("mo_rc", [128, 1], F32, 3)
                ot = self.ring("mo_ot", [128, 128], F32, 3)
                c.dve([pobank[j]], [rc], lambda j=j: nc.vector.reciprocal(rc[:, :], pobank[j][:, pooff[j] + 128:pooff[j] + 129]))
                c.dve([pobank[j], rc], [ot], lambda j=j: nc.vector.tensor_scalar(
                    ot[:, :], pobank[j][:, pooff[j]:pooff[j] + 128], rc[:, 0:1], None, ALU.mult))
                c.dma("sp", [ot], [y], lambda e: e.dma_start(
                    out=y[qt * 128:(qt + 1) * 128, col_off + h * 128:col_off + (h + 1) * 128], in_=ot[:, :]))


def _moba(self, *a, **kw):
    for _ in _moba_g(self, *a, **kw):
        pass


MixProg.moba = _moba
MixProg.moba_g = _moba_g


def moba_consts():
    esel = np.zeros((32, 32, 128), np.float32)
    for s in range(32):
        esel[s, s, :] = 1.0
    pb = np.zeros((16, 16), np.float32)
    for blk in range(16):
        pb[blk, blk:] = -1e30
    i = np.arange(128)
    tri = np.where(i[:, None] <= i[None, :], 0.0, -30000.0).astype(np.float32)
    triw = np.zeros((4, 128, 512), np.float32)
    for j in range(4):
        triw[j, :, j * 128:(j + 1) * 128] = tri
    return esel.reshape(32, 32 * 128), pb.reshape(256), triw


def build_moba(nheads=2):
    nc = new_nc()
    es = ExitStack()
    with es:
        c = Ctx(nc, es)
        di = lambda n, sh: c.dram(n, sh, F32, "ExternalInput")
        cst = di("cst", [5, 128, 128])
        qT = di("mqT", [nheads, 128, T]); kT = di("mkT", [nheads, 128, T]); v = di("mv", [T, nheads * 128])
        esel = di("esel", [32, 32 * 128]); pbias = di("pbias", [256]); triw = di("triw", [4, 128, 512])
        y = c.dram("y", [T, nheads * 128], F32, "ExternalOutput")
        mp = MixProg(c, cst.h)
        mp.moba(qT.h, kT.h, v.h, esel.h, pbias.h, triw.h, y, nheads)
        c.finish()
    return nc, c


def build_mix_even():
    nc = new_nc()
    es = ExitStack()
    with es:
        c = Ctx(nc, es)
        di = lambda n, sh: c.dram(n, sh, F32, "ExternalInput")
        cst = di("cst", [5, 128, 128])
        qT = di("mqT", [2, 128, T]); kT = di("mkT", [2, 128, T]); v = di("mv", [T, 256])
        esel = di("esel", [32, 32 * 128]); pbias = di("pbias", [256]); triw = di("triw", [4, 128, 512])
        qkvT = di("qkvT", [3, 2, 128, T]); cw = di("cw", [3, 2, 128, 4]); gate = di("gate", [T, 256])
        ba = di("ba", [2, 128, 64]); sc = di("sc", [2, 2]); ong = di("ong", [128])
        y = c.dram("y", [T, 512], F32, "ExternalOutput")
        mp = MixProg(c, cst.h)
        bigs = None
        mp.prings["dn"] = [[4, 5, 6, 7], 0]
        lockstep([mp.moba_g(qT.h, kT.h, v.h, esel.h, pbias.h, triw.h, y, 2, 0, None),
                  mp.gdn_g(qkvT.h, cw.h, gate.h, ba.h, sc.h, ong.h, y, 2, 256, bigs, "dn")])
        c.finish()
    return nc, c


_PROGS = {}


def _prog(key, fn):
    if key not in _PROGS:
        _PROGS[key] = fn()[0]
    return _PROGS[key]


def _run(nc, maps):
    maps = [{k: np.ascontiguousarray(v, dtype=np.float32) for k, v in m.items()} for m in maps]
    return run_bass_kernel_spmd(nc, maps, core_ids=list(range(NCORE))).results


def kernel(x, mem, norm_g, mem_norm_g, final_norm_g, ffn_w_in, ffn_w_out, ab_w_in, ab_conv_w, ab_a_log,
           ab_dt_bias, ab_o_norm_g, ab_w_out, c_w_in, c_lb_logits, c_o_norm_g, c_w_out, x_w_q, x_w_kv, x_w_o):
    f = lambda a: np.asarray(a, dtype=np.float32)
    x, mem, norm_g = f(x), f(mem), f(norm_g)
    ident = np.eye(128, dtype=np.float32)
    cst = mix_consts()
    esel, pbias, triw = moba_consts()
    xs = x.reshape(NCORE, TPC, D)
    depth = 4

    def pre_inputs(l):
        kind = "even" if l % 2 == 0 else "odd"
        w_in = f(ab_w_in[l // 2]) if kind == "even" else f(c_w_in[l // 2])
        return kind, dict(g0=norm_g[l, 0], g1=norm_g[l, 1], f0_in=f(ffn_w_in[l, 0]), f0_out=f(ffn_w_out[l, 0]), w_in=w_in)

    def mix_inputs(l):
        w_mo = f(ab_w_out[l // 2]) if l % 2 == 0 else f(c_w_out[l // 2])
        return dict(w_mo=w_mo, mem_g=f(mem_norm_g), w_q=f(x_w_q[l]), w_kv=f(x_w_kv[l]), w_o=f(x_w_o[l]),
                    g2=norm_g[l, 2], g3=norm_g[l, 3], f1_in=f(ffn_w_in[l, 1]), f1_out=f(ffn_w_out[l, 1]))

    kind, pin = pre_inputs(0)
    res = _run(_prog(("tok", False, kind, False), lambda: build_tok(False, kind, False)),
               [dict(pin, x=xs[i], ident=ident) for i in range(NCORE)])
    out = None
    for l in range(depth):
        xcur = [r["xo"] for r in res]
        pf = [np.concatenate([res[b * 4 + q]["pf"] for q in range(4)], axis=1) for b in range(B)]
        pt = [np.concatenate([res[b * 4 + q]["pt"] for q in range(4)], axis=0) for b in range(B)]
        yfull = np.zeros((B, T, D), np.float32)
        if l % 2 == 0:
            e = l // 2
            convw = f(ab_conv_w[e])
            maps = []
            for i in range(NCORE):
                b, hp = i // 4, i % 4
                hds = [2 * hp, 2 * hp + 1]
                P = pf[b]
                m = dict(cst=cst, esel=esel, pbias=pbias, triw=triw)
                m["mqT"] = np.stack([P[h * 128:(h + 1) * 128] for h in hds])
                m["mkT"] = np.stack([P[1024 + h * 128:1024 + (h + 1) * 128] for h in hds])
                m["mv"] = pt[b][:, 2 * hp * 128:(2 * hp + 2) * 128]
                m["qkvT"] = np.stack([np.stack([P[2048 + xi * 1024 + h * 128:2048 + xi * 1024 + (h + 1) * 128] for h in hds])
                                      for xi in range(3)])
                m["cw"] = np.stack([np.stack([convw[:, xi * 1024 + h * 128:xi * 1024 + (h + 1) * 128].T for h in hds])
                                    for xi in range(3)])
                m["gate"] = pt[b][:, 1024 + 2 * hp * 128:1024 + (2 * hp + 2) * 128]
                ba = np.zeros((2, 128, 64), np.float32)
                for hh, h in enumerate(hds):
                    ba[hh, :, :32] = pt[b][:, 2048 + h].reshape(32, 128).T
                    ba[hh, :, 32:] = pt[b][:, 2056 + h].reshape(32, 128).T
                m["ba"] = ba
                m["sc"] = np.stack([f(ab_a_log[e])[hds], f(ab_dt_bias[e])[hds]], axis=1)
                m["ong"] = f(ab_o_norm_g[e])
                maps.append(m)
            rb = _run(_prog("mix_even", build_mix_even), maps)
            for i in range(NCORE):
                b, hp = i // 4, i % 4
                yfull[b][:, 2 * hp * 128:(2 * hp + 2) * 128] = rb[i]["y"][:, :256]
                yfull[b][:, 1024 + 2 * hp * 128:1024 + (2 * hp + 2) * 128] = rb[i]["y"][:, 256:]
        else:
            o = l // 2
            lgt = f(c_lb_logits)
            coef = np.zeros(4, np.float32)
            coef[1:l + 1] = 1.0
            maps = []
            for i in range(NCORE):
                b, hp = i // 4, i % 4
                cs = slice(hp * 512, (hp + 1) * 512)
                m = dict(cst=cst, coef=coef, ong=f(c_o_norm_g[o]))
                m["qT"] = pf[b][cs].reshape(4, 128, T)
                m["f"] = pt[b][:, cs]
                m["iv"] = pt[b][:, 2048 + hp * 512:2048 + (hp + 1) * 512]
                m["gg"] = pt[b][:, 4096 + hp * 512:4096 + (hp + 1) * 512]
                m["lg"] = lgt[:, cs].reshape(4, 4, 128).transpose(1, 2, 0)
                maps.append(m)
            rb = _run(_prog("mix_odd", lambda: build_hgrn2(4)), maps)
            for i in range(NCORE):
                b, hp = i // 4, i % 4
                yfull[b][:, hp * 512:(hp + 1) * 512] = rb[i]["y"]
        ys = yfull.reshape(NCORE, TPC, D)
        mems = [mem[i // 4] for i in range(NCORE)]
        min_ = mix_inputs(l)
        if l + 1 < depth:
            kind, pin = pre_inputs(l + 1)
            res = _run(_prog(("tok", True, kind, False), lambda: build_tok(True, kind, False)),
                       [dict(min_, **pin, x=xcur[i], y=ys[i], mem=mems[i], ident=ident) for i in range(NCORE)])
        else:
            res = _run(_prog(("tok", True, None, True), lambda: build_tok(True, None, True)),
                       [dict(min_, gf=f(final_norm_g), x=xcur[i], y=ys[i], mem=mems[i], ident=ident) for i in range(NCORE)])
            out = np.stack([r["out"] for r in res]).reshape(B, T, D)
    return out.astype(np.float32)
```

```python
import numpy as np
from contextlib import ExitStack
import concourse.bass as bass
import concourse.mybir as mybir
from concourse.bass_utils import run_bass_kernel_spmd

F32 = mybir.dt.float32
BF16 = mybir.dt.bfloat16
AF = mybir.ActivationFunctionType
ALU = mybir.AluOpType
AX = mybir.AxisListType

D = 2048
DFF = 5504
T = 4096
B = 2
NCORE = 8
TPC = 1024
NTT = TPC // 128
EPS = 1e-6
AB_IN = 7184


class Tl:
    def __init__(self, h, name, nsub=1, psum=False):
        self.h = h
        self.name = name
        self.nsub = nsub
        self.psum = psum

    def __getitem__(self, k):
        return self.h[k]


class Ctx:
    SAME_ENG_SYNC = True

    def __init__(self, nc, es):
        self.nc = nc
        self.es = es
        self.eng = {"pe": nc.tensor, "act": nc.scalar, "dve": nc.vector, "pool": nc.gpsimd, "sp": nc.sync}
        self.sem = {}
        self.cnt = {}
        for e in ("pe", "act", "dve", "pool"):
            self.sem[e] = es.enter_context(nc.semaphore("s_" + e))
            self.cnt[e] = 0
        self.waited = {e: {} for e in self.eng}
        self.dq = {}
        for q in ("sp", "pool", "act"):
            n = 12 if q != "act" else 4
            self.dq[q] = {"sems": [es.enter_context(nc.semaphore("d_%s%d" % (q, i))) for i in range(n)],
                          "val": [0] * n, "i": 0}
        self.lastw = {}
        self.readers = {}
        self.psum_names = set()
        self.uid = 0
        self.ninstr = 0

    def sb(self, shape, dt, name=None, nsub=1):
        self.uid += 1
        name = (name or "t") + "_%d" % self.uid
        h = self.es.enter_context(self.nc.sbuf_tensor(name, list(shape), dt))
        return Tl(h, name, nsub)

    def ps(self, shape, dt, name=None):
        self.uid += 1
        name = (name or "p") + "_%d" % self.uid
        h = self.es.enter_context(self.nc.psum_tensor(name, list(shape), dt))
        self.psum_names.add(name)
        return Tl(h, name, 1, True)

    def dram(self, name, shape, dt, kind):
        h = self.nc.dram_tensor(name, list(shape), dt, kind=kind)
        return Tl(h.ap(), name, 1)

    def _keys(self, specs):
        ks = []
        for s in specs:
            if s is None:
                continue
            if isinstance(s, tuple):
                t, i = s
                ks.append((t.name, i))
            else:
                for i in range(s.nsub):
                    ks.append((s.name, i))
        return ks

    def _wait(self, e, tok):
        sem, val, te, sid = tok
        if te == e and (e in ("pe", "sp") or not self.SAME_ENG_SYNC):
            return
        w = self.waited[e]
        if w.get(sid, 0) >= val:
            return
        self.eng[e].wait_ge(sem, val)
        w[sid] = val

    def _deps(self, e, r, w):
        rk = self._keys(r)
        wk = self._keys(w)
        toks = {}
        def add(tok):
            if tok is None:
                return
            sid = tok[3]
            if sid not in toks or toks[sid][1] < tok[1]:
                toks[sid] = tok
        for k in rk:
            add(self.lastw.get(k))
            if k[0] in self.psum_names and e != "pe":
                for tok in self.readers.get(k, {}).values():
                    if tok[2] != "pe":
                        add(tok)
        for k in wk:
            add(self.lastw.get(k))
            for tok in self.readers.get(k, {}).values():
                add(tok)
        for tok in toks.values():
            self._wait(e, tok)
        return rk, wk

    def _commit(self, tok, rk, wk):
        for k in wk:
            self.lastw[k] = tok
            self.readers[k] = {}
        for k in rk:
            d = self.readers.setdefault(k, {})
            sid = tok[3]
            if sid not in d or d[sid][1] < tok[1]:
                d[sid] = tok

    def op(self, e, r, w, fn):
        rk, wk = self._deps(e, r, w)
        ins = fn()
        self.cnt[e] += 1
        ins.then_inc(self.sem[e], 1)
        self.ninstr += 1
        self._commit((self.sem[e], self.cnt[e], e, "c_" + e), rk, wk)
        return ins

    def pe(self, r, w, fn):
        return self.op("pe", r, w, fn)

    def act(self, r, w, fn):
        return self.op("act", r, w, fn)

    def dve(self, r, w, fn):
        return self.op("dve", r, w, fn)

    def pool(self, r, w, fn):
        return self.op("pool", r, w, fn)

    def dma(self, q, r, w, fn):
        rk, wk = self._deps(q, r, w)
        dq = self.dq[q]
        i = dq["i"]
        dq["i"] = (i + 1) % len(dq["sems"])
        sem = dq["sems"][i]
        sid = "d_%s%d" % (q, i)
        if dq["val"][i] > 0 and self.waited[q].get(sid, 0) < dq["val"][i]:
            self.eng[q].wait_ge(sem, dq["val"][i])
            self.waited[q][sid] = dq["val"][i]
        ins = fn(self.eng[q])
        ins.then_inc(sem, 16)
        dq["val"][i] += 16
        self.ninstr += 1
        self._commit((sem, dq["val"][i], "dma", sid), rk, wk)
        return ins

    def finish(self):
        for q, dq in self.dq.items():
            for i, sem in enumerate(dq["sems"]):
                if dq["val"][i] > 0:
                    self.nc.sync.wait_ge(sem, dq["val"][i])
        for e in ("pe", "act", "dve", "pool"):
            if self.cnt[e] > 0:
                self.nc.sync.wait_ge(self.sem[e], self.cnt[e])


class TokProg:
    def __init__(self, c):
        self.c = c
        nc = c.nc
        self.nc = nc
        self.ident_f = c.sb([128, 128], F32, "identf")
        self.ident_b = c.sb([128, 128], BF16, "identb")
        self.psb = [c.ps([128, 512], F32, "psb") for _ in range(6)]
        self.pst = [c.ps([128, 1024], BF16, "pst") for _ in range(2)]
        self.psi = 0
        self.pti = 0
        self.junk = c.sb([128, D], BF16, "junk")
        self.hn = [c.sb([128, D], BF16, "hn") for _ in range(2)]
        self.hni = 0
        self.gbc = c.sb([128, D], F32, "gbc")
        self.st = c.sb([128, 8], F32, "stat")
        self.junk2 = c.sb([128, D], F32, "junk2")
        self.stg = [c.sb([128, 512], F32, "stg") for _ in range(2)]
        self.stgi = 0
        self.wri = 0
        self.evi = 0

    def load_ident(self, ident_dram):
        c = self.c
        c.dma("sp", [], [self.ident_f], lambda e: e.dma_start(out=self.ident_f[:], in_=ident_dram[:, :]))
        c.dve([self.ident_f], [self.ident_b], lambda: self.nc.vector.tensor_copy(self.ident_b[:], self.ident_f[:]))

    def nps(self):
        p = self.psb[self.psi % len(self.psb)]
        self.psi += 1
        return p

    def npt(self):
        p = self.pst[self.pti % len(self.pst)]
        self.pti += 1
        return p

    def load_gain(self, g_ap):
        c = self.c
        c.dma("sp", [], [self.gbc], lambda e: e.dma_start(out=self.gbc[:], in_=g_ap.partition_broadcast(128)))

    def rmsnorm_rows(self, xt, out_ap_tile, out_ap, width=D, gbc=None):
        c, nc = self.c, self.nc
        gbc = gbc or self.gbc
        st = self.st
        c.act([xt], [self.junk, st], lambda: nc.scalar.activation(
            out=self.junk[:, :width], in_=xt[:, :width], func=AF.Square, accum_out=st[:, 0:1]))
        c.dve([st], [st], lambda: nc.vector.tensor_scalar(
            st[:, 1:2], st[:, 0:1], 1.0 / width, EPS, ALU.mult, ALU.add))
        c.act([st], [st], lambda: nc.scalar.sqrt(st[:, 3:4], st[:, 1:2]))
        c.dve([st], [st], lambda: nc.vector.reciprocal(st[:, 2:3], st[:, 3:4]))
        c.dve([xt, st, gbc], [out_ap_tile], lambda: nc.vector.scalar_tensor_tensor(
            out=out_ap, in0=xt[:, :width], scalar=st[:, 2:3], in1=gbc[:, :width], op0=ALU.mult, op1=ALU.mult))

    def norm_T(self, xt, hT, t):
        c, nc = self.c, self.nc
        hn = self.hn[self.hni % 2]
        self.hni += 1
        self.rmsnorm_rows(xt, hn, hn[:, :])
        for half in range(2):
            pt = self.npt()
            for kk in range(8):
                k = half * 8 + kk
                c.pe([hn, self.ident_b], [pt], lambda k=k, kk=kk: nc.tensor.transpose(
                    pt[:, kk * 128:(kk + 1) * 128], hn[:, k * 128:(k + 1) * 128], self.ident_b[:]))
            src = pt[:, :].rearrange("p (k n) -> p k n", k=8)
            dst = hT[:, half * 8:(half + 1) * 8, t * 128:(t + 1) * 128]
            if half == 0:
                c.act([pt], [(hT, t)], lambda: nc.scalar.copy(out=dst, in_=src))
            else:
                c.dve([pt], [(hT, t)], lambda: nc.vector.tensor_copy(dst, src))

    def ffn_alloc(self):
        c = self.c
        self.wa = [c.sb([128, 16, 256], BF16, "wa") for _ in range(2)]
        self.wb = [c.sb([128, 16, 256], BF16, "wb") for _ in range(2)]
        self.wo = c.sb([128, 4, D], BF16, "wo")
        self.actT = c.sb([128, 4, TPC], BF16, "actT")
        self.sA = [c.sb([128, 512], F32, "sA") for _ in range(2)]
        self.sAi = 0

    def ffn(self, xs, hT, w_in, w_out):
        c, nc = self.c, self.nc
        win = w_in.rearrange("(k p) n -> p k n", p=128)
        wout = w_out.rearrange("(j p) n -> p j n", p=128)
        NCH = DFF // 128
        hus = [(j, min(2, NCH - j)) for j in range(0, NCH, 2)]

        def load_hu(i):
            j0, n = hus[i]
            wa, wb = self.wa[i % 2], self.wb[i % 2]
            c.dma("pool", [], [wa], lambda e: e.dma_start(out=wa[:, :, :n * 128], in_=win[:, :, j0 * 128:(j0 + n) * 128]))
            c.dma("pool", [], [wb], lambda e: e.dma_start(out=wb[:, :, :n * 128], in_=win[:, :, DFF + j0 * 128:DFF + (j0 + n) * 128]))

        def load_wo(u):
            j0 = u * 4
            n = min(4, NCH - j0)
            c.dma("pool", [], [self.wo], lambda e: e.dma_start(out=self.wo[:, :n, :], in_=wout[:, j0:j0 + n, :]))

        load_hu(0)
        for i, (j0, n) in enumerate(hus):
            u = i // 2
            if i + 1 < len(hus):
                load_hu(i + 1)
            if i % 2 == 0:
                load_wo(u)
            wa, wb = self.wa[i % 2], self.wb[i % 2]
            for jj in range(n):
                j4 = (i % 2) * 2 + jj
                for th in range(2):
                    pA, pB = self.nps(), self.nps()
                    for (pp, ww) in ((pA, wa), (pB, wb)):
                        for k in range(16):
                            c.pe([ww, hT], [pp], lambda pp=pp, ww=ww, k=k: nc.tensor.matmul(
                                pp[:, :], ww[:, k, jj * 128:(jj + 1) * 128], hT[:, k, th * 512:(th + 1) * 512],
                                start=(k == 0), stop=(k == 15)))
                    sA = self.sA[self.sAi % 2]
                    self.sAi += 1
                    c.act([pA], [sA], lambda: nc.scalar.activation(out=sA[:, :], in_=pA[:, :], func=AF.Silu))
                    c.dve([sA, pB], [self.actT], lambda: nc.vector.tensor_tensor(
                        self.actT[:, j4, th * 512:(th + 1) * 512], sA[:, :], pB[:, :], ALU.mult))
            if i % 2 == 1 or i == len(hus) - 1:
                nj = min(4, NCH - u * 4)
                for cb in range(4):
                    for t in range(NTT):
                        pp = self.nps()
                        for j4 in range(nj):
                            c.pe([self.actT, self.wo], [pp], lambda j4=j4, pp=pp: nc.tensor.matmul(
                                pp[:, :], self.actT[:, j4, t * 128:(t + 1) * 128], self.wo[:, j4, cb * 512:(cb + 1) * 512],
                                start=(j4 == 0), stop=(j4 == nj - 1)))
                        xs_t = xs[t]
                        c.dve([pp, xs_t], [xs_t], lambda pp=pp, xs_t=xs_t: nc.vector.scalar_tensor_tensor(
                            out=xs_t[:, cb * 512:(cb + 1) * 512], in0=pp[:, :], scalar=0.5,
                            in1=xs_t[:, cb * 512:(cb + 1) * 512], op0=ALU.mult, op1=ALU.add))


    def wring(self):
        r = [self.wa[0], self.wb[0], self.wa[1], self.wb[1]]
        w = r[self.wri % 4]
        self.wri += 1
        return w

    def stage(self):
        s = self.stg[self.stgi % len(self.stg)]
        self.stgi += 1
        return s

    def evac(self, ps_ap, ps_t, dst_ap, dst_t):
        c, nc = self.c, self.nc
        self.evi += 1
        if self.evi % 2 == 0:
            c.act([ps_t], [dst_t], lambda: nc.scalar.copy(out=dst_ap, in_=ps_ap))
        else:
            c.dve([ps_t], [dst_t], lambda: nc.vector.tensor_copy(dst_ap, ps_ap))

    def plain_T(self, yt, hT, t):
        c, nc = self.c, self.nc
        hn = self.hn[self.hni % 2]
        self.hni += 1
        c.dve([yt], [hn], lambda: nc.vector.tensor_copy(hn[:, :], yt[:, :]))
        self._T16(hn, hT, t)

    def _T16(self, hn, hT, t, nk=16):
        c, nc = self.c, self.nc
        for half in range((nk + 7) // 8):
            pt = self.npt()
            n8 = min(8, nk - half * 8)
            for kk in range(n8):
                k = half * 8 + kk
                c.pe([hn, self.ident_b], [pt], lambda k=k, kk=kk: nc.tensor.transpose(
                    pt[:, kk * 128:(kk + 1) * 128], hn[:, k * 128:(k + 1) * 128], self.ident_b[:]))
            src = pt[:, :n8 * 128].rearrange("p (k n) -> p k n", k=n8)
            dst = hT[:, half * 8:half * 8 + n8, t * 128:(t + 1) * 128]
            self.evac(src, pt, dst, (hT, t))

    def inproj(self, hT, W, segs, pf, pt):
        c, nc = self.c, self.nc
        Wv = W.rearrange("(k p) n -> p k n", p=128)
        blocks = []
        for (c0, c1, mode, off) in segs:
            cc = c0
            while cc < c1:
                n = min(256, c1 - cc)
                blocks.append((cc, n, mode, off + cc - c0))
                cc += n

        def load(bi):
            cc, n, mode, off = blocks[bi]
            w = self.wring()
            c.dma("pool", [], [w], lambda e: e.dma_start(out=w[:, :, :n], in_=Wv[:, :, cc:cc + n]))
            return w
        wnext = load(0)
        for bi, (cc, n, mode, off) in enumerate(blocks):
            w = wnext
            if bi + 1 < len(blocks):
                wnext = load(bi + 1)
            if mode == "fm":
                for j in range(n // 128):
                    for th in range(2):
                        pp = self.nps()
                        for k in range(16):
                            c.pe([w, hT], [pp], lambda k=k, pp=pp: nc.tensor.matmul(
                                pp[:, :], w[:, k, j * 128:(j + 1) * 128], hT[:, k, th * 512:(th + 1) * 512],
                                start=(k == 0), stop=(k == 15)))
                        sg = self.stage()
                        self.evac(pp[:, :], pp, sg[:, :], sg)
                        c.dma("sp", [sg], [pf], lambda e, sg=sg: e.dma_start(
                            out=pf[off + j * 128:off + (j + 1) * 128, th * 512:(th + 1) * 512], in_=sg[:, :]))
            else:
                for t in range(NTT):
                    pp = self.nps()
                    for k in range(16):
                        c.pe([w, hT], [pp], lambda k=k, pp=pp: nc.tensor.matmul(
                            pp[:, :n], hT[:, k, t * 128:(t + 1) * 128], w[:, k, :n],
                            start=(k == 0), stop=(k == 15)))
                    sg = self.stage()
                    self.evac(pp[:, :n], pp, sg[:, :n], sg)
                    c.dma("sp", [sg], [pt], lambda e, sg=sg: e.dma_start(
                        out=pt[t * 128:(t + 1) * 128, off:off + n], in_=sg[:, :n]))

    def linear_res(self, xs, hT, W, N=D):
        c, nc = self.c, self.nc
        Wv = W.rearrange("(k p) n -> p k n", p=128)
        nb = N // 256

        def load(bi):
            w = self.wring()
            c.dma("pool", [], [w], lambda e: e.dma_start(out=w[:, :, :], in_=Wv[:, :, bi * 256:(bi + 1) * 256]))
            return w
        wnext = load(0)
        for bi in range(nb):
            w = wnext
            if bi + 1 < nb:
                wnext = load(bi + 1)
            for t in range(NTT):
                pp = self.nps()
                for k in range(16):
                    c.pe([w, hT], [pp], lambda k=k, pp=pp: nc.tensor.matmul(
                        pp[:, :256], hT[:, k, t * 128:(t + 1) * 128], w[:, k, :],
                        start=(k == 0), stop=(k == 15)))
                xt = xs[t]
                c.dve([pp, xt], [xt], lambda pp=pp, xt=xt: nc.vector.tensor_tensor(
                    xt[:, bi * 256:(bi + 1) * 256], pp[:, :256], xt[:, bi * 256:(bi + 1) * 256], ALU.add))

    def xattn(self, xs, hT, mem, mem_g, w_q, w_kv, w_o):
        c, nc = self.c, self.nc
        scale = 128 ** -0.5
        memT = c.sb([128, 16, 256], BF16, "memT", nsub=2)
        self.load_gain(mem_g)
        mt = self.junk2
        for t in range(2):
            c.dma("sp", [], [mt], lambda e, t=t: e.dma_start(out=mt[:, :], in_=mem[t * 128:(t + 1) * 128, :]))
            hn = self.hn[self.hni % 2]
            self.hni += 1
            self.rmsnorm_rows(mt, hn, hn[:, :])
            self._T16(hn, memT, t)
        kT = c.sb([128, 4, 256], BF16, "xkT")
        vv = c.sb([128, 2, 512], BF16, "xv")
        qT = self.actT
        Wkv = w_kv.rearrange("(k p) n -> p k n", p=128)
        Wq = w_q.rearrange("(k p) n -> p k n", p=128)
        for bi in range(2):
            w = self.wring()
            c.dma("pool", [], [w], lambda e, w=w: e.dma_start(out=w[:, :, :], in_=Wkv[:, :, bi * 256:(bi + 1) * 256]))
            for j in range(2):
                pp = self.nps()
                for k in range(16):
                    c.pe([w, memT], [pp], lambda k=k, pp=pp, w=w: nc.tensor.matmul(
                        pp[:, :256], w[:, k, j * 128:(j + 1) * 128], memT[:, k, :], start=(k == 0), stop=(k == 15)))
                self.evac(pp[:, :256], pp, kT[:, bi * 2 + j, :], kT)
        for bi in range(2):
            w = self.wring()
            c.dma("pool", [], [w], lambda e, w=w: e.dma_start(out=w[:, :, :], in_=Wkv[:, :, 512 + bi * 256:512 + (bi + 1) * 256]))
            for t in range(2):
                pp = self.nps()
                for k in range(16):
                    c.pe([w, memT], [pp], lambda k=k, pp=pp, w=w: nc.tensor.matmul(
                        pp[:, :256], memT[:, k, t * 128:(t + 1) * 128], w[:, k, :], start=(k == 0), stop=(k == 15)))
                self.evac(pp[:, :256], pp, vv[:, t, bi * 256:(bi + 1) * 256], vv)
        for bi in range(2):
            w = self.wring()
            c.dma("pool", [], [w], lambda e, w=w: e.dma_start(out=w[:, :, :], in_=Wq[:, :, bi * 256:(bi + 1) * 256]))
            for j in range(2):
                for th in range(2):
                    pp = self.nps()
                    for k in range(16):
                        c.pe([w, hT], [pp], lambda k=k, pp=pp, w=w: nc.tensor.matmul(
                            pp[:, :], w[:, k, j * 128:(j + 1) * 128], hT[:, k, th * 512:(th + 1) * 512],
                            start=(k == 0), stop=(k == 15)))
                    self.evac(pp[:, :], pp, qT[:, bi * 2 + j, th * 512:(th + 1) * 512], qT)
        Wo = w_o.rearrange("(k p) n -> p k n", p=128)
        c.dma("pool", [], [self.wo], lambda e: e.dma_start(out=self.wo[:, :, :], in_=Wo[:, :, :]))
        Ps = [[c.sb([128, 256], BF16, "xP") for _ in range(4)]] * 2
        PTs = [c.sb([128, 4, 256], BF16, "xPT")] * 2
        osbs = [c.sb([128, 512], BF16, "xo")] * 2
        oTs = [c.sb([128, 4, 128], BF16, "xoT")] * 2
        sts = [[c.sb([128, 8], F32, "xst") for _ in range(4)] for _ in range(2)]
        for t in range(NTT):
            P, PT, osb, oT, st = Ps[t % 2], PTs[t % 2], osbs[t % 2], oTs[t % 2], sts[t % 2]
            pps = [self.nps() for _ in range(4)]
            for h in range(4):
                c.pe([qT, kT], [pps[h]], lambda h=h: nc.tensor.matmul(
                    pps[h][:, :256], qT[:, h, t * 128:(t + 1) * 128], kT[:, h, :], start=True, stop=True))
            for h in range(4):
                c.dve([pps[h]], [st[h]], lambda h=h: nc.vector.reduce_max(st[h][:, 0:1], pps[h][:, :256], axis=AX.X))
                c.dve([st[h]], [st[h]], lambda h=h: nc.vector.tensor_scalar(st[h][:, 1:2], st[h][:, 0:1], -scale, None, ALU.mult))
            for h in range(4):
                c.act([pps[h], st[h]], [P[h], st[h]], lambda h=h: nc.scalar.activation(
                    out=P[h][:, :], in_=pps[h][:, :256], func=AF.Exp, bias=st[h][:, 1:2], scale=scale, accum_out=st[h][:, 2:3]))
            for h in range(4):
                c.dve([st[h]], [st[h]], lambda h=h: nc.vector.reciprocal(st[h][:, 3:4], st[h][:, 2:3]))
            ptp = self.npt()
            for h in range(4):
                for mc in range(2):
                    c.pe([P[h], self.ident_b], [ptp], lambda mc=mc, h=h: nc.tensor.transpose(
                        ptp[:, h * 256 + mc * 128:h * 256 + (mc + 1) * 128], P[h][:, mc * 128:(mc + 1) * 128], self.ident_b[:]))
            c.act([ptp], [PT], lambda: nc.scalar.copy(
                out=PT[:, :, :], in_=ptp[:, :1024].rearrange("p (k n) -> p k n", k=4)))
            po = self.nps()
            for h in range(4):
                for mc in range(2):
                    c.pe([PT, vv], [po], lambda mc=mc, h=h: nc.tensor.matmul(
                        po[:, h * 128:(h + 1) * 128], PT[:, h, mc * 128:(mc + 1) * 128], vv[:, mc, h * 128:(h + 1) * 128],
                        start=(mc == 0), stop=(mc == 1)))
            for h in range(4):
                c.dve([po, st[h]], [osb], lambda h=h: nc.vector.tensor_scalar(
                    osb[:, h * 128:(h + 1) * 128], po[:, h * 128:(h + 1) * 128], st[h][:, 3:4], None, ALU.mult))
            ptp2 = self.npt()
            for h in range(4):
                c.pe([osb, self.ident_b], [ptp2], lambda h=h: nc.tensor.transpose(
                    ptp2[:, h * 128:(h + 1) * 128], osb[:, h * 128:(h + 1) * 128], self.ident_b[:]))
            c.act([ptp2], [oT], lambda: nc.scalar.copy(
                out=oT[:, :, :], in_=ptp2[:, :512].rearrange("p (k n) -> p k n", k=4)))
            xt = xs[t]
            for cb in range(4):
                pp = self.nps()
                for h in range(4):
                    c.pe([oT, self.wo], [pp], lambda h=h, pp=pp: nc.tensor.matmul(
                        pp[:, :], oT[:, h, :], self.wo[:, h, cb * 512:(cb + 1) * 512], start=(h == 0), stop=(h == 3)))
                c.dve([pp, xt], [xt], lambda pp=pp: nc.vector.tensor_tensor(
                    xt[:, cb * 512:(cb + 1) * 512], pp[:, :], xt[:, cb * 512:(cb + 1) * 512], ALU.add))


def new_nc():
    return bass.Bass("TRN2", target_bir_lowering=False)


EVEN_SEGS = [(0, 2048, "fm", 0), (3072, 6144, "fm", 2048),
             (2048, 3072, "tm", 0), (6144, 7168, "tm", 1024), (7168, 7184, "tm", 2048)]
ODD_SEGS = [(0, 2048, "fm", 0), (2048, 8192, "tm", 0)]
NF = {"even": 5120, "odd": 2048}
NT = {"even": 2064, "odd": 6144}
NWIN = {"even": AB_IN, "odd": 8192}


def build_tok(mix, pre, final):
    nc = new_nc()
    es = ExitStack()
    with es:
        c = Ctx(nc, es)
        di = lambda n, sh: c.dram(n, sh, F32, "ExternalInput")
        x = di("x", [TPC, D])
        ident = di("ident", [128, 128])
        if mix:
            y = di("y", [TPC, D]); w_mo = di("w_mo", [D, D]); mem = di("mem", [256, D]); mem_g = di("mem_g", [D])
            w_q = di("w_q", [D, 512]); w_kv = di("w_kv", [D, 1024]); w_o = di("w_o", [512, D])
            g2 = di("g2", [D]); g3 = di("g3", [D]); f1_in = di("f1_in", [D, 2 * DFF]); f1_out = di("f1_out", [DFF, D])
        if pre:
            g0 = di("g0", [D]); g1 = di("g1", [D]); f0_in = di("f0_in", [D, 2 * DFF]); f0_out = di("f0_out", [DFF, D])
            w_in = di("w_in", [D, NWIN[pre]])
            xo = c.dram("xo", [TPC, D], F32, "ExternalOutput")
            pf = c.dram("pf", [NF[pre], TPC], F32, "ExternalOutput")
            pt = c.dram("pt", [TPC, NT[pre]], F32, "ExternalOutput")
        if final:
            gf = di("gf", [D])
            out = c.dram("out", [TPC, D], F32, "ExternalOutput")
        tp = TokProg(c)
        tp.load_ident(ident)
        tp.ffn_alloc()
        xs = [c.sb([128, D], F32, "x") for _ in range(NTT)]
        hT = c.sb([128, 16, TPC], BF16, "hT", nsub=NTT)
        for t in range(NTT):
            c.dma("sp", [], [xs[t]], lambda e, t=t: e.dma_start(out=xs[t][:, :], in_=x[t * 128:(t + 1) * 128, :]))
        if mix:
            for t in range(NTT):
                yt = tp.junk2
                c.dma("sp", [], [yt], lambda e, t=t: e.dma_start(out=yt[:, :], in_=y[t * 128:(t + 1) * 128, :]))
                tp.plain_T(yt, hT, t)
            tp.linear_res(xs, hT, w_mo.h)
            tp.load_gain(g2.h)
            for t in range(NTT):
                tp.norm_T(xs[t], hT, t)
            tp.xattn(xs, hT, mem.h, mem_g.h, w_q.h, w_kv.h, w_o.h)
            tp.load_gain(g3.h)
            for t in range(NTT):
                tp.norm_T(xs[t], hT, t)
            tp.ffn(xs, hT, f1_in.h, f1_out.h)
        if pre:
            tp.load_gain(g0.h)
            for t in range(NTT):
                tp.norm_T(xs[t], hT, t)
            tp.ffn(xs, hT, f0_in.h, f0_out.h)
            for t in range(NTT):
                c.dma("sp", [xs[t]], [xo], lambda e, t=t: e.dma_start(out=xo[t * 128:(t + 1) * 128, :], in_=xs[t][:, :]))
            tp.load_gain(g1.h)
            for t in range(NTT):
                tp.norm_T(xs[t], hT, t)
            tp.inproj(hT, w_in.h, EVEN_SEGS if pre == "even" else ODD_SEGS, pf, pt)
        if final:
            tp.load_gain(gf.h)
            for t in range(NTT):
                ot = tp.junk2
                tp.rmsnorm_rows(xs[t], ot, ot[:, :])
                c.dma("sp", [ot], [out], lambda e, t=t: e.dma_start(out=out[t * 128:(t + 1) * 128, :], in_=ot[:, :]))
        c.finish()
    return nc, c


def build_ffn_test():
    nc = new_nc()
    es = ExitStack()
    with es:
        c = Ctx(nc, es)
        x = c.dram("x", [TPC, D], F32, "ExternalInput")
        g = c.dram("g", [D], F32, "ExternalInput")
        w_in = c.dram("w_in", [D, 2 * DFF], F32, "ExternalInput")
        w_out = c.dram("w_out", [DFF, D], F32, "ExternalInput")
        ident = c.dram("ident", [128, 128], F32, "ExternalInput")
        y = c.dram("y", [TPC, D], F32, "ExternalOutput")
        tp = TokProg(c)
        tp.load_ident(ident)
        tp.ffn_alloc()
        xs = [c.sb([128, D], F32, "x") for _ in range(NTT)]
        hT = c.sb([128, 16, TPC], BF16, "hT", nsub=NTT)
        tp.load_gain(g.h)
        for t in range(NTT):
            c.dma("sp", [], [xs[t]], lambda e, t=t: e.dma_start(out=xs[t][:, :], in_=x[t * 128:(t + 1) * 128, :]))
        for t in range(NTT):
            tp.norm_T(xs[t], hT, t)
        tp.ffn(xs, hT, w_in.h, w_out.h)
        for t in range(NTT):
            c.dma("sp", [xs[t]], [y], lambda e, t=t: e.dma_start(out=y[t * 128:(t + 1) * 128, :], in_=xs[t][:, :]))
        c.finish()
    return nc, c


class MixProg:
    def __init__(self, c, cst):
        self.c = c
        nc = c.nc
        self.nc = nc
        self.K = c.sb([128, 5, 128], F32, "consts")
        c.dma("sp", [], [self.K], lambda e: e.dma_start(out=self.K[:, :, :], in_=cst.rearrange("k p n -> p k n")))
        self.identf = self.K[:, 0, :]
        self.U1 = self.K[:, 1, :]
        self.U2 = self.K[:, 2, :]
        self.ones = self.K[:, 3, :]
        self.Kb = c.sb([128, 5, 128], BF16, "constsb")
        c.dve([self.K], [self.Kb], lambda: nc.vector.tensor_copy(self.Kb[:, :, :], self.K[:, :, :]))
        self.identb = self.Kb[:, 0, :]
        self.triTb = self.Kb[:, 4, :]
        self.pb = [c.ps([128, 512], F32, "pm") for _ in range(8)]
        self.prings = {"all": [list(range(8)), 0]}
        self.rings = {}

    def nps(self, ring="all"):
        r = self.prings[ring]
        p = self.pb[r[0][r[1] % len(r[0])]]
        r[1] += 1
        return p

    def ring(self, key, shape, dt, n=2):
        if key not in self.rings:
            self.rings[key] = [[self.c.sb(shape, dt, key) for _ in range(n)], 0]
        r = self.rings[key]
        t = r[0][r[1] % n]
        r[1] += 1
        return t

    def rms_gate_out(self, o_ps, gbc, gt, y_dram, row0, col0, tag):
        for _ in self.rms_gate_out_g(o_ps, gbc, gt, y_dram, row0, col0, tag):
            pass

    def rms_gate_out_g(self, o_ps, gbc, gt, y_dram, row0, col0, tag):
        c, nc = self.c, self.nc
        st = self.ring(tag + "st", [128, 8], F32, 3)
        jk = self.ring(tag + "jk", [128, 128], F32, 2)
        c.act([o_ps], [jk, st], lambda: nc.scalar.activation(
            out=jk[:, :], in_=o_ps[:, :128], func=AF.Square, accum_out=st[:, 0:1]))
        c.dve([st], [st], lambda: nc.vector.tensor_scalar(st[:, 1:2], st[:, 0:1], 1.0 / 128, EPS, ALU.mult, ALU.add))
        yield
        c.act([st], [st], lambda: nc.scalar.sqrt(st[:, 3:4], st[:, 1:2]))
        c.dve([st], [st], lambda: nc.vector.reciprocal(st[:, 2:3], st[:, 3:4]))
        on = self.ring(tag + "on", [128, 128], F32, 2)
        c.dve([o_ps, st, gbc], [on], lambda: nc.vector.scalar_tensor_tensor(
            out=on[:, :], in0=o_ps[:, :128], scalar=st[:, 2:3], in1=gbc[:, :], op0=ALU.mult, op1=ALU.mult))
        yield
        sg = self.ring(tag + "sg", [128, 128], F32, 2)
        c.act([gt], [sg], lambda: nc.scalar.activation(out=sg[:, :], in_=gt[:, :], func=AF.Silu))
        yt = self.ring(tag + "yt", [128, 128], F32, 3)
        c.dve([on, sg], [yt], lambda: nc.vector.tensor_tensor(yt[:, :], on[:, :], sg[:, :], ALU.mult))
        c.dma("sp", [yt], [y_dram], lambda e: e.dma_start(out=y_dram[row0:row0 + 128, col0:col0 + 128], in_=yt[:, :]))

    def hgrn2(self, qT, f, iv, gg, lg, coef, ong, y, nheads=4):
        c, nc = self.c, self.nc
        assert nheads == 4
        scale = 128 ** -0.5
        NR = T // 128
        W = 512
        v3 = lambda ap: ap.rearrange("p (h n) -> p h n", h=4)
        gbc = c.sb([128, 128], F32, "hg_gbc")
        c.dma("sp", [], [gbc], lambda e: e.dma_start(out=gbc[:, :], in_=ong.partition_broadcast(128)))
        cf = c.sb([128, 4], F32, "hg_coef")
        c.dma("sp", [], [cf], lambda e: e.dma_start(out=cf[:, :], in_=coef.partition_broadcast(128)))
        lbB = c.sb([128, W], F32, "lbB")
        omlB = c.sb([128, W], F32, "omlB")
        S = c.sb([128, W], F32, "S")
        Sb = c.sb([128, W], BF16, "Sb")
        pp = self.nps()
        for h in range(4):
            lgt = c.sb([128, 4], F32, "lgt")
            c.dma("sp", [], [lgt], lambda e, h=h, lgt=lgt: e.dma_start(out=lgt[:, :], in_=lg[h]))
            st = c.sb([128, 8], F32, "lbst")
            c.dve([lgt], [st], lambda: nc.vector.reduce_max(st[:, 0:1], lgt[:, :], axis=AX.X))
            c.dve([st], [st], lambda: nc.vector.tensor_scalar(st[:, 1:2], st[:, 0:1], -1.0, None, ALU.mult))
            ee = c.sb([128, 4], F32, "lbe")
            c.act([lgt, st], [ee, st], lambda: nc.scalar.activation(
                out=ee[:, :], in_=lgt[:, :], func=AF.Exp, bias=st[:, 1:2], scale=1.0, accum_out=st[:, 2:3]))
            c.dve([st], [st], lambda: nc.vector.reciprocal(st[:, 3:4], st[:, 2:3]))
            c.dve([ee, cf], [ee], lambda: nc.vector.tensor_tensor(ee[:, :], ee[:, :], cf[:, :], ALU.mult))
            c.dve([ee], [st], lambda: nc.vector.reduce_sum(st[:, 4:5], ee[:, :], axis=AX.X))
            c.dve([st], [st], lambda: nc.vector.tensor_tensor(st[:, 5:6], st[:, 4:5], st[:, 3:4], ALU.mult))
            lbc = c.sb([128, 128], F32, "lbcB")
            c.dve([st, self.K], [lbc], lambda: nc.vector.tensor_scalar(lbc[:, :], self.ones, st[:, 5:6], None, ALU.mult))
            c.pe([lbc, self.K], [pp], lambda h=h, lbc=lbc: nc.tensor.matmul(
                pp[:, h * 128:(h + 1) * 128], lbc[:, :], self.identf, start=True, stop=True))
        c.act([pp], [lbB], lambda: nc.scalar.copy(out=lbB[:, :], in_=pp[:, :]))
        c.dve([pp], [omlB], lambda: nc.vector.tensor_scalar(omlB[:, :], pp[:, :], -1.0, 1.0, ALU.mult, ALU.add))
        c.dve([], [S], lambda: nc.vector.memset(S[:, :], 0.0))
        c.dve([], [Sb], lambda: nc.vector.memset(Sb[:, :], 0.0))
        qTv = qT.rearrange("h c t -> c h t")
        RF = lambda k, n=2: self.ring("hg_" + k, [128, W], F32, n)
        RB = lambda k, n=2: self.ring("hg_" + k, [128, W], BF16, n)

        def prep(r, out):
            rows = slice(r * 128, (r + 1) * 128)
            ft, gt, qt = RF("ft"), RF("gt", 3), RF("qt")
            vt = RB("vt", 3)
            c.dma("sp", [], [ft], lambda e: e.dma_start(out=ft[:, :], in_=f[rows, :]))
            c.dma("pool", [], [vt], lambda e: e.dma_start(out=vt[:, :], in_=iv[rows, :]))
            c.dma("sp", [], [gt], lambda e: e.dma_start(out=gt[:, :], in_=gg[rows, :]))
            c.dma("sp", [], [qt], lambda e: e.dma_start(out=v3(qt[:, :]), in_=qTv[:, :, rows]))
            sg = RF("sig")
            c.act([ft], [sg], lambda: nc.scalar.activation(out=sg[:, :], in_=ft[:, :], func=AF.Sigmoid))
            fg = RF("fg")
            c.dve([sg, omlB], [fg], lambda: nc.vector.tensor_tensor(fg[:, :], sg[:, :], omlB[:, :], ALU.mult))
            c.dve([fg, lbB], [fg], lambda: nc.vector.tensor_tensor(fg[:, :], fg[:, :], lbB[:, :], ALU.add))
            sq = RF("sq")
            c.act([qt], [sq], lambda: nc.scalar.activation(out=sq[:, :], in_=qt[:, :], func=AF.Silu))
            sgt = RF("sgt", 3)
            c.act([gt], [sgt], lambda: nc.scalar.activation(out=sgt[:, :], in_=gt[:, :], func=AF.Silu))
            logf = RF("logf")
            c.act([fg], [logf], lambda: nc.scalar.activation(out=logf[:, :], in_=fg[:, :], func=AF.Ln))
            kt = RF("kt")
            c.pool([fg], [kt], lambda: nc.gpsimd.tensor_scalar(kt[:, :], fg[:, :], -1.0, 1.0, ALU.mult, ALU.add))
            pbT, pblr, pkT = self.nps(), self.nps(), self.nps()
            for h in range(4):
                hs_ = slice(h * 128, (h + 1) * 128)
                c.pe([logf, self.K], [pbT], lambda hs_=hs_: nc.tensor.matmul(pbT[:, hs_], logf[:, hs_], self.U1, start=True, stop=True))
            c.pe([logf, self.K], [pblr], lambda: nc.tensor.matmul(pblr[:, :], self.U2, logf[:, :], start=True, stop=True))
            for h in range(4):
                hs_ = slice(h * 128, (h + 1) * 128)
                c.pe([kt, self.K], [pkT], lambda hs_=hs_: nc.tensor.transpose(pkT[:, hs_], kt[:, hs_], self.identf))
            bm = self.ring("hg_bm", [128, 4], F32, 3)
            c.dve([pbT], [bm], lambda: nc.vector.tensor_copy(bm[:, :], v3(pbT[:, :])[:, :, 63]))
            D1 = RF("D1")
            c.dve([pbT, bm], [D1], lambda: nc.vector.tensor_tensor(
                v3(D1[:, :]), v3(pbT[:, :]), bm[:, :].unsqueeze(2).to_broadcast([128, 4, 128]), ALU.subtract))
            e1, e2, e3, eb = RF("e1"), RF("e2"), RF("e3"), RF("eb")
            c.act([D1], [e1], lambda: nc.scalar.activation(out=e1[:, :], in_=D1[:, :], func=AF.Exp))
            c.act([D1], [e2], lambda: nc.scalar.activation(out=e2[:, :], in_=D1[:, :], func=AF.Exp, scale=-1.0))
            c.act([pbT], [e3], lambda: nc.scalar.activation(out=e3[:, :], in_=pbT[:, :], func=AF.Exp))
            c.act([pblr], [eb], lambda: nc.scalar.activation(out=eb[:, :], in_=pblr[:, :], func=AF.Exp))
            qtil, qhat, ktil, kdec = RB("qtil"), RB("qhat", 3), RB("ktil"), RB("kdec", 3)
            c.dve([sq, e1], [qtil], lambda: nc.vector.scalar_tensor_tensor(
                out=qtil[:, :], in0=sq[:, :], scalar=scale, in1=e1[:, :], op0=ALU.mult, op1=ALU.mult))
            c.dve([sq, e3], [qhat], lambda: nc.vector.scalar_tensor_tensor(
                out=qhat[:, :], in0=sq[:, :], scalar=scale, in1=e3[:, :], op0=ALU.mult, op1=ALU.mult))
            c.dve([pkT, e2], [ktil], lambda: nc.vector.tensor_tensor(ktil[:, :], pkT[:, :], e2[:, :], ALU.mult))
            c.pool([kt, eb], [kdec], lambda: nc.gpsimd.tensor_tensor(kdec[:, :], kt[:, :], eb[:, :], ALU.mult))
            el = self.ring("hg_el", [128, 4], F32, 3)
            c.dve([e3], [el], lambda: nc.vector.tensor_copy(el[:, :], v3(e3[:, :])[:, :, 127]))
            pa = self.nps()
            for h in range(4):
                hs_ = slice(h * 128, (h + 1) * 128)
                c.pe([ktil, qtil], [pa], lambda hs_=hs_: nc.tensor.matmul(pa[:, hs_], ktil[:, hs_], qtil[:, hs_], start=True, stop=True))
            aT = RB("aT", 3)
            c.dve([pa, self.K], [aT], lambda: nc.vector.tensor_tensor(
                v3(aT[:, :]), v3(pa[:, :]), self.U1.unsqueeze(1).to_broadcast([128, 4, 128]), ALU.mult))
            out.update(vt=vt, gt=sgt, qhat=qhat, aT=aT, kdec=kdec, el=el)

        def seq(r, p):
            vt, sgt, qhat, aT, kdec, el = (p[k] for k in ("vt", "gt", "qhat", "aT", "kdec", "el"))
            po, pS = self.nps(), self.nps()
            for h in range(4):
                hs_ = slice(h * 128, (h + 1) * 128)
                c.pe([aT, vt], [po], lambda hs_=hs_: nc.tensor.matmul(po[:, hs_], aT[:, hs_], vt[:, hs_], start=True, stop=False))
                c.pe([qhat, Sb], [po], lambda hs_=hs_: nc.tensor.matmul(po[:, hs_], qhat[:, hs_], Sb[:, hs_], start=False, stop=True))
            for h in range(4):
                hs_ = slice(h * 128, (h + 1) * 128)
                c.pe([kdec, vt], [pS], lambda hs_=hs_: nc.tensor.matmul(pS[:, hs_], kdec[:, hs_], vt[:, hs_], start=True, stop=True))
            c.dve([S, el], [S], lambda: nc.vector.tensor_tensor(
                v3(S[:, :]), v3(S[:, :]), el[:, :].unsqueeze(2).to_broadcast([128, 4, 128]), ALU.mult))
            c.dve([S, pS], [S], lambda: nc.vector.tensor_tensor(S[:, :], S[:, :], pS[:, :], ALU.add))
            c.pool([S], [Sb], lambda: nc.gpsimd.tensor_copy(Sb[:, :], S[:, :]))
            o2 = RF("o2")
            c.act([po], [o2], lambda: nc.scalar.activation(out=o2[:, :], in_=po[:, :], func=AF.Square))
            st = self.ring("hg_st", [128, 16], F32, 3)
            c.dve([o2], [st], lambda: nc.vector.tensor_reduce(out=st[:, 0:4], in_=v3(o2[:, :]), axis=AX.X, op=ALU.add))
            c.dve([st], [st], lambda: nc.vector.tensor_scalar(st[:, 4:8], st[:, 0:4], 1.0 / 128, EPS, ALU.mult, ALU.add))
            c.act([st], [st], lambda: nc.scalar.activation(out=st[:, 8:12], in_=st[:, 4:8], func=AF.Ln))
            c.act([st], [st], lambda: nc.scalar.activation(out=st[:, 12:16], in_=st[:, 8:12], func=AF.Exp, scale=-0.5))
            on = RF("on")
            c.dve([po, st], [on], lambda: nc.vector.tensor_tensor(
                v3(on[:, :]), v3(po[:, :]), st[:, 12:16].unsqueeze(2).to_broadcast([128, 4, 128]), ALU.mult))
            c.pool([on, gbc], [on], lambda: nc.gpsimd.tensor_tensor(
                v3(on[:, :]), v3(on[:, :]), gbc[:, :].unsqueeze(1).to_broadcast([128, 4, 128]), ALU.mult))
            yt = RF("yt", 3)
            c.dve([on, sgt], [yt], lambda: nc.vector.tensor_tensor(yt[:, :], on[:, :], sgt[:, :], ALU.mult))
            c.dma("sp", [yt], [y], lambda e: e.dma_start(out=y[r * 128:(r + 1) * 128, :], in_=yt[:, :]))

        P = {0: {}}
        prep(0, P[0])
        for r in range(NR):
            if r + 1 < NR:
                P[r + 1] = {}
                prep(r + 1, P[r + 1])
            seq(r, P.pop(r))


def lockstep(gens):
    gens = list(gens)
    while gens:
        nxt = []
        for g in gens:
            try:
                next(g)
                nxt.append(g)
            except StopIteration:
                pass
        gens = nxt


def mix_consts():
    i = np.arange(128)
    ident = np.eye(128, dtype=np.float32)
    U1 = (i[:, None] <= i[None, :]).astype(np.float32)
    U2 = (i[:, None] > i[None, :]).astype(np.float32)
    ones = np.ones((128, 128), np.float32)
    triT = np.where(i[:, None] <= i[None, :], 0.0, -30000.0).astype(np.float32)
    return np.stack([ident, U1, U2, ones, triT]).astype(np.float32)


def build_hgrn2(nheads=4):
    nc = new_nc()
    es = ExitStack()
    with es:
        c = Ctx(nc, es)
        di = lambda n, sh: c.dram(n, sh, F32, "ExternalInput")
        cst = di("cst", [5, 128, 128])
        qT = di("qT", [nheads, 128, T]); f = di("f", [T, nheads * 128]); iv = di("iv", [T, nheads * 128]); gg = di("gg", [T, nheads * 128])
        lg = di("lg", [nheads, 128, 4]); coef = di("coef", [4]); ong = di("ong", [128])
        y = c.dram("y", [T, nheads * 128], F32, "ExternalOutput")
        mp = MixProg(c, cst.h)
        mp.hgrn2(qT.h, f.h, iv.h, gg.h, lg.h, coef.h, ong.h, y, nheads)
        c.finish()
    return nc, c


def lockstep_g(gens):
    gens = list(gens)
    while gens:
        nxt = []
        for g in gens:
            try:
                next(g)
                nxt.append(g)
            except StopIteration:
                pass
        gens = nxt
        yield


def _gdn_g(self, qkvT, cw, gate, ba, sc, ong, y, nheads=2, col_off=0, bigs=None, pr="all"):
    c, nc = self.c, self.nc
    NR = T // 128
    I, U1, U2, ONES = self.identf, self.U1, self.U2, self.ones
    nps = lambda: self.nps(pr)
    gbc = c.sb([128, 128], F32, "dn_gbc")
    c.dma("sp", [], [gbc], lambda e: e.dma_start(out=gbc[:, :], in_=ong.partition_broadcast(128)))
    HS = []
    for h in range(nheads):
        d = {}
        d["cw"] = [c.sb([128, 4], F32, "dn_cw") for _ in range(3)]
        for xi in range(3):
            c.dma("sp", [], [d["cw"][xi]], lambda e, xi=xi: e.dma_start(out=d["cw"][xi][:, :], in_=cw[xi, h]))
        bat = c.sb([128, 64], F32, "dn_bat")
        sct = c.sb([128, 2], F32, "dn_sct")
        c.dma("sp", [], [bat], lambda e: e.dma_start(out=bat[:, :], in_=ba[h]))
        c.dma("sp", [], [sct], lambda e: e.dma_start(out=sct[:, :], in_=sc[h].partition_broadcast(128)))
        beta = c.sb([128, 32], F32, "dn_beta")
        nbeta = c.sb([128, 32], F32, "dn_nbeta")
        gg = c.sb([128, 32], F32, "dn_g")
        ea = c.sb([128, 1], F32, "dn_ea")
        c.act([bat], [beta], lambda: nc.scalar.activation(out=beta[:, :], in_=bat[:, 0:32], func=AF.Sigmoid))
        c.dve([beta], [nbeta], lambda: nc.vector.tensor_scalar(nbeta[:, :], beta[:, :], -1.0, None, ALU.mult))
        c.act([bat, sct], [gg], lambda: nc.scalar.activation(out=gg[:, :], in_=bat[:, 32:64], func=AF.Exp, bias=sct[:, 1:2], scale=1.0))
        c.dve([gg], [gg], lambda: nc.vector.tensor_scalar(gg[:, :], gg[:, :], 1.0, None, ALU.add))
        c.act([gg], [gg], lambda: nc.scalar.activation(out=gg[:, :], in_=gg[:, :], func=AF.Ln))
        c.act([sct], [ea], lambda: nc.scalar.activation(out=ea[:, :], in_=sct[:, 0:1], func=AF.Exp))
        c.dve([gg, ea], [gg], lambda: nc.vector.tensor_scalar(gg[:, :], gg[:, :], ea[:, 0:1], -1.0, ALU.mult, ALU.mult))
        S = c.sb([128, 128], F32, "dn_S")
        c.dve([], [S], lambda: nc.vector.memset(S[:, :], 0.0))
        d.update(beta=beta, nbeta=nbeta, gg=gg, S=S)
        HS.append(d)
    yield

    def prep(r, h, out):
        d = HS[h]
        beta, nbeta, gg = d["beta"], d["nbeta"], d["gg"]
        tg = "dn%d_" % h
        R_ = lambda k, n=2: self.ring(tg + k, [128, 128], F32, n)
        RB_ = lambda k, n=2: self.ring(tg + k, [128, 128], BF16, n)
        rows = slice(r * 128, (r + 1) * 128)
        gcol = gg[:, r:r + 1]
        gt = self.ring(tg + "gt", [128, 128], F32, 4)
        c.dma("sp", [], [gt], lambda e: e.dma_start(out=gt[:, :], in_=gate[rows, h * 128:(h + 1) * 128]))
        X = []
        for xi in range(3):
            rw = self.ring(tg + "raw%d" % xi, [128, 131], F32, 2)
            if r == 0:
                c.dve([], [rw], lambda: nc.vector.memset(rw[:, 0:3], 0.0))
                c.dma("sp", [], [rw], lambda e: e.dma_start(out=rw[:, 3:131], in_=qkvT[xi, h][:, 0:128]))
            else:
                c.dma("sp", [], [rw], lambda e: e.dma_start(out=rw[:, :], in_=qkvT[xi, h][:, r * 128 - 3:(r + 1) * 128]))
            cwt = d["cw"][xi]
            acc = R_("acc%d" % xi)
            eng = c.dve
            ee = nc.vector
            eng([rw, cwt], [acc], lambda: ee.tensor_scalar(acc[:, :], rw[:, 3:131], cwt[:, 3:4], None, ALU.mult))
            for sh in (1, 2, 3):
                eng([rw, cwt, acc], [acc], lambda sh=sh: ee.scalar_tensor_tensor(
                    out=acc[:, :], in0=rw[:, 3 - sh:131 - sh], scalar=cwt[:, 3 - sh:4 - sh], in1=acc[:, :],
                    op0=ALU.mult, op1=ALU.add))
            X.append(acc)
        yield
        XS = []
        for xi in range(3):
            xs_ = R_("x%d" % xi)
            c.act([X[xi]], [xs_], lambda xi=xi, xs_=xs_: nc.scalar.activation(out=xs_[:, :], in_=X[xi][:, :], func=AF.Silu))
            XS.append(xs_)
        qs, ks, vs = XS
        yield
        for xi in range(2):
            Xt = XS[xi]
            sq = R_("sq%d" % xi)
            c.act([Xt], [sq], lambda: nc.scalar.activation(out=sq[:, :], in_=Xt[:, :], func=AF.Square))
            pp = nps()
            c.pe([sq, self.K], [pp], lambda: nc.tensor.matmul(pp[:, :128], ONES, sq[:, :], start=True, stop=True))
            rn = R_("rn%d" % xi)
            c.dve([pp], [rn], lambda: nc.vector.tensor_scalar(rn[:, :], pp[:, :128], EPS, None, ALU.add))
            XS.append(rn)
        yield
        for xi in range(2):
            rn = XS[3 + xi]
            c.act([rn], [rn], lambda: nc.scalar.sqrt(rn[:, :], rn[:, :]))
            c.dve([rn], [rn], lambda: nc.vector.reciprocal(rn[:, :], rn[:, :]))
            Xt = XS[xi]
            sc_ = (128 ** -0.5) if xi == 0 else 1.0
            c.dve([Xt, rn], [Xt], lambda: nc.vector.scalar_tensor_tensor(
                out=Xt[:, :], in0=Xt[:, :], scalar=sc_, in1=rn[:, :], op0=ALU.mult, op1=ALU.mult))
        yield
        pk, pv = nps(), nps()
        c.pe([ks, self.K], [pk], lambda: nc.tensor.transpose(pk[:, :128], ks[:, :], I))
        c.pe([vs, self.K], [pv], lambda: nc.tensor.transpose(pv[:, :128], vs[:, :], I))
        ktm, Vb = R_("ktm"), R_("Vb")
        c.act([pk], [ktm], lambda: nc.scalar.copy(out=ktm[:, :], in_=pk[:, :128]))
        c.dve([pv, beta], [Vb], lambda: nc.vector.tensor_scalar(Vb[:, :], pv[:, :128], beta[:, r:r + 1], None, ALU.mult))
        gU2, gcB = R_("gU2"), R_("gcB")
        c.pool([gg, self.K], [gU2], lambda: nc.gpsimd.tensor_scalar(gU2[:, :], U2, gcol, None, ALU.mult))
        c.pool([gg, self.K], [gcB], lambda: nc.gpsimd.tensor_scalar(gcB[:, :], ONES, gcol, None, ALU.mult))
        yield
        pD, pDT, pGB, pcol = nps(), nps(), nps(), nps()
        c.pe([gU2, self.K], [pD], lambda: nc.tensor.matmul(pD[:, :128], U1, gU2[:, :], start=True, stop=True))
        c.pe([gU2, self.K], [pDT], lambda: nc.tensor.matmul(pDT[:, :128], gU2[:, :], U1, start=True, stop=True))
        c.pe([gcB, self.K], [pGB], lambda: nc.tensor.matmul(pGB[:, :128], gcB[:, :], U1, start=True, stop=True))
        c.pe([gcB, self.K], [pcol], lambda: nc.tensor.matmul(pcol[:, 0:128], U1, gcB[:, :], start=True, stop=True))
        c.pe([gcB, self.K], [pcol], lambda: nc.tensor.matmul(pcol[:, 128:256], U2, gcB[:, :], start=True, stop=True))
        c.pe([gcB, self.K], [pcol], lambda: nc.tensor.matmul(pcol[:, 256:384], ONES, gcB[:, :], start=True, stop=True))
        ecol = self.ring(tg + "ecol", [128, 4], F32, 4)
        c.act([pcol], [ecol], lambda: nc.scalar.activation(
            out=ecol[:, 0:3], in_=pcol[:, 0:384].rearrange("p (k n) -> p k n", n=128)[:, :, 0], func=AF.Exp))
        expD, expDT, eGB = R_("expD"), R_("expDT"), R_("eGB")
        c.act([pD], [expD], lambda: nc.scalar.activation(out=expD[:, :], in_=pD[:, :128], func=AF.Exp))
        c.act([pDT], [expDT], lambda: nc.scalar.activation(out=expDT[:, :], in_=pDT[:, :128], func=AF.Exp))
        c.act([pGB], [eGB], lambda: nc.scalar.activation(out=eGB[:, :], in_=pGB[:, :128], func=AF.Exp))
        c.pool([expD, self.K], [expD], lambda: nc.gpsimd.tensor_tensor(expD[:, :], expD[:, :], U2, ALU.mult))
        c.pool([expDT, self.K], [expDT], lambda: nc.gpsimd.tensor_tensor(expDT[:, :], expDT[:, :], U1, ALU.mult))
        yield
        pKK, pQK = nps(), nps()
        c.pe([ks], [pKK], lambda: nc.tensor.matmul(pKK[:, :128], ks[:, :], ks[:, :], start=True, stop=True))
        c.pe([ks, qs], [pQK], lambda: nc.tensor.matmul(pQK[:, :128], ks[:, :], qs[:, :], start=True, stop=True))
        A = R_("A")
        c.dve([pKK, nbeta, expD], [A], lambda: nc.vector.scalar_tensor_tensor(
            out=A[:, :], in0=pKK[:, :128], scalar=nbeta[:, r:r + 1], in1=expD[:, :], op0=ALU.mult, op1=ALU.mult))
        aqkT = R_("aqkT", 4)
        c.dve([pQK, expDT], [aqkT], lambda: nc.vector.tensor_tensor(aqkT[:, :], pQK[:, :128], expDT[:, :], ALU.mult))
        qdT = R_("qdT", 4)
        c.pool([qs, eGB], [qdT], lambda: nc.gpsimd.tensor_tensor(qdT[:, :], qs[:, :], eGB[:, :], ALU.mult))
        bcol = self.ring(tg + "bcol", [128, 1], F32, 3)
        c.dve([beta, ecol], [bcol], lambda: nc.vector.tensor_tensor(bcol[:, :], beta[:, r:r + 1], ecol[:, 0:1], ALU.mult))
        kbg, kdec = R_("kbg"), R_("kdec", 4)
        c.dve([ktm, bcol], [kbg], lambda: nc.vector.tensor_scalar(kbg[:, :], ktm[:, :], bcol[:, 0:1], None, ALU.mult))
        c.pool([ktm, ecol], [kdec], lambda: nc.gpsimd.tensor_scalar(kdec[:, :], ktm[:, :], ecol[:, 1:2], None, ALU.mult))
        yield
        pB = nps()
        c.pe([A, self.K], [pB], lambda: nc.tensor.transpose(pB[:, :128], A[:, :], I))
        Bm = R_("B")
        c.act([pB], [Bm], lambda: nc.scalar.copy(out=Bm[:, :], in_=pB[:, :128]))
        Rm = R_("R", 3)
        c.dve([pB, self.K], [Rm], lambda: nc.vector.tensor_tensor(Rm[:, :], pB[:, :128], I, ALU.add))
        yield
        Aj, Bj = A, Bm
        for lvl in range(6):
            pA2 = nps()
            c.pe([Aj, Bj], [pA2], lambda Aj=Aj, Bj=Bj, pA2=pA2: nc.tensor.matmul(pA2[:, :128], Bj[:, :], Aj[:, :], start=True, stop=True))
            if lvl < 5:
                pB2 = nps()
                c.pe([Aj, Bj], [pB2], lambda Aj=Aj, Bj=Bj, pB2=pB2: nc.tensor.matmul(pB2[:, :128], Aj[:, :], Bj[:, :], start=True, stop=True))
                A2, B2 = R_("A2", 3), R_("B2", 3)
            IA = R_("IA")
            c.dve([pA2, self.K], [IA], lambda IA=IA, pA2=pA2: nc.vector.tensor_tensor(IA[:, :], pA2[:, :128], I, ALU.add))
            if lvl < 5:
                c.act([pA2], [A2], lambda A2=A2, pA2=pA2: nc.scalar.copy(out=A2[:, :], in_=pA2[:, :128]))
                c.act([pB2], [B2], lambda B2=B2, pB2=pB2: nc.scalar.copy(out=B2[:, :], in_=pB2[:, :128]))
            pR = nps()
            c.pe([IA, Rm], [pR], lambda IA=IA, Rm=Rm, pR=pR: nc.tensor.matmul(pR[:, :128], IA[:, :], Rm[:, :], start=True, stop=True))
            Rn = R_("R", 3)
            c.dve([pR], [Rn], lambda Rn=Rn, pR=pR: nc.vector.tensor_copy(Rn[:, :], pR[:, :128]))
            Rm = Rn
            if lvl < 5:
                Aj, Bj = A2, B2
            yield
        pu, pw = nps(), nps()
        c.pe([Rm, Vb], [pu], lambda: nc.tensor.matmul(pu[:, :128], Rm[:, :], Vb[:, :], start=True, stop=True))
        c.pe([Rm, kbg], [pw], lambda: nc.tensor.matmul(pw[:, :128], kbg[:, :], Rm[:, :], start=True, stop=True))
        u, wT = R_("u", 4), R_("wT", 4)
        c.act([pu], [u], lambda: nc.scalar.copy(out=u[:, :], in_=pu[:, :128]))
        c.dve([pw], [wT], lambda: nc.vector.tensor_copy(wT[:, :], pw[:, :128]))
        out.update(u=u, wT=wT, qdT=qdT, aqkT=aqkT, kdec=kdec, ecol=ecol, gt=gt)
        yield

    def seq(r, h, p):
        S = HS[h]["S"]
        tg = "dn%d_" % h
        R_ = lambda k, n=2: self.ring(tg + k, [128, 128], F32, n)
        u, wT, qdT, aqkT, kdec, ecol, gt = (p[k] for k in ("u", "wT", "qdT", "aqkT", "kdec", "ecol", "gt"))
        pvn = nps()
        c.pe([wT, S], [pvn], lambda: nc.tensor.matmul(pvn[:, :128], wT[:, :], S[:, :], start=True, stop=True))
        vn = R_("vn")
        c.dve([u, pvn], [vn], lambda: nc.vector.tensor_tensor(vn[:, :], u[:, :], pvn[:, :128], ALU.subtract))
        po = nps()
        c.pe([qdT, S], [po], lambda: nc.tensor.matmul(po[:, :128], qdT[:, :], S[:, :], start=True, stop=False))
        c.pe([aqkT, vn], [po], lambda: nc.tensor.matmul(po[:, :128], aqkT[:, :], vn[:, :], start=False, stop=True))
        pS = nps()
        c.pe([kdec, vn], [pS], lambda: nc.tensor.matmul(pS[:, :128], kdec[:, :], vn[:, :], start=True, stop=True))
        c.dve([pS, ecol, S], [S], lambda: nc.vector.scalar_tensor_tensor(
            out=S[:, :], in0=S[:, :], scalar=ecol[:, 2:3], in1=pS[:, :128], op0=ALU.mult, op1=ALU.add))
        osb = R_("osb")
        c.act([po], [osb], lambda: nc.scalar.copy(out=osb[:, :], in_=po[:, :128]))
        yield
        yield from self.rms_gate_out_g(osb, gbc, gt, y, r * 128, col_off + h * 128, tg)

    P = {}
    NB = 2

    def run_prep(b):
        gens = []
        for r in range(b * NB, (b + 1) * NB):
            for h in range(nheads):
                P[(r, h)] = {}
                gens.append(prep(r, h, P[(r, h)]))
        yield from lockstep_g(gens)
    yield from run_prep(0)
    for b in range(NR // NB):
        if b + 1 < NR // NB:
            yield from run_prep(b + 1)
        for r in range(b * NB, (b + 1) * NB):
            yield from lockstep_g([seq(r, h, P.pop((r, h))) for h in range(nheads)])


def _gdn(self, *a, **kw):
    for _ in _gdn_g(self, *a, **kw):
        pass


MixProg.gdn_g = _gdn_g
MixProg.gdn = _gdn
MixProg.gdn = _gdn


def build_gdn(nheads=2):
    nc = new_nc()
    es = ExitStack()
    with es:
        c = Ctx(nc, es)
        di = lambda n, sh: c.dram(n, sh, F32, "ExternalInput")
        cst = di("cst", [5, 128, 128])
        qkvT = di("qkvT", [3, nheads, 128, T]); cw = di("cw", [3, nheads, 128, 4]); gate = di("gate", [T, nheads * 128])
        ba = di("ba", [nheads, 128, 64]); sc = di("sc", [nheads, 2]); ong = di("ong", [128])
        y = c.dram("y", [T, nheads * 128], F32, "ExternalOutput")
        mp = MixProg(c, cst.h)
        mp.gdn(qkvT.h, cw.h, gate.h, ba.h, sc.h, ong.h, y, nheads)
        c.finish()
    return nc, c


def _moba_g(self, qT, kT, v, esel, pbias, triw, y, nheads=2, col_off=0, bigs=None):
    c, nc = self.c, self.nc
    scale = 128 ** -0.5
    NEG = -1e30
    Eb = c.sb([32, 32 * 128], BF16, "mo_esel")
    c.dma("pool", [], [Eb], lambda e: e.dma_start(out=Eb[:, :], in_=esel))
    pbt = c.sb([128, 256], F32, "mo_pb")
    c.dma("sp", [], [pbt], lambda e: e.dma_start(out=pbt[:, :], in_=pbias.partition_broadcast(128)))
    trb = c.sb([128, 4, 512], BF16, "mo_tri")
    c.dma("pool", [], [trb], lambda e: e.dma_start(out=trb[:, :, :], in_=triw.rearrange("j p n -> p j n")))
    zeros = c.sb([128, 32], F32, "mo_z")
    c.dve([], [zeros], lambda: nc.vector.memset(zeros[:, :], 0.0))
    zb = c.sb([128, 512], BF16, "mo_zb")
    c.dve([], [zb], lambda: nc.vector.memset(zb[:, :], 0.0))
    stg = [c.sb([128, 1024], F32, "mo_stg") for _ in range(2)]
    gall = c.sb([128, 32, 16], F32, "mo_gall")
    qb = c.sb([128, T], BF16, "mo_qb")
    kb = c.sb([128, T], BF16, "mo_kb")
    vaug = c.sb([128, 32, 132], BF16, "mo_v")
    kmT = c.sb([128, 16], F32, "mo_km")
    pobank = [self.pb[0], self.pb[0], self.pb[1], self.pb[1]]
    pooff = [0, 256, 0, 256]
    self.prings["mo"] = [[2, 3], 0]
    nps = lambda: self.nps("mo")
    for h in range(nheads):
        c.dma("pool", [], [vaug], lambda e: e.dma_start(
            out=vaug[:, :, 0:128], in_=v[:, h * 128:(h + 1) * 128].rearrange("(r p) d -> p r d", p=128)))
        c.dve([], [vaug], lambda: nc.vector.memset(vaug[:, :, 128:129], 1.0))
        for ch in range(4):
            sg = stg[ch % 2]
            cs = slice(ch * 1024, (ch + 1) * 1024)
            c.dma("sp", [], [sg], lambda e: e.dma_start(out=sg[:, :], in_=kT[h][:, cs]))
            c.dve([sg], [kb], lambda: nc.vector.tensor_copy(kb[:, cs], sg[:, :]))
            c.dve([sg], [kmT], lambda: nc.vector.tensor_reduce(
                out=kmT[:, ch * 4:(ch + 1) * 4], in_=sg[:, :].rearrange("p (n k) -> p n k", k=256), axis=AX.X, op=ALU.add))
            yield
        c.dve([kmT], [kmT], lambda: nc.vector.tensor_scalar(kmT[:, :], kmT[:, :], 1.0 / 256, None, ALU.mult))
        for ch in range(4):
            sg = stg[ch % 2]
            cs = slice(ch * 1024, (ch + 1) * 1024)
            c.dma("sp", [], [sg], lambda e: e.dma_start(out=sg[:, :], in_=qT[h][:, cs]))
            c.act([sg], [qb], lambda: nc.scalar.copy(out=qb[:, cs], in_=sg[:, :]))
            for j8 in range(8):
                qt_ = ch * 8 + j8
                pg = nps()
                c.pe([sg, kmT], [pg], lambda pg=pg: nc.tensor.matmul(
                    pg[:, :16], sg[:, j8 * 128:(j8 + 1) * 128], kmT[:, :], start=True, stop=True))
                c.dve([pg], [gall], lambda pg=pg: nc.vector.tensor_copy(gall[:, qt_, :], pg[:, :16]))
            yield
        def setup_g(qc, biasT):
            for j in range(4):
                qt = 4 * qc + j
                blk = qt // 2
                qcols = slice(qt * 128, (qt + 1) * 128)
                b32 = self.ring("mo_b32", [128, 32], F32, 2)
                mx = self.ring("mo_mx", [128, 16], F32, 2)
                nkc = qt // 4 + 1
                for kc in range(nkc):
                    pm = nps()
                    c.pe([qb, kb], [pm], lambda pm=pm, kc=kc: nc.tensor.matmul(
                        pm[:, :], qb[:, qcols], kb[:, kc * 512:(kc + 1) * 512], start=True, stop=True))
                    c.dve([pm], [mx], lambda pm=pm, kc=kc: nc.vector.reduce_max(mx[:, kc:kc + 1], pm[:, :], axis=AX.X))
                c.dve([mx], [mx], lambda: nc.vector.reduce_max(mx[:, 8:9], mx[:, 0:nkc], axis=AX.X))
                c.dve([], [b32], lambda: nc.vector.memset(b32[:, :], NEG))
                c.dve([mx, zeros], [b32], lambda: nc.vector.tensor_scalar(
                    b32[:, 0:qt + 1], zeros[:, 0:qt + 1], mx[:, 8:9], None, ALU.subtract))
                if blk > 0:
                    selb = self.ring("mo_selb", [128, 16], F32, 2)
                    if blk > 3:
                        gm = self.ring("mo_gm", [128, 16], F32, 2)
                        c.dve([gall, pbt], [gm], lambda: nc.vector.tensor_tensor(
                            gm[:, :], gall[:, qt, :], pbt[:, blk * 16:(blk + 1) * 16], ALU.add))
                        t8 = self.ring("mo_t8", [128, 8], F32, 2)
                        c.dve([gm], [t8], lambda: nc.vector.max(out=t8[:, :], in_=gm[:, :]))
                        c.dve([gm, t8], [selb], lambda: nc.vector.tensor_scalar(
                            selb[:, :], gm[:, :], t8[:, 2:3], None, ALU.is_ge))
                        c.dve([selb], [selb], lambda: nc.vector.tensor_scalar(
                            selb[:, :], selb[:, :], -NEG, NEG, ALU.mult, ALU.add))
                    else:
                        c.dve([], [selb], lambda: nc.vector.memset(selb[:, :], 0.0))
                    b32v = b32[:, 0:2 * blk].rearrange("p (n two) -> p n two", two=2)
                    for e2 in range(2):
                        c.dve([b32, selb], [b32], lambda e2=e2: nc.vector.tensor_tensor(
                            b32v[:, :, e2], b32v[:, :, e2], selb[:, 0:blk], ALU.add))
                pT = nps()
                c.pe([b32, self.K], [pT], lambda pT=pT: nc.tensor.transpose(pT[0:32, 0:128], b32[:, :], self.identf))
                c.act([pT], [biasT], lambda pT=pT: nc.scalar.copy(out=biasT[:, j * 128:(j + 1) * 128], in_=pT[0:32, 0:128]))
                yield

        bT = {0: self.ring("mo_biasT", [32, 512], BF16, 2)}
        yield from setup_g(0, bT[0])
        for qc in range(8):
            biasT = bT.pop(qc)
            nxt = None
            if qc + 1 < 8:
                bT[qc + 1] = self.ring("mo_biasT", [32, 512], BF16, 2)
                nxt = setup_g(qc + 1, bT[qc + 1])
            qcs = slice(qc * 512, (qc + 1) * 512)
            ns = 4 * qc + 4
            for bk in (self.pb[0], self.pb[1]):
                c.pe([zb], [bk], lambda bk=bk: nc.tensor.matmul(bk[:, :], zb[:, 0:128], zb[:, :], start=True, stop=True))
            def score(s):
                ps = nps()
                jd = s - 4 * qc
                c.pe([kb, qb], [ps], lambda: nc.tensor.matmul(
                    ps[:, :], kb[:, s * 128:(s + 1) * 128], qb[:, qcs], start=True, stop=False))
                c.pe([Eb, biasT], [ps], lambda: nc.tensor.matmul(
                    ps[:, :], Eb[:, s * 128:(s + 1) * 128], biasT[:, :], start=False, stop=(jd < 0)))
                if jd >= 0:
                    c.pe([self.Kb, trb], [ps], lambda: nc.tensor.matmul(
                        ps[:, :], self.identb, trb[:, jd, :], start=False, stop=True))
                PT = self.ring("mo_PT", [128, 512], BF16, 3)
                c.act([ps], [PT], lambda: nc.scalar.activation(out=PT[:, :], in_=ps[:, :], func=AF.Exp, scale=scale))
                return PT
            PTn = score(0)
            for s in range(ns):
                PT = PTn
                if s + 1 < ns:
                    PTn = score(s + 1)
                for j in range(4):
                    qt = 4 * qc + j
                    if s <= qt:
                        c.pe([PT, vaug], [pobank[j]], lambda j=j: nc.tensor.matmul(
                            pobank[j][:, pooff[j]:pooff[j] + 129], PT[:, j * 128:(j + 1) * 128], vaug[:, s, 0:129],
                            start=False, stop=(s == qt), skip_group_check=True))
                if nxt is not None and s % 2 == 1:
                    try:
                        next(nxt)
                    except StopIteration:
                        nxt = None
                yield
            if nxt is not None:
                for _ in nxt:
                    yield
            for j in range(4):
                qt = 4 * qc + j
                rc = self.ring("mo_rc", [128, 1], F32, 3)
                ot = self.ring("mo_ot", [128, 128], F32, 3)
                c.dve([pobank[j]], [rc], lambda j=j: nc.vector.reciprocal(rc[:, :], pobank[j][:, pooff[j] + 128:pooff[j] + 129]))
                c.dve([pobank[j], rc], [ot], lambda j=j: nc.vector.tensor_scalar(
                    ot[:, :], pobank[j][:, pooff[j]:pooff[j] + 128], rc[:, 0:1], None, ALU.mult))
                c.dma("sp", [ot], [y], lambda e: e.dma_start(
                    out=y[qt * 128:(qt + 1) * 128, col_off + h * 128:col_off + (h + 1) * 128], in_=ot[:, :]))


def _moba(self, *a, **kw):
    for _ in _moba_g(self, *a, **kw):
        pass


MixProg.moba = _moba
MixProg.moba_g = _moba_g


def moba_consts():
    esel = np.zeros((32, 32, 128), np.float32)
    for s in range(32):
        esel[s, s, :] = 1.0
    pb = np.zeros((16, 16), np.float32)
    for blk in range(16):
        pb[blk, blk:] = -1e30
    i = np.arange(128)
    tri = np.where(i[:, None] <= i[None, :], 0.0, -30000.0).astype(np.float32)
    triw = np.zeros((4, 128, 512), np.float32)
    for j in range(4):
        triw[j, :, j * 128:(j + 1) * 128] = tri
    return esel.reshape(32, 32 * 128), pb.reshape(256), triw


def build_moba(nheads=2):
    nc = new_nc()
    es = ExitStack()
    with es:
        c = Ctx(nc, es)
        di = lambda n, sh: c.dram(n, sh, F32, "ExternalInput")
        cst = di("cst", [5, 128, 128])
        qT = di("mqT", [nheads, 128, T]); kT = di("mkT", [nheads, 128, T]); v = di("mv", [T, nheads * 128])
        esel = di("esel", [32, 32 * 128]); pbias = di("pbias", [256]); triw = di("triw", [4, 128, 512])
        y = c.dram("y", [T, nheads * 128], F32, "ExternalOutput")
        mp = MixProg(c, cst.h)
        mp.moba(qT.h, kT.h, v.h, esel.h, pbias.h, triw.h, y, nheads)
        c.finish()
    return nc, c


def build_mix_even():
    nc = new_nc()
    es = ExitStack()
    with es:
        c = Ctx(nc, es)
        di = lambda n, sh: c.dram(n, sh, F32, "ExternalInput")
        cst = di("cst", [5, 128, 128])
        qT = di("mqT", [2, 128, T]); kT = di("mkT", [2, 128, T]); v = di("mv", [T, 256])
        esel = di("esel", [32, 32 * 128]); pbias = di("pbias", [256]); triw = di("triw", [4, 128, 512])
        qkvT = di("qkvT", [3, 2, 128, T]); cw = di("cw", [3, 2, 128, 4]); gate = di("gate", [T, 256])
        ba = di("ba", [2, 128, 64]); sc = di("sc", [2, 2]); ong = di("ong", [128])
        y = c.dram("y", [T, 512], F32, "ExternalOutput")
        mp = MixProg(c, cst.h)
        bigs = None
        mp.prings["dn"] = [[4, 5, 6, 7], 0]
        lockstep([mp.moba_g(qT.h, kT.h, v.h, esel.h, pbias.h, triw.h, y, 2, 0, None),
                  mp.gdn_g(qkvT.h, cw.h, gate.h, ba.h, sc.h, ong.h, y, 2, 256, bigs, "dn")])
        c.finish()
    return nc, c


_PROGS = {}


def _prog(key, fn):
    if key not in _PROGS:
        _PROGS[key] = fn()[0]
    return _PROGS[key]


def _run(nc, maps):
    maps = [{k: np.ascontiguousarray(v, dtype=np.float32) for k, v in m.items()} for m in maps]
    return run_bass_kernel_spmd(nc, maps, core_ids=list(range(NCORE))).results


def kernel(x, mem, norm_g, mem_norm_g, final_norm_g, ffn_w_in, ffn_w_out, ab_w_in, ab_conv_w, ab_a_log,
           ab_dt_bias, ab_o_norm_g, ab_w_out, c_w_in, c_lb_logits, c_o_norm_g, c_w_out, x_w_q, x_w_kv, x_w_o):
    f = lambda a: np.asarray(a, dtype=np.float32)
    x, mem, norm_g = f(x), f(mem), f(norm_g)
    ident = np.eye(128, dtype=np.float32)
    cst = mix_consts()
    esel, pbias, triw = moba_consts()
    xs = x.reshape(NCORE, TPC, D)
    depth = 4

    def pre_inputs(l):
        kind = "even" if l % 2 == 0 else "odd"
        w_in = f(ab_w_in[l // 2]) if kind == "even" else f(c_w_in[l // 2])
        return kind, dict(g0=norm_g[l, 0], g1=norm_g[l, 1], f0_in=f(ffn_w_in[l, 0]), f0_out=f(ffn_w_out[l, 0]), w_in=w_in)

    def mix_inputs(l):
        w_mo = f(ab_w_out[l // 2]) if l % 2 == 0 else f(c_w_out[l // 2])
        return dict(w_mo=w_mo, mem_g=f(mem_norm_g), w_q=f(x_w_q[l]), w_kv=f(x_w_kv[l]), w_o=f(x_w_o[l]),
                    g2=norm_g[l, 2], g3=norm_g[l, 3], f1_in=f(ffn_w_in[l, 1]), f1_out=f(ffn_w_out[l, 1]))

    kind, pin = pre_inputs(0)
    res = _run(_prog(("tok", False, kind, False), lambda: build_tok(False, kind, False)),
               [dict(pin, x=xs[i], ident=ident) for i in range(NCORE)])
    out = None
    for l in range(depth):
        xcur = [r["xo"] for r in res]
        pf = [np.concatenate([res[b * 4 + q]["pf"] for q in range(4)], axis=1) for b in range(B)]
        pt = [np.concatenate([res[b * 4 + q]["pt"] for q in range(4)], axis=0) for b in range(B)]
        yfull = np.zeros((B, T, D), np.float32)
        if l % 2 == 0:
            e = l // 2
            convw = f(ab_conv_w[e])
            maps = []
            for i in range(NCORE):
                b, hp = i // 4, i % 4
                hds = [2 * hp, 2 * hp + 1]
                P = pf[b]
                m = dict(cst=cst, esel=esel, pbias=pbias, triw=triw)
                m["mqT"] = np.stack([P[h * 128:(h + 1) * 128] for h in hds])
                m["mkT"] = np.stack([P[1024 + h * 128:1024 + (h + 1) * 128] for h in hds])
                m["mv"] = pt[b][:, 2 * hp * 128:(2 * hp + 2) * 128]
                m["qkvT"] = np.stack([np.stack([P[2048 + xi * 1024 + h * 128:2048 + xi * 1024 + (h + 1) * 128] for h in hds])
                                      for xi in range(3)])
                m["cw"] = np.stack([np.stack([convw[:, xi * 1024 + h * 128:xi * 1024 + (h + 1) * 128].T for h in hds])
                                    for xi in range(3)])
                m["gate"] = pt[b][:, 1024 + 2 * hp * 128:1024 + (2 * hp + 2) * 128]
                ba = np.zeros((2, 128, 64), np.float32)
                for hh, h in enumerate(hds):
                    ba[hh, :, :32] = pt[b][:, 2048 + h].reshape(32, 128).T
                    ba[hh, :, 32:] = pt[b][:, 2056 + h].reshape(32, 128).T
                m["ba"] = ba
                m["sc"] = np.stack([f(ab_a_log[e])[hds], f(ab_dt_bias[e])[hds]], axis=1)
                m["ong"] = f(ab_o_norm_g[e])
                maps.append(m)
            rb = _run(_prog("mix_even", build_mix_even), maps)
            for i in range(NCORE):
                b, hp = i // 4, i % 4
                yfull[b][:, 2 * hp * 128:(2 * hp + 2) * 128] = rb[i]["y"][:, :256]
                yfull[b][:, 1024 + 2 * hp * 128:1024 + (2 * hp + 2) * 128] = rb[i]["y"][:, 256:]
        else:
            o = l // 2
            lgt = f(c_lb_logits)
            coef = np.zeros(4, np.float32)
            coef[1:l + 1] = 1.0
            maps = []
            for i in range(NCORE):
                b, hp = i // 4, i % 4
                cs = slice(hp * 512, (hp + 1) * 512)
                m = dict(cst=cst, coef=coef, ong=f(c_o_norm_g[o]))
                m["qT"] = pf[b][cs].reshape(4, 128, T)
                m["f"] = pt[b][:, cs]
                m["iv"] = pt[b][:, 2048 + hp * 512:2048 + (hp + 1) * 512]
                m["gg"] = pt[b][:, 4096 + hp * 512:4096 + (hp + 1) * 512]
                m["lg"] = lgt[:, cs].reshape(4, 4, 128).transpose(1, 2, 0)
                maps.append(m)
            rb = _run(_prog("mix_odd", lambda: build_hgrn2(4)), maps)
            for i in range(NCORE):
                b, hp = i // 4, i % 4
                yfull[b][:, hp * 512:(hp + 1) * 512] = rb[i]["y"]
        ys = yfull.reshape(NCORE, TPC, D)
        mems = [mem[i // 4] for i in range(NCORE)]
        min_ = mix_inputs(l)
        if l + 1 < depth:
            kind, pin = pre_inputs(l + 1)
            res = _run(_prog(("tok", True, kind, False), lambda: build_tok(True, kind, False)),
                       [dict(min_, **pin, x=xcur[i], y=ys[i], mem=mems[i], ident=ident) for i in range(NCORE)])
        else:
            res = _run(_prog(("tok", True, None, True), lambda: build_tok(True, None, True)),
                       [dict(min_, gf=f(final_norm_g), x=xcur[i], y=ys[i], mem=mems[i], ident=ident) for i in range(NCORE)])
            out = np.stack([r["out"] for r in res]).reshape(B, T, D)
    return out.astype(np.float32)
```

```python
import numpy as np
from contextlib import ExitStack
import concourse.bass as bass
import concourse.mybir as mybir
from concourse.bass_utils import run_bass_kernel_spmd

F32 = mybir.dt.float32
BF16 = mybir.dt.bfloat16
F32R = mybir.dt.float32r
AF = mybir.ActivationFunctionType
ALU = mybir.AluOpType
AX = mybir.AxisListType

D = 2048
DFF = 5504
T = 4096
B = 2
NCORE = 8
TPC = 1024
NTT = TPC // 128
EPS = 1e-6
AB_IN = 7184


class Tl:
    def __init__(self, h, name, nsub=1, psum=False):
        self.h = h
        self.name = name
        self.nsub = nsub
        self.psum = psum

    def __getitem__(self, k):
        return self.h[k]


class Ctx:
    SAME_ENG_SYNC = True

    def __init__(self, nc, es):
        self.nc = nc
        self.es = es
        self.eng = {"pe": nc.tensor, "act": nc.scalar, "dve": nc.vector, "pool": nc.gpsimd, "sp": nc.sync}
        self.sem = {}
        self.cnt = {}
        for e in ("pe", "act", "dve", "pool"):
            self.sem[e] = es.enter_context(nc.semaphore("s_" + e))
            self.cnt[e] = 0
        self.waited = {e: {} for e in self.eng}
        self.dq = {}
        for q in ("sp", "pool", "act"):
            n = 12 if q != "act" else 4
            self.dq[q] = {"sems": [es.enter_context(nc.semaphore("d_%s%d" % (q, i))) for i in range(n)],
                          "val": [0] * n, "i": 0}
        self.lastw = {}
        self.readers = {}
        self.psum_names = set()
        self.uid = 0
        self.ninstr = 0

    def sb(self, shape, dt, name=None, nsub=1):
        self.uid += 1
        name = (name or "t") + "_%d" % self.uid
        h = self.es.enter_context(self.nc.sbuf_tensor(name, list(shape), dt))
        return Tl(h, name, nsub)

    def ps(self, shape, dt, name=None):
        self.uid += 1
        name = (name or "p") + "_%d" % self.uid
        h = self.es.enter_context(self.nc.psum_tensor(name, list(shape), dt))
        self.psum_names.add(name)
        return Tl(h, name, 1, True)

    def dram(self, name, shape, dt, kind):
        h = self.nc.dram_tensor(name, list(shape), dt, kind=kind)
        return Tl(h.ap(), name, 1)

    def _keys(self, specs):
        ks = []
        for s in specs:
            if s is None:
                continue
            if isinstance(s, tuple):
                t, i = s
                ks.append((t.name, i))
            else:
                for i in range(s.nsub):
                    ks.append((s.name, i))
        return ks

    def _wait(self, e, tok):
        sem, val, te, sid = tok
        if te == e and (e in ("pe", "sp") or not self.SAME_ENG_SYNC):
            return
        w = self.waited[e]
        if w.get(sid, 0) >= val:
            return
        self.eng[e].wait_ge(sem, val)
        w[sid] = val

    def _deps(self, e, r, w):
        rk = self._keys(r)
        wk = self._keys(w)
        toks = {}
        def add(tok):
            if tok is None:
                return
            sid = tok[3]
            if sid not in toks or toks[sid][1] < tok[1]:
                toks[sid] = tok
        for k in rk:
            add(self.lastw.get(k))
            if k[0] in self.psum_names and e != "pe":
                for tok in self.readers.get(k, {}).values():
                    if tok[2] != "pe":
                        add(tok)
        for k in wk:
            add(self.lastw.get(k))
            for tok in self.readers.get(k, {}).values():
                add(tok)
        for tok in toks.values():
            self._wait(e, tok)
        return rk, wk

    def _commit(self, tok, rk, wk):
        for k in wk:
            self.lastw[k] = tok
            self.readers[k] = {}
        for k in rk:
            d = self.readers.setdefault(k, {})
            sid = tok[3]
            if sid not in d or d[sid][1] < tok[1]:
                d[sid] = tok

    def op(self, e, r, w, fn):
        rk, wk = self._deps(e, r, w)
        ins = fn()
        self.cnt[e] += 1
        ins.then_inc(self.sem[e], 1)
        self.ninstr += 1
        self._commit((self.sem[e], self.cnt[e], e, "c_" + e), rk, wk)
        return ins

    def pe(self, r, w, fn):
        return self.op("pe", r, w, fn)

    def act(self, r, w, fn):
        return self.op("act", r, w, fn)

    def dve(self, r, w, fn):
        return self.op("dve", r, w, fn)

    def pool(self, r, w, fn):
        return self.op("pool", r, w, fn)

    def dma(self, q, r, w, fn):
        rk, wk = self._deps(q, r, w)
        dq = self.dq[q]
        i = dq["i"]
        dq["i"] = (i + 1) % len(dq["sems"])
        sem = dq["sems"][i]
        sid = "d_%s%d" % (q, i)
        if dq["val"][i] > 0 and self.waited[q].get(sid, 0) < dq["val"][i]:
            self.eng[q].wait_ge(sem, dq["val"][i])
            self.waited[q][sid] = dq["val"][i]
        ins = fn(self.eng[q])
        ins.then_inc(sem, 16)
        dq["val"][i] += 16
        self.ninstr += 1
        self._commit((sem, dq["val"][i], "dma", sid), rk, wk)
        return ins

    def finish(self):
        for q, dq in self.dq.items():
            for i, sem in enumerate(dq["sems"]):
                if dq["val"][i] > 0:
                    self.nc.sync.wait_ge(sem, dq["val"][i])
        for e in ("pe", "act", "dve", "pool"):
            if self.cnt[e] > 0:
                self.nc.sync.wait_ge(self.sem[e], self.cnt[e])


class TokProg:
    def __init__(self, c):
        self.c = c
        nc = c.nc
        self.nc = nc
        self.ident_f = c.sb([128, 128], F32, "identf")
        self.ident_b = c.sb([128, 128], BF16, "identb")
        self.psb = [c.ps([128, 512], F32, "psb") for _ in range(6)]
        self.pst = [c.ps([128, 1024], BF16, "pst") for _ in range(2)]
        self.psi = 0
        self.pti = 0
        self.junk = c.sb([128, D], BF16, "junk")
        self.hn = [c.sb([128, D], BF16, "hn") for _ in range(2)]
        self.hni = 0
        self.gbc = c.sb([128, D], F32, "gbc")
        self.st = c.sb([128, 8], F32, "stat")
        self.junk2 = c.sb([128, D], F32, "junk2")
        self.stg = [c.sb([128, 512], F32, "stg") for _ in range(2)]
        self.stgi = 0
        self.wri = 0
        self.evi = 0

    def load_ident(self, ident_dram):
        c = self.c
        c.dma("sp", [], [self.ident_f], lambda e: e.dma_start(out=self.ident_f[:], in_=ident_dram[:, :]))
        c.dve([self.ident_f], [self.ident_b], lambda: self.nc.vector.tensor_copy(self.ident_b[:], self.ident_f[:]))

    def nps(self):
        p = self.psb[self.psi % len(self.psb)]
        self.psi += 1
        return p

    def npt(self):
        p = self.pst[self.pti % len(self.pst)]
        self.pti += 1
        return p

    def load_gain(self, g_ap):
        c = self.c
        c.dma("sp", [], [self.gbc], lambda e: e.dma_start(out=self.gbc[:], in_=g_ap.partition_broadcast(128)))

    def rmsnorm_rows(self, xt, out_ap_tile, out_ap, width=D, gbc=None):
        c, nc = self.c, self.nc
        gbc = gbc or self.gbc
        st = self.st
        c.act([xt], [self.junk, st], lambda: nc.scalar.activation(
            out=self.junk[:, :width], in_=xt[:, :width], func=AF.Square, accum_out=st[:, 0:1]))
        c.dve([st], [st], lambda: nc.vector.tensor_scalar(
            st[:, 1:2], st[:, 0:1], 1.0 / width, EPS, ALU.mult, ALU.add))
        c.act([st], [st], lambda: nc.scalar.sqrt(st[:, 3:4], st[:, 1:2]))
        c.dve([st], [st], lambda: nc.vector.reciprocal(st[:, 2:3], st[:, 3:4]))
        c.dve([xt, st, gbc], [out_ap_tile], lambda: nc.vector.scalar_tensor_tensor(
            out=out_ap, in0=xt[:, :width], scalar=st[:, 2:3], in1=gbc[:, :width], op0=ALU.mult, op1=ALU.mult))

    def norm_T(self, xt, hT, t):
        c, nc = self.c, self.nc
        hn = self.hn[self.hni % 2]
        self.hni += 1
        self.rmsnorm_rows(xt, hn, hn[:, :])
        for half in range(2):
            pt = self.npt()
            for kk in range(8):
                k = half * 8 + kk
                c.pe([hn, self.ident_b], [pt], lambda k=k, kk=kk: nc.tensor.transpose(
                    pt[:, kk * 128:(kk + 1) * 128], hn[:, k * 128:(k + 1) * 128], self.ident_b[:]))
            src = pt[:, :].rearrange("p (k n) -> p k n", k=8)
            dst = hT[:, half * 8:(half + 1) * 8, t * 128:(t + 1) * 128]
            if half == 0:
                c.act([pt], [(hT, t)], lambda: nc.scalar.copy(out=dst, in_=src))
            else:
                c.dve([pt], [(hT, t)], lambda: nc.vector.tensor_copy(dst, src))

    def ffn_alloc(self):
        c = self.c
        self.wa = [c.sb([128, 16, 256], BF16, "wa") for _ in range(2)]
        self.wb = [c.sb([128, 16, 256], BF16, "wb") for _ in range(2)]
        self.wo = c.sb([128, 4, D], BF16, "wo")
        self.actT = c.sb([128, 4, TPC], BF16, "actT")
        self.sA = [c.sb([128, 512], F32, "sA") for _ in range(2)]
        self.sAi = 0

    def ffn(self, xs, hT, w_in, w_out):
        c, nc = self.c, self.nc
        win = w_in.rearrange("(k p) n -> p k n", p=128)
        wout = w_out.rearrange("(j p) n -> p j n", p=128)
        NCH = DFF // 128
        hus = [(j, min(2, NCH - j)) for j in range(0, NCH, 2)]

        def load_hu(i):
            j0, n = hus[i]
            wa, wb = self.wa[i % 2], self.wb[i % 2]
            c.dma("pool", [], [wa], lambda e: e.dma_start(out=wa[:, :, :n * 128], in_=win[:, :, j0 * 128:(j0 + n) * 128]))
            c.dma("pool", [], [wb], lambda e: e.dma_start(out=wb[:, :, :n * 128], in_=win[:, :, DFF + j0 * 128:DFF + (j0 + n) * 128]))

        def load_wo(u):
            j0 = u * 4
            n = min(4, NCH - j0)
            c.dma("pool", [], [self.wo], lambda e: e.dma_start(out=self.wo[:, :n, :], in_=wout[:, j0:j0 + n, :]))

        load_hu(0)
        for i, (j0, n) in enumerate(hus):
            u = i // 2
            if i + 1 < len(hus):
                load_hu(i + 1)
            if i % 2 == 0:
                load_wo(u)
            wa, wb = self.wa[i % 2], self.wb[i % 2]
            for jj in range(n):
                j4 = (i % 2) * 2 + jj
                for th in range(2):
                    pA, pB = self.nps(), self.nps()
                    for (pp, ww) in ((pA, wa), (pB, wb)):
                        for k in range(16):
                            c.pe([ww, hT], [pp], lambda pp=pp, ww=ww, k=k: nc.tensor.matmul(
                                pp[:, :], ww[:, k, jj * 128:(jj + 1) * 128], hT[:, k, th * 512:(th + 1) * 512],
                                start=(k == 0), stop=(k == 15)))
                    sA = self.sA[self.sAi % 2]
                    self.sAi += 1
                    c.act([pA], [sA], lambda: nc.scalar.activation(out=sA[:, :], in_=pA[:, :], func=AF.Silu))
                    c.dve([sA, pB], [self.actT], lambda: nc.vector.tensor_tensor(
                        self.actT[:, j4, th * 512:(th + 1) * 512], sA[:, :], pB[:, :], ALU.mult))
            if i % 2 == 1 or i == len(hus) - 1:
                nj = min(4, NCH - u * 4)
                for cb in range(4):
                    for t in range(NTT):
                        pp = self.nps()
                        for j4 in range(nj):
                            c.pe([self.actT, self.wo], [pp], lambda j4=j4, pp=pp: nc.tensor.matmul(
                                pp[:, :], self.actT[:, j4, t * 128:(t + 1) * 128], self.wo[:, j4, cb * 512:(cb + 1) * 512],
                                start=(j4 == 0), stop=(j4 == nj - 1)))
                        xs_t = xs[t]
                        c.dve([pp, xs_t], [xs_t], lambda pp=pp, xs_t=xs_t: nc.vector.scalar_tensor_tensor(
                            out=xs_t[:, cb * 512:(cb + 1) * 512], in0=pp[:, :], scalar=0.5,
                            in1=xs_t[:, cb * 512:(cb + 1) * 512], op0=ALU.mult, op1=ALU.add))


    def wring(self):
        r = [self.wa[0], self.wb[0], self.wa[1], self.wb[1]]
        w = r[self.wri % 4]
        self.wri += 1
        return w

    def stage(self):
        s = self.stg[self.stgi % len(self.stg)]
        self.stgi += 1
        return s

    def evac(self, ps_ap, ps_t, dst_ap, dst_t):
        c, nc = self.c, self.nc
        self.evi += 1
        if self.evi % 2 == 0:
            c.act([ps_t], [dst_t], lambda: nc.scalar.copy(out=dst_ap, in_=ps_ap))
        else:
            c.dve([ps_t], [dst_t], lambda: nc.vector.tensor_copy(dst_ap, ps_ap))

    def plain_T(self, yt, hT, t):
        c, nc = self.c, self.nc
        hn = self.hn[self.hni % 2]
        self.hni += 1
        c.dve([yt], [hn], lambda: nc.vector.tensor_copy(hn[:, :], yt[:, :]))
        self._T16(hn, hT, t)

    def _T16(self, hn, hT, t, nk=16):
        c, nc = self.c, self.nc
        for half in range((nk + 7) // 8):
            pt = self.npt()
            n8 = min(8, nk - half * 8)
            for kk in range(n8):
                k = half * 8 + kk
                c.pe([hn, self.ident_b], [pt], lambda k=k, kk=kk: nc.tensor.transpose(
                    pt[:, kk * 128:(kk + 1) * 128], hn[:, k * 128:(k + 1) * 128], self.ident_b[:]))
            src = pt[:, :n8 * 128].rearrange("p (k n) -> p k n", k=n8)
            dst = hT[:, half * 8:half * 8 + n8, t * 128:(t + 1) * 128]
            self.evac(src, pt, dst, (hT, t))

    def inproj(self, hT, W, segs, pf, pt):
        c, nc = self.c, self.nc
        Wv = W.rearrange("(k p) n -> p k n", p=128)
        blocks = []
        for (c0, c1, mode, off) in segs:
            cc = c0
            while cc < c1:
                n = min(256, c1 - cc)
                blocks.append((cc, n, mode, off + cc - c0))
                cc += n

        def load(bi):
            cc, n, mode, off = blocks[bi]
            w = self.wring()
            c.dma("pool", [], [w], lambda e: e.dma_start(out=w[:, :, :n], in_=Wv[:, :, cc:cc + n]))
            return w
        wnext = load(0)
        for bi, (cc, n, mode, off) in enumerate(blocks):
            w = wnext
            if bi + 1 < len(blocks):
                wnext = load(bi + 1)
            if mode == "fm":
                for j in range(n // 128):
                    for th in range(2):
                        pp = self.nps()
                        for k in range(16):
                            c.pe([w, hT], [pp], lambda k=k, pp=pp: nc.tensor.matmul(
                                pp[:, :], w[:, k, j * 128:(j + 1) * 128], hT[:, k, th * 512:(th + 1) * 512],
                                start=(k == 0), stop=(k == 15)))
                        sg = self.stage()
                        self.evac(pp[:, :], pp, sg[:, :], sg)
                        c.dma("sp", [sg], [pf], lambda e, sg=sg: e.dma_start(
                            out=pf[off + j * 128:off + (j + 1) * 128, th * 512:(th + 1) * 512], in_=sg[:, :]))
            else:
                for t in range(NTT):
                    pp = self.nps()
                    for k in range(16):
                        c.pe([w, hT], [pp], lambda k=k, pp=pp: nc.tensor.matmul(
                            pp[:, :n], hT[:, k, t * 128:(t + 1) * 128], w[:, k, :n],
                            start=(k == 0), stop=(k == 15)))
                    sg = self.stage()
                    self.evac(pp[:, :n], pp, sg[:, :n], sg)
                    c.dma("sp", [sg], [pt], lambda e, sg=sg: e.dma_start(
                        out=pt[t * 128:(t + 1) * 128, off:off + n], in_=sg[:, :n]))

    def linear_res(self, xs, hT, W, N=D):
        c, nc = self.c, self.nc
        Wv = W.rearrange("(k p) n -> p k n", p=128)
        nb = N // 256

        def load(bi):
            w = self.wring()
            c.dma("pool", [], [w], lambda e: e.dma_start(out=w[:, :, :], in_=Wv[:, :, bi * 256:(bi + 1) * 256]))
            return w
        wnext = load(0)
        for bi in range(nb):
            w = wnext
            if bi + 1 < nb:
                wnext = load(bi + 1)
            for t in range(NTT):
                pp = self.nps()
                for k in range(16):
                    c.pe([w, hT], [pp], lambda k=k, pp=pp: nc.tensor.matmul(
                        pp[:, :256], hT[:, k, t * 128:(t + 1) * 128], w[:, k, :],
                        start=(k == 0), stop=(k == 15)))
                xt = xs[t]
                c.dve([pp, xt], [xt], lambda pp=pp, xt=xt: nc.vector.tensor_tensor(
                    xt[:, bi * 256:(bi + 1) * 256], pp[:, :256], xt[:, bi * 256:(bi + 1) * 256], ALU.add))

    def xattn(self, xs, hT, mem, mem_g, w_q, w_kv, w_o):
        c, nc = self.c, self.nc
        scale = 128 ** -0.5
        memT = c.sb([128, 16, 256], BF16, "memT", nsub=2)
        self.load_gain(mem_g)
        mt = self.junk2
        for t in range(2):
            c.dma("sp", [], [mt], lambda e, t=t: e.dma_start(out=mt[:, :], in_=mem[t * 128:(t + 1) * 128, :]))
            hn = self.hn[self.hni % 2]
            self.hni += 1
            self.rmsnorm_rows(mt, hn, hn[:, :])
            self._T16(hn, memT, t)
        kT = c.sb([128, 4, 256], BF16, "xkT")
        vv = c.sb([128, 2, 512], BF16, "xv")
        qT = self.actT
        Wkv = w_kv.rearrange("(k p) n -> p k n", p=128)
        Wq = w_q.rearrange("(k p) n -> p k n", p=128)
        for bi in range(2):
            w = self.wring()
            c.dma("pool", [], [w], lambda e, w=w: e.dma_start(out=w[:, :, :], in_=Wkv[:, :, bi * 256:(bi + 1) * 256]))
            for j in range(2):
                pp = self.nps()
                for k in range(16):
                    c.pe([w, memT], [pp], lambda k=k, pp=pp, w=w: nc.tensor.matmul(
                        pp[:, :256], w[:, k, j * 128:(j + 1) * 128], memT[:, k, :], start=(k == 0), stop=(k == 15)))
                self.evac(pp[:, :256], pp, kT[:, bi * 2 + j, :], kT)
        for bi in range(2):
            w = self.wring()
            c.dma("pool", [], [w], lambda e, w=w: e.dma_start(out=w[:, :, :], in_=Wkv[:, :, 512 + bi * 256:512 + (bi + 1) * 256]))
            for t in range(2):
                pp = self.nps()
                for k in range(16):
                    c.pe([w, memT], [pp], lambda k=k, pp=pp, w=w: nc.tensor.matmul(
                        pp[:, :256], memT[:, k, t * 128:(t + 1) * 128], w[:, k, :], start=(k == 0), stop=(k == 15)))
                self.evac(pp[:, :256], pp, vv[:, t, bi * 256:(bi + 1) * 256], vv)
        for bi in range(2):
            w = self.wring()
            c.dma("pool", [], [w], lambda e, w=w: e.dma_start(out=w[:, :, :], in_=Wq[:, :, bi * 256:(bi + 1) * 256]))
            for j in range(2):
                for th in range(2):
                    pp = self.nps()
                    for k in range(16):
                        c.pe([w, hT], [pp], lambda k=k, pp=pp, w=w: nc.tensor.matmul(
                            pp[:, :], w[:, k, j * 128:(j + 1) * 128], hT[:, k, th * 512:(th + 1) * 512],
                            start=(k == 0), stop=(k == 15)))
                    self.evac(pp[:, :], pp, qT[:, bi * 2 + j, th * 512:(th + 1) * 512], qT)
        Wo = w_o.rearrange("(k p) n -> p k n", p=128)
        c.dma("pool", [], [self.wo], lambda e: e.dma_start(out=self.wo[:, :, :], in_=Wo[:, :, :]))
        Ps = [[c.sb([128, 256], BF16, "xP") for _ in range(4)]] * 2
        PTs = [c.sb([128, 4, 256], BF16, "xPT")] * 2
        osbs = [c.sb([128, 512], BF16, "xo")] * 2
        oTs = [c.sb([128, 4, 128], BF16, "xoT")] * 2
        sts = [[c.sb([128, 8], F32, "xst") for _ in range(4)] for _ in range(2)]
        for t in range(NTT):
            P, PT, osb, oT, st = Ps[t % 2], PTs[t % 2], osbs[t % 2], oTs[t % 2], sts[t % 2]
            pps = [self.nps() for _ in range(4)]
            for h in range(4):
                c.pe([qT, kT], [pps[h]], lambda h=h: nc.tensor.matmul(
                    pps[h][:, :256], qT[:, h, t * 128:(t + 1) * 128], kT[:, h, :], start=True, stop=True))
            for h in range(4):
                c.dve([pps[h]], [st[h]], lambda h=h: nc.vector.reduce_max(st[h][:, 0:1], pps[h][:, :256], axis=AX.X))
                c.dve([st[h]], [st[h]], lambda h=h: nc.vector.tensor_scalar(st[h][:, 1:2], st[h][:, 0:1], -scale, None, ALU.mult))
            for h in range(4):
                c.act([pps[h], st[h]], [P[h], st[h]], lambda h=h: nc.scalar.activation(
                    out=P[h][:, :], in_=pps[h][:, :256], func=AF.Exp, bias=st[h][:, 1:2], scale=scale, accum_out=st[h][:, 2:3]))
            for h in range(4):
                c.dve([st[h]], [st[h]], lambda h=h: nc.vector.reciprocal(st[h][:, 3:4], st[h][:, 2:3]))
            ptp = self.npt()
            for h in range(4):
                for mc in range(2):
                    c.pe([P[h], self.ident_b], [ptp], lambda mc=mc, h=h: nc.tensor.transpose(
                        ptp[:, h * 256 + mc * 128:h * 256 + (mc + 1) * 128], P[h][:, mc * 128:(mc + 1) * 128], self.ident_b[:]))
            c.act([ptp], [PT], lambda: nc.scalar.copy(
                out=PT[:, :, :], in_=ptp[:, :1024].rearrange("p (k n) -> p k n", k=4)))
            po = self.nps()
            for h in range(4):
                for mc in range(2):
                    c.pe([PT, vv], [po], lambda mc=mc, h=h: nc.tensor.matmul(
                        po[:, h * 128:(h + 1) * 128], PT[:, h, mc * 128:(mc + 1) * 128], vv[:, mc, h * 128:(h + 1) * 128],
                        start=(mc == 0), stop=(mc == 1)))
            for h in range(4):
                c.dve([po, st[h]], [osb], lambda h=h: nc.vector.tensor_scalar(
                    osb[:, h * 128:(h + 1) * 128], po[:, h * 128:(h + 1) * 128], st[h][:, 3:4], None, ALU.mult))
            ptp2 = self.npt()
            for h in range(4):
                c.pe([osb, self.ident_b], [ptp2], lambda h=h: nc.tensor.transpose(
                    ptp2[:, h * 128:(h + 1) * 128], osb[:, h * 128:(h + 1) * 128], self.ident_b[:]))
            c.act([ptp2], [oT], lambda: nc.scalar.copy(
                out=oT[:, :, :], in_=ptp2[:, :512].rearrange("p (k n) -> p k n", k=4)))
            xt = xs[t]
            for cb in range(4):
                pp = self.nps()
                for h in range(4):
                    c.pe([oT, self.wo], [pp], lambda h=h, pp=pp: nc.tensor.matmul(
                        pp[:, :], oT[:, h, :], self.wo[:, h, cb * 512:(cb + 1) * 512], start=(h == 0), stop=(h == 3)))
                c.dve([pp, xt], [xt], lambda pp=pp: nc.vector.tensor_tensor(
                    xt[:, cb * 512:(cb + 1) * 512], pp[:, :], xt[:, cb * 512:(cb + 1) * 512], ALU.add))


def new_nc():
    return bass.Bass("TRN2", target_bir_lowering=False)


EVEN_SEGS = [(0, 2048, "fm", 0), (3072, 6144, "fm", 2048),
             (2048, 3072, "tm", 0), (6144, 7168, "tm", 1024), (7168, 7184, "tm", 2048)]
ODD_SEGS = [(0, 2048, "fm", 0), (2048, 8192, "tm", 0)]
NF = {"even": 5120, "odd": 2048}
NT = {"even": 2064, "odd": 6144}
NWIN = {"even": AB_IN, "odd": 8192}


def build_tok(mix, pre, final):
    nc = new_nc()
    es = ExitStack()
    with es:
        c = Ctx(nc, es)
        di = lambda n, sh: c.dram(n, sh, F32, "ExternalInput")
        x = di("x", [TPC, D])
        ident = di("ident", [128, 128])
        if mix:
            y = di("y", [TPC, D]); w_mo = di("w_mo", [D, D]); mem = di("mem", [256, D]); mem_g = di("mem_g", [D])
            w_q = di("w_q", [D, 512]); w_kv = di("w_kv", [D, 1024]); w_o = di("w_o", [512, D])
            g2 = di("g2", [D]); g3 = di("g3", [D]); f1_in = di("f1_in", [D, 2 * DFF]); f1_out = di("f1_out", [DFF, D])
        if pre:
            g0 = di("g0", [D]); g1 = di("g1", [D]); f0_in = di("f0_in", [D, 2 * DFF]); f0_out = di("f0_out", [DFF, D])
            w_in = di("w_in", [D, NWIN[pre]])
            xo = c.dram("xo", [TPC, D], F32, "ExternalOutput")
            pf = c.dram("pf", [NF[pre], TPC], F32, "ExternalOutput")
            pt = c.dram("pt", [TPC, NT[pre]], F32, "ExternalOutput")
        if final:
            gf = di("gf", [D])
            out = c.dram("out", [TPC, D], F32, "ExternalOutput")
        tp = TokProg(c)
        tp.load_ident(ident)
        tp.ffn_alloc()
        xs = [c.sb([128, D], F32, "x") for _ in range(NTT)]
        hT = c.sb([128, 16, TPC], BF16, "hT", nsub=NTT)
        for t in range(NTT):
            c.dma("sp", [], [xs[t]], lambda e, t=t: e.dma_start(out=xs[t][:, :], in_=x[t * 128:(t + 1) * 128, :]))
        if mix:
            for t in range(NTT):
                yt = tp.junk2
                c.dma("sp", [], [yt], lambda e, t=t: e.dma_start(out=yt[:, :], in_=y[t * 128:(t + 1) * 128, :]))
                tp.plain_T(yt, hT, t)
            tp.linear_res(xs, hT, w_mo.h)
            tp.load_gain(g2.h)
            for t in range(NTT):
                tp.norm_T(xs[t], hT, t)
            tp.xattn(xs, hT, mem.h, mem_g.h, w_q.h, w_kv.h, w_o.h)
            tp.load_gain(g3.h)
            for t in range(NTT):
                tp.norm_T(xs[t], hT, t)
            tp.ffn(xs, hT, f1_in.h, f1_out.h)
        if pre:
            tp.load_gain(g0.h)
            for t in range(NTT):
                tp.norm_T(xs[t], hT, t)
            tp.ffn(xs, hT, f0_in.h, f0_out.h)
            for t in range(NTT):
                c.dma("sp", [xs[t]], [xo], lambda e, t=t: e.dma_start(out=xo[t * 128:(t + 1) * 128, :], in_=xs[t][:, :]))
            tp.load_gain(g1.h)
            for t in range(NTT):
                tp.norm_T(xs[t], hT, t)
            tp.inproj(hT, w_in.h, EVEN_SEGS if pre == "even" else ODD_SEGS, pf, pt)
        if final:
            tp.load_gain(gf.h)
            for t in range(NTT):
                ot = tp.junk2
                tp.rmsnorm_rows(xs[t], ot, ot[:, :])
                c.dma("sp", [ot], [out], lambda e, t=t: e.dma_start(out=out[t * 128:(t + 1) * 128, :], in_=ot[:, :]))
        c.finish()
    return nc, c


def build_ffn_test():
    nc = new_nc()
    es = ExitStack()
    with es:
        c = Ctx(nc, es)
        x = c.dram("x", [TPC, D], F32, "ExternalInput")
        g = c.dram("g", [D], F32, "ExternalInput")
        w_in = c.dram("w_in", [D, 2 * DFF], F32, "ExternalInput")
        w_out = c.dram("w_out", [DFF, D], F32, "ExternalInput")
        ident = c.dram("ident", [128, 128], F32, "ExternalInput")
        y = c.dram("y", [TPC, D], F32, "ExternalOutput")
        tp = TokProg(c)
        tp.load_ident(ident)
        tp.ffn_alloc()
        xs = [c.sb([128, D], F32, "x") for _ in range(NTT)]
        hT = c.sb([128, 16, TPC], BF16, "hT", nsub=NTT)
        tp.load_gain(g.h)
        for t in range(NTT):
            c.dma("sp", [], [xs[t]], lambda e, t=t: e.dma_start(out=xs[t][:, :], in_=x[t * 128:(t + 1) * 128, :]))
        for t in range(NTT):
            tp.norm_T(xs[t], hT, t)
        tp.ffn(xs, hT, w_in.h, w_out.h)
        for t in range(NTT):
            c.dma("sp", [xs[t]], [y], lambda e, t=t: e.dma_start(out=y[t * 128:(t + 1) * 128, :], in_=xs[t][:, :]))
        c.finish()
    return nc, c


class MixProg:
    def __init__(self, c, cst):
        self.c = c
        nc = c.nc
        self.nc = nc
        self.K = c.sb([128, 5, 128], F32, "consts")
        c.dma("sp", [], [self.K], lambda e: e.dma_start(out=self.K[:, :, :], in_=cst.rearrange("k p n -> p k n")))
        self.identf = self.K[:, 0, :]
        self.U1 = self.K[:, 1, :]
        self.U2 = self.K[:, 2, :]
        self.ones = self.K[:, 3, :]
        self.Kb = c.sb([128, 5, 128], BF16, "constsb")
        c.dve([self.K], [self.Kb], lambda: nc.vector.tensor_copy(self.Kb[:, :, :], self.K[:, :, :]))
        self.identb = self.Kb[:, 0, :]
        self.triTb = self.Kb[:, 4, :]
        self.pb = [c.ps([128, 512], F32, "pm") for _ in range(8)]
        self.prings = {"all": [list(range(8)), 0]}
        self.rings = {}

    def nps(self, ring="all"):
        r = self.prings[ring]
        p = self.pb[r[0][r[1] % len(r[0])]]
        r[1] += 1
        return p

    def ring(self, key, shape, dt, n=2):
        if key not in self.rings:
            self.rings[key] = [[self.c.sb(shape, dt, key) for _ in range(n)], 0]
        r = self.rings[key]
        t = r[0][r[1] % n]
        r[1] += 1
        return t

    def rms_gate_out(self, o_ps, gbc, gt, y_dram, row0, col0, tag):
        for _ in self.rms_gate_out_g(o_ps, gbc, gt, y_dram, row0, col0, tag):
            pass

    def rms_gate_out_g(self, o_ps, gbc, gt, y_dram, row0, col0, tag):
        c, nc = self.c, self.nc
        st = self.ring(tag + "st", [128, 8], F32, 3)
        jk = self.ring(tag + "jk", [128, 128], F32, 2)
        c.act([o_ps], [jk, st], lambda: nc.scalar.activation(
            out=jk[:, :], in_=o_ps[:, :128], func=AF.Square, accum_out=st[:, 0:1]))
        c.dve([st], [st], lambda: nc.vector.tensor_scalar(st[:, 1:2], st[:, 0:1], 1.0 / 128, EPS, ALU.mult, ALU.add))
        yield
        c.act([st], [st], lambda: nc.scalar.sqrt(st[:, 3:4], st[:, 1:2]))
        c.dve([st], [st], lambda: nc.vector.reciprocal(st[:, 2:3], st[:, 3:4]))
        on = self.ring(tag + "on", [128, 128], F32, 2)
        c.dve([o_ps, st, gbc], [on], lambda: nc.vector.scalar_tensor_tensor(
            out=on[:, :], in0=o_ps[:, :128], scalar=st[:, 2:3], in1=gbc[:, :], op0=ALU.mult, op1=ALU.mult))
        yield
        sg = self.ring(tag + "sg", [128, 128], F32, 2)
        c.act([gt], [sg], lambda: nc.scalar.activation(out=sg[:, :], in_=gt[:, :], func=AF.Silu))
        yt = self.ring(tag + "yt", [128, 128], F32, 3)
        c.dve([on, sg], [yt], lambda: nc.vector.tensor_tensor(yt[:, :], on[:, :], sg[:, :], ALU.mult))
        c.dma("sp", [yt], [y_dram], lambda e: e.dma_start(out=y_dram[row0:row0 + 128, col0:col0 + 128], in_=yt[:, :]))

    def hgrn2(self, qT, f, iv, gg, lg, coef, ong, y, nheads=4):
        c, nc = self.c, self.nc
        assert nheads == 4
        scale = 128 ** -0.5
        NR = T // 128
        W = 512
        v3 = lambda ap: ap.rearrange("p (h n) -> p h n", h=4)
        gbc = c.sb([128, 128], F32, "hg_gbc")
        c.dma("sp", [], [gbc], lambda e: e.dma_start(out=gbc[:, :], in_=ong.partition_broadcast(128)))
        cf = c.sb([128, 4], F32, "hg_coef")
        c.dma("sp", [], [cf], lambda e: e.dma_start(out=cf[:, :], in_=coef.partition_broadcast(128)))
        lbB = c.sb([128, W], F32, "lbB")
        omlB = c.sb([128, W], F32, "omlB")
        S = c.sb([128, W], F32, "S")
        Sb = c.sb([128, W], BF16, "Sb")
        pp = self.nps()
        for h in range(4):
            lgt = c.sb([128, 4], F32, "lgt")
            c.dma("sp", [], [lgt], lambda e, h=h, lgt=lgt: e.dma_start(out=lgt[:, :], in_=lg[h]))
            st = c.sb([128, 8], F32, "lbst")
            c.dve([lgt], [st], lambda: nc.vector.reduce_max(st[:, 0:1], lgt[:, :], axis=AX.X))
            c.dve([st], [st], lambda: nc.vector.tensor_scalar(st[:, 1:2], st[:, 0:1], -1.0, None, ALU.mult))
            ee = c.sb([128, 4], F32, "lbe")
            c.act([lgt, st], [ee, st], lambda: nc.scalar.activation(
                out=ee[:, :], in_=lgt[:, :], func=AF.Exp, bias=st[:, 1:2], scale=1.0, accum_out=st[:, 2:3]))
            c.dve([st], [st], lambda: nc.vector.reciprocal(st[:, 3:4], st[:, 2:3]))
            c.dve([ee, cf], [ee], lambda: nc.vector.tensor_tensor(ee[:, :], ee[:, :], cf[:, :], ALU.mult))
            c.dve([ee], [st], lambda: nc.vector.reduce_sum(st[:, 4:5], ee[:, :], axis=AX.X))
            c.dve([st], [st], lambda: nc.vector.tensor_tensor(st[:, 5:6], st[:, 4:5], st[:, 3:4], ALU.mult))
            lbc = c.sb([128, 128], F32, "lbcB")
            c.dve([st, self.K], [lbc], lambda: nc.vector.tensor_scalar(lbc[:, :], self.ones, st[:, 5:6], None, ALU.mult))
            c.pe([lbc, self.K], [pp], lambda h=h, lbc=lbc: nc.tensor.matmul(
                pp[:, h * 128:(h + 1) * 128], lbc[:, :], self.identf, start=True, stop=True))
        c.act([pp], [lbB], lambda: nc.scalar.copy(out=lbB[:, :], in_=pp[:, :]))
        c.dve([pp], [omlB], lambda: nc.vector.tensor_scalar(omlB[:, :], pp[:, :], -1.0, 1.0, ALU.mult, ALU.add))
        c.dve([], [S], lambda: nc.vector.memset(S[:, :], 0.0))
        c.dve([], [Sb], lambda: nc.vector.memset(Sb[:, :], 0.0))
        qTv = qT.rearrange("h c t -> c h t")
        RF = lambda k, n=2: self.ring("hg_" + k, [128, W], F32, n)
        RB = lambda k, n=2: self.ring("hg_" + k, [128, W], BF16, n)

        def prep(r, out):
            rows = slice(r * 128, (r + 1) * 128)
            ft, gt, qt = RF("ft"), RF("gt", 3), RF("qt")
            vt = RB("vt", 3)
            c.dma("sp", [], [ft], lambda e: e.dma_start(out=ft[:, :], in_=f[rows, :]))
            c.dma("pool", [], [vt], lambda e: e.dma_start(out=vt[:, :], in_=iv[rows, :]))
            c.dma("sp", [], [gt], lambda e: e.dma_start(out=gt[:, :], in_=gg[rows, :]))
            c.dma("sp", [], [qt], lambda e: e.dma_start(out=v3(qt[:, :]), in_=qTv[:, :, rows]))
            sg = RF("sig")
            c.act([ft], [sg], lambda: nc.scalar.activation(out=sg[:, :], in_=ft[:, :], func=AF.Sigmoid))
            fg = RF("fg")
            c.dve([sg, omlB], [fg], lambda: nc.vector.tensor_tensor(fg[:, :], sg[:, :], omlB[:, :], ALU.mult))
            c.dve([fg, lbB], [fg], lambda: nc.vector.tensor_tensor(fg[:, :], fg[:, :], lbB[:, :], ALU.add))
            sq = RF("sq")
            c.act([qt], [sq], lambda: nc.scalar.activation(out=sq[:, :], in_=qt[:, :], func=AF.Silu))
            sgt = RF("sgt", 3)
            c.act([gt], [sgt], lambda: nc.scalar.activation(out=sgt[:, :], in_=gt[:, :], func=AF.Silu))
            logf = RF("logf")
            c.act([fg], [logf], lambda: nc.scalar.activation(out=logf[:, :], in_=fg[:, :], func=AF.Ln))
            kt = RF("kt")
            c.pool([fg], [kt], lambda: nc.gpsimd.tensor_scalar(kt[:, :], fg[:, :], -1.0, 1.0, ALU.mult, ALU.add))
            pbT, pblr, pkT = self.nps(), self.nps(), self.nps()
            for h in range(4):
                hs_ = slice(h * 128, (h + 1) * 128)
                c.pe([logf, self.K], [pbT], lambda hs_=hs_: nc.tensor.matmul(pbT[:, hs_], logf[:, hs_], self.U1, start=True, stop=True))
            c.pe([logf, self.K], [pblr], lambda: nc.tensor.matmul(pblr[:, :], self.U2, logf[:, :], start=True, stop=True))
            for h in range(4):
                hs_ = slice(h * 128, (h + 1) * 128)
                c.pe([kt, self.K], [pkT], lambda hs_=hs_: nc.tensor.transpose(pkT[:, hs_], kt[:, hs_], self.identf))
            bm = self.ring("hg_bm", [128, 4], F32, 3)
            c.dve([pbT], [bm], lambda: nc.vector.tensor_copy(bm[:, :], v3(pbT[:, :])[:, :, 63]))
            D1 = RF("D1")
            c.dve([pbT, bm], [D1], lambda: nc.vector.tensor_tensor(
                v3(D1[:, :]), v3(pbT[:, :]), bm[:, :].unsqueeze(2).to_broadcast([128, 4, 128]), ALU.subtract))
            e1, e2, e3, eb = RF("e1"), RF("e2"), RF("e3"), RF("eb")
            c.act([D1], [e1], lambda: nc.scalar.activation(out=e1[:, :], in_=D1[:, :], func=AF.Exp))
            c.act([D1], [e2], lambda: nc.scalar.activation(out=e2[:, :], in_=D1[:, :], func=AF.Exp, scale=-1.0))
            c.act([pbT], [e3], lambda: nc.scalar.activation(out=e3[:, :], in_=pbT[:, :], func=AF.Exp))
            c.act([pblr], [eb], lambda: nc.scalar.activation(out=eb[:, :], in_=pblr[:, :], func=AF.Exp))
            qtil, qhat, ktil, kdec = RB("qtil"), RB("qhat", 3), RB("ktil"), RB("kdec", 3)
            c.dve([sq, e1], [qtil], lambda: nc.vector.scalar_tensor_tensor(
                out=qtil[:, :], in0=sq[:, :], scalar=scale, in1=e1[:, :], op0=ALU.mult, op1=ALU.mult))
            c.dve([sq, e3], [qhat], lambda: nc.vector.scalar_tensor_tensor(
                out=qhat[:, :], in0=sq[:, :], scalar=scale, in1=e3[:, :], op0=ALU.mult, op1=ALU.mult))
            c.dve([pkT, e2], [ktil], lambda: nc.vector.tensor_tensor(ktil[:, :], pkT[:, :], e2[:, :], ALU.mult))
            c.pool([kt, eb], [kdec], lambda: nc.gpsimd.tensor_tensor(kdec[:, :], kt[:, :], eb[:, :], ALU.mult))
            el = self.ring("hg_el", [128, 4], F32, 3)
            c.dve([e3], [el], lambda: nc.vector.tensor_copy(el[:, :], v3(e3[:, :])[:, :, 127]))
            pa = self.nps()
            for h in range(4):
                hs_ = slice(h * 128, (h + 1) * 128)
                c.pe([ktil, qtil], [pa], lambda hs_=hs_: nc.tensor.matmul(pa[:, hs_], ktil[:, hs_], qtil[:, hs_], start=True, stop=True))
            aT = RB("aT", 3)
            c.dve([pa, self.K], [aT], lambda: nc.vector.tensor_tensor(
                v3(aT[:, :]), v3(pa[:, :]), self.U1.unsqueeze(1).to_broadcast([128, 4, 128]), ALU.mult))
            out.update(vt=vt, gt=sgt, qhat=qhat, aT=aT, kdec=kdec, el=el)

        def seq(r, p):
            vt, sgt, qhat, aT, kdec, el = (p[k] for k in ("vt", "gt", "qhat", "aT", "kdec", "el"))
            po, pS = self.nps(), self.nps()
            for h in range(4):
                hs_ = slice(h * 128, (h + 1) * 128)
                c.pe([aT, vt], [po], lambda hs_=hs_: nc.tensor.matmul(po[:, hs_], aT[:, hs_], vt[:, hs_], start=True, stop=False))
                c.pe([qhat, Sb], [po], lambda hs_=hs_: nc.tensor.matmul(po[:, hs_], qhat[:, hs_], Sb[:, hs_], start=False, stop=True))
            for h in range(4):
                hs_ = slice(h * 128, (h + 1) * 128)
                c.pe([kdec, vt], [pS], lambda hs_=hs_: nc.tensor.matmul(pS[:, hs_], kdec[:, hs_], vt[:, hs_], start=True, stop=True))
            c.dve([S, el], [S], lambda: nc.vector.tensor_tensor(
                v3(S[:, :]), v3(S[:, :]), el[:, :].unsqueeze(2).to_broadcast([128, 4, 128]), ALU.mult))
            c.dve([S, pS], [S], lambda: nc.vector.tensor_tensor(S[:, :], S[:, :], pS[:, :], ALU.add))
            c.pool([S], [Sb], lambda: nc.gpsimd.tensor_copy(Sb[:, :], S[:, :]))
            o2 = RF("o2")
            c.act([po], [o2], lambda: nc.scalar.activation(out=o2[:, :], in_=po[:, :], func=AF.Square))
            st = self.ring("hg_st", [128, 16], F32, 3)
            c.dve([o2], [st], lambda: nc.vector.tensor_reduce(out=st[:, 0:4], in_=v3(o2[:, :]), axis=AX.X, op=ALU.add))
            c.dve([st], [st], lambda: nc.vector.tensor_scalar(st[:, 4:8], st[:, 0:4], 1.0 / 128, EPS, ALU.mult, ALU.add))
            c.act([st], [st], lambda: nc.scalar.activation(out=st[:, 8:12], in_=st[:, 4:8], func=AF.Ln))
            c.act([st], [st], lambda: nc.scalar.activation(out=st[:, 12:16], in_=st[:, 8:12], func=AF.Exp, scale=-0.5))
            on = RF("on")
            c.dve([po, st], [on], lambda: nc.vector.tensor_tensor(
                v3(on[:, :]), v3(po[:, :]), st[:, 12:16].unsqueeze(2).to_broadcast([128, 4, 128]), ALU.mult))
            c.pool([on, gbc], [on], lambda: nc.gpsimd.tensor_tensor(
                v3(on[:, :]), v3(on[:, :]), gbc[:, :].unsqueeze(1).to_broadcast([128, 4, 128]), ALU.mult))
            yt = RF("yt", 3)
            c.dve([on, sgt], [yt], lambda: nc.vector.tensor_tensor(yt[:, :], on[:, :], sgt[:, :], ALU.mult))
            c.dma("sp", [yt], [y], lambda e: e.dma_start(out=y[r * 128:(r + 1) * 128, :], in_=yt[:, :]))

        P = {0: {}}
        prep(0, P[0])
        for r in range(NR):
            if r + 1 < NR:
                P[r + 1] = {}
                prep(r + 1, P[r + 1])
            seq(r, P.pop(r))


def lockstep(gens):
    gens = list(gens)
    while gens:
        nxt = []
        for g in gens:
            try:
                next(g)
                nxt.append(g)
            except StopIteration:
                pass
        gens = nxt


def mix_consts():
    i = np.arange(128)
    ident = np.eye(128, dtype=np.float32)
    U1 = (i[:, None] <= i[None, :]).astype(np.float32)
    U2 = (i[:, None] > i[None, :]).astype(np.float32)
    ones = np.ones((128, 128), np.float32)
    triT = np.where(i[:, None] <= i[None, :], 0.0, -30000.0).astype(np.float32)
    return np.stack([ident, U1, U2, ones, triT]).astype(np.float32)


def build_hgrn2(nheads=4):
    nc = new_nc()
    es = ExitStack()
    with es:
        c = Ctx(nc, es)
        di = lambda n, sh: c.dram(n, sh, F32, "ExternalInput")
        cst = di("cst", [5, 128, 128])
        qT = di("qT", [nheads, 128, T]); f = di("f", [T, nheads * 128]); iv = di("iv", [T, nheads * 128]); gg = di("gg", [T, nheads * 128])
        lg = di("lg", [nheads, 128, 4]); coef = di("coef", [4]); ong = di("ong", [128])
        y = c.dram("y", [T, nheads * 128], F32, "ExternalOutput")
        mp = MixProg(c, cst.h)
        mp.hgrn2(qT.h, f.h, iv.h, gg.h, lg.h, coef.h, ong.h, y, nheads)
        c.finish()
    return nc, c


def lockstep_g(gens):
    gens = list(gens)
    while gens:
        nxt = []
        for g in gens:
            try:
                next(g)
                nxt.append(g)
            except StopIteration:
                pass
        gens = nxt
        yield


def _gdn_g(self, qkvT, cw, gate, ba, sc, ong, y, nheads=2, col_off=0, bigs=None, pr="all"):
    c, nc = self.c, self.nc
    NR = T // 128
    I, U1, U2, ONES = self.identf, self.U1, self.U2, self.ones
    nps = lambda: self.nps(pr)
    gbc = c.sb([128, 128], F32, "dn_gbc")
    c.dma("sp", [], [gbc], lambda e: e.dma_start(out=gbc[:, :], in_=ong.partition_broadcast(128)))
    HS = []
    for h in range(nheads):
        d = {}
        d["cw"] = [c.sb([128, 4], F32, "dn_cw") for _ in range(3)]
        for xi in range(3):
            c.dma("sp", [], [d["cw"][xi]], lambda e, xi=xi: e.dma_start(out=d["cw"][xi][:, :], in_=cw[xi, h]))
        bat = c.sb([128, 64], F32, "dn_bat")
        sct = c.sb([128, 2], F32, "dn_sct")
        c.dma("sp", [], [bat], lambda e: e.dma_start(out=bat[:, :], in_=ba[h]))
        c.dma("sp", [], [sct], lambda e: e.dma_start(out=sct[:, :], in_=sc[h].partition_broadcast(128)))
        beta = c.sb([128, 32], F32, "dn_beta")
        nbeta = c.sb([128, 32], F32, "dn_nbeta")
        gg = c.sb([128, 32], F32, "dn_g")
        ea = c.sb([128, 1], F32, "dn_ea")
        c.act([bat], [beta], lambda: nc.scalar.activation(out=beta[:, :], in_=bat[:, 0:32], func=AF.Sigmoid))
        c.dve([beta], [nbeta], lambda: nc.vector.tensor_scalar(nbeta[:, :], beta[:, :], -1.0, None, ALU.mult))
        c.act([bat, sct], [gg], lambda: nc.scalar.activation(out=gg[:, :], in_=bat[:, 32:64], func=AF.Exp, bias=sct[:, 1:2], scale=1.0))
        c.dve([gg], [gg], lambda: nc.vector.tensor_scalar(gg[:, :], gg[:, :], 1.0, None, ALU.add))
        c.act([gg], [gg], lambda: nc.scalar.activation(out=gg[:, :], in_=gg[:, :], func=AF.Ln))
        c.act([sct], [ea], lambda: nc.scalar.activation(out=ea[:, :], in_=sct[:, 0:1], func=AF.Exp))
        c.dve([gg, ea], [gg], lambda: nc.vector.tensor_scalar(gg[:, :], gg[:, :], ea[:, 0:1], -1.0, ALU.mult, ALU.mult))
        S = c.sb([128, 128], F32, "dn_S")
        c.dve([], [S], lambda: nc.vector.memset(S[:, :], 0.0))
        d.update(beta=beta, nbeta=nbeta, gg=gg, S=S)
        HS.append(d)
    yield

    def prep(r, h, out):
        d = HS[h]
        beta, nbeta, gg = d["beta"], d["nbeta"], d["gg"]
        tg = "dn%d_" % h
        R_ = lambda k, n=2: self.ring(tg + k, [128, 128], F32, n)
        RR_ = lambda k, n=2: self.ring(tg + k, [128, 128], F32R, n)
        rows = slice(r * 128, (r + 1) * 128)
        gcol = gg[:, r:r + 1]
        gt = self.ring(tg + "gt", [128, 128], F32, 4)
        c.dma("sp", [], [gt], lambda e: e.dma_start(out=gt[:, :], in_=gate[rows, h * 128:(h + 1) * 128]))
        X = []
        for xi in range(3):
            rw = self.ring(tg + "raw%d" % xi, [128, 131], F32, 2)
            if r == 0:
                c.dve([], [rw], lambda: nc.vector.memset(rw[:, 0:3], 0.0))
                c.dma("sp", [], [rw], lambda e: e.dma_start(out=rw[:, 3:131], in_=qkvT[xi, h][:, 0:128]))
            else:
                c.dma("sp", [], [rw], lambda e: e.dma_start(out=rw[:, :], in_=qkvT[xi, h][:, r * 128 - 3:(r + 1) * 128]))
            cwt = d["cw"][xi]
            acc = R_("acc%d" % xi)
            eng = c.dve
            ee = nc.vector
            eng([rw, cwt], [acc], lambda: ee.tensor_scalar(acc[:, :], rw[:, 3:131], cwt[:, 3:4], None, ALU.mult))
            for sh in (1, 2, 3):
                eng([rw, cwt, acc], [acc], lambda sh=sh: ee.scalar_tensor_tensor(
                    out=acc[:, :], in0=rw[:, 3 - sh:131 - sh], scalar=cwt[:, 3 - sh:4 - sh], in1=acc[:, :],
                    op0=ALU.mult, op1=ALU.add))
            X.append(acc)
        yield
        XS = []
        for xi in range(3):
            xs_ = R_("x%d" % xi)
            c.act([X[xi]], [xs_], lambda xi=xi, xs_=xs_: nc.scalar.activation(out=xs_[:, :], in_=X[xi][:, :], func=AF.Silu))
            XS.append(xs_)
        qs, ks, vs = XS
        yield
        for xi in range(2):
            Xt = XS[xi]
            sq = R_("sq%d" % xi)
            c.act([Xt], [sq], lambda: nc.scalar.activation(out=sq[:, :], in_=Xt[:, :], func=AF.Square))
            pp = nps()
            c.pe([sq, self.K], [pp], lambda: nc.tensor.matmul(pp[:, :128], ONES, sq[:, :], start=True, stop=True))
            rn = R_("rn%d" % xi)
            c.dve([pp], [rn], lambda: nc.vector.tensor_scalar(rn[:, :], pp[:, :128], EPS, None, ALU.add))
            XS.append(rn)
        yield
        for xi in range(2):
            rn = XS[3 + xi]
            c.act([rn], [rn], lambda: nc.scalar.sqrt(rn[:, :], rn[:, :]))
            c.dve([rn], [rn], lambda: nc.vector.reciprocal(rn[:, :], rn[:, :]))
            Xt = XS[xi]
            sc_ = (128 ** -0.5) if xi == 0 else 1.0
            c.dve([Xt, rn], [Xt], lambda: nc.vector.scalar_tensor_tensor(
                out=Xt[:, :], in0=Xt[:, :], scalar=sc_, in1=rn[:, :], op0=ALU.mult, op1=ALU.mult))
        yield
        pk, pv = nps(), nps()
        c.pe([ks, self.K], [pk], lambda: nc.tensor.transpose(pk[:, :128], ks[:, :], I))
        c.pe([vs, self.K], [pv], lambda: nc.tensor.transpose(pv[:, :128], vs[:, :], I))
        ktm, Vb = R_("ktm"), RR_("Vb")
        c.act([pk], [ktm], lambda: nc.scalar.copy(out=ktm[:, :], in_=pk[:, :128]))
        c.dve([pv, beta], [Vb], lambda: nc.vector.tensor_scalar(Vb[:, :], pv[:, :128], beta[:, r:r + 1], None, ALU.mult))
        gU2, gcB = R_("gU2"), R_("gcB")
        c.pool([gg, self.K], [gU2], lambda: nc.gpsimd.tensor_scalar(gU2[:, :], U2, gcol, None, ALU.mult))
        c.pool([gg, self.K], [gcB], lambda: nc.gpsimd.tensor_scalar(gcB[:, :], ONES, gcol, None, ALU.mult))
        yield
        pD, pDT, pGB, pcol = nps(), nps(), nps(), nps()
        c.pe([gU2, self.K], [pD], lambda: nc.tensor.matmul(pD[:, :128], U1, gU2[:, :], start=True, stop=True))
        c.pe([gU2, self.K], [pDT], lambda: nc.tensor.matmul(pDT[:, :128], gU2[:, :], U1, start=True, stop=True))
        c.pe([gcB, self.K], [pGB], lambda: nc.tensor.matmul(pGB[:, :128], gcB[:, :], U1, start=True, stop=True))
        c.pe([gcB, self.K], [pcol], lambda: nc.tensor.matmul(pcol[:, 0:128], U1, gcB[:, :], start=True, stop=True))
        c.pe([gcB, self.K], [pcol], lambda: nc.tensor.matmul(pcol[:, 128:256], U2, gcB[:, :], start=True, stop=True))
        c.pe([gcB, self.K], [pcol], lambda: nc.tensor.matmul(pcol[:, 256:384], ONES, gcB[:, :], start=True, stop=True))
        ecol = self.ring(tg + "ecol", [128, 4], F32, 4)
        c.act([pcol], [ecol], lambda: nc.scalar.activation(
            out=ecol[:, 0:3], in_=pcol[:, 0:384].rearrange("p (k n) -> p k n", n=128)[:, :, 0], func=AF.Exp))
        expD, expDT, eGB = R_("expD"), R_("expDT"), R_("eGB")
        c.act([pD], [expD], lambda: nc.scalar.activation(out=expD[:, :], in_=pD[:, :128], func=AF.Exp))
        c.act([pDT], [expDT], lambda: nc.scalar.activation(out=expDT[:, :], in_=pDT[:, :128], func=AF.Exp))
        c.act([pGB], [eGB], lambda: nc.scalar.activation(out=eGB[:, :], in_=pGB[:, :128], func=AF.Exp))
        c.pool([expD, self.K], [expD], lambda: nc.gpsimd.tensor_tensor(expD[:, :], expD[:, :], U2, ALU.mult))
        c.pool([expDT, self.K], [expDT], lambda: nc.gpsimd.tensor_tensor(expDT[:, :], expDT[:, :], U1, ALU.mult))
        yield
        pKK, pQK = nps(), nps()
        c.pe([ks], [pKK], lambda: nc.tensor.matmul(pKK[:, :128], ks[:, :], ks[:, :], start=True, stop=True))
        c.pe([ks, qs], [pQK], lambda: nc.tensor.matmul(pQK[:, :128], ks[:, :], qs[:, :], start=True, stop=True))
        A = RR_("A")
        c.dve([pKK, nbeta, expD], [A], lambda: nc.vector.scalar_tensor_tensor(
            out=A[:, :], in0=pKK[:, :128], scalar=nbeta[:, r:r + 1], in1=expD[:, :], op0=ALU.mult, op1=ALU.mult))
        aqkT = R_("aqkT", 4)
        c.dve([pQK, expDT], [aqkT], lambda: nc.vector.tensor_tensor(aqkT[:, :], pQK[:, :128], expDT[:, :], ALU.mult))
        qdT = R_("qdT", 4)
        c.pool([qs, eGB], [qdT], lambda: nc.gpsimd.tensor_tensor(qdT[:, :], qs[:, :], eGB[:, :], ALU.mult))
        bcol = self.ring(tg + "bcol", [128, 1], F32, 3)
        c.dve([beta, ecol], [bcol], lambda: nc.vector.tensor_tensor(bcol[:, :], beta[:, r:r + 1], ecol[:, 0:1], ALU.mult))
        kbg, kdec = RR_("kbg"), R_("kdec", 4)
        c.dve([ktm, bcol], [kbg], lambda: nc.vector.tensor_scalar(kbg[:, :], ktm[:, :], bcol[:, 0:1], None, ALU.mult))
        c.pool([ktm, ecol], [kdec], lambda: nc.gpsimd.tensor_scalar(kdec[:, :], ktm[:, :], ecol[:, 1:2], None, ALU.mult))
        yield
        pB = nps()
        c.pe([A, self.K], [pB], lambda: nc.tensor.transpose(pB[:, :128], A[:, :].bitcast(F32), I))
        Bm = RR_("B")
        c.act([pB], [Bm], lambda: nc.scalar.copy(out=Bm[:, :], in_=pB[:, :128]))
        Rm = RR_("R", 3)
        c.dve([pB, self.K], [Rm], lambda: nc.vector.tensor_tensor(Rm[:, :], pB[:, :128], I, ALU.add))
        yield
        Aj, Bj = A, Bm
        for lvl in range(6):
            pA2 = nps()
            c.pe([Aj, Bj], [pA2], lambda Aj=Aj, Bj=Bj, pA2=pA2: nc.tensor.matmul(pA2[:, :128], Bj[:, :], Aj[:, :], start=True, stop=True))
            if lvl < 5:
                pB2 = nps()
                c.pe([Aj, Bj], [pB2], lambda Aj=Aj, Bj=Bj, pB2=pB2: nc.tensor.matmul(pB2[:, :128], Aj[:, :], Bj[:, :], start=True, stop=True))
                A2, B2 = RR_("A2", 3), RR_("B2", 3)
            IA = RR_("IA")
            c.dve([pA2, self.K], [IA], lambda IA=IA, pA2=pA2: nc.vector.tensor_tensor(IA[:, :], pA2[:, :128], I, ALU.add))
            if lvl < 5:
                c.act([pA2], [A2], lambda A2=A2, pA2=pA2: nc.scalar.copy(out=A2[:, :], in_=pA2[:, :128]))
                c.act([pB2], [B2], lambda B2=B2, pB2=pB2: nc.scalar.copy(out=B2[:, :], in_=pB2[:, :128]))
            pR = nps()
            c.pe([IA, Rm], [pR], lambda IA=IA, Rm=Rm, pR=pR: nc.tensor.matmul(pR[:, :128], IA[:, :], Rm[:, :], start=True, stop=True))
            Rn = RR_("R", 3)
            c.dve([pR], [Rn], lambda Rn=Rn, pR=pR: nc.vector.tensor_copy(Rn[:, :], pR[:, :128]))
            Rm = Rn
            if lvl < 5:
                Aj, Bj = A2, B2
            yield
        pu, pw = nps(), nps()
        c.pe([Rm, Vb], [pu], lambda: nc.tensor.matmul(pu[:, :128], Rm[:, :], Vb[:, :], start=True, stop=True))
        c.pe([Rm, kbg], [pw], lambda: nc.tensor.matmul(pw[:, :128], kbg[:, :], Rm[:, :], start=True, stop=True))
        u, wT = R_("u", 4), R_("wT", 4)
        c.act([pu], [u], lambda: nc.scalar.copy(out=u[:, :], in_=pu[:, :128]))
        c.dve([pw], [wT], lambda: nc.vector.tensor_copy(wT[:, :], pw[:, :128]))
        out.update(u=u, wT=wT, qdT=qdT, aqkT=aqkT, kdec=kdec, ecol=ecol, gt=gt)
        yield

    def seq(r, h, p):
        S = HS[h]["S"]
        tg = "dn%d_" % h
        R_ = lambda k, n=2: self.ring(tg + k, [128, 128], F32, n)
        u, wT, qdT, aqkT, kdec, ecol, gt = (p[k] for k in ("u", "wT", "qdT", "aqkT", "kdec", "ecol", "gt"))
        pvn = nps()
        c.pe([wT, S], [pvn], lambda: nc.tensor.matmul(pvn[:, :128], wT[:, :], S[:, :], start=True, stop=True))
        vn = R_("vn")
        c.dve([u, pvn], [vn], lambda: nc.vector.tensor_tensor(vn[:, :], u[:, :], pvn[:, :128], ALU.subtract))
        po = nps()
        c.pe([qdT, S], [po], lambda: nc.tensor.matmul(po[:, :128], qdT[:, :], S[:, :], start=True, stop=False))
        c.pe([aqkT, vn], [po], lambda: nc.tensor.matmul(po[:, :128], aqkT[:, :], vn[:, :], start=False, stop=True))
        pS = nps()
        c.pe([kdec, vn], [pS], lambda: nc.tensor.matmul(pS[:, :128], kdec[:, :], vn[:, :], start=True, stop=True))
        c.dve([pS, ecol, S], [S], lambda: nc.vector.scalar_tensor_tensor(
            out=S[:, :], in0=S[:, :], scalar=ecol[:, 2:3], in1=pS[:, :128], op0=ALU.mult, op1=ALU.add))
        osb = R_("osb")
        c.act([po], [osb], lambda: nc.scalar.copy(out=osb[:, :], in_=po[:, :128]))
        yield
        yield from self.rms_gate_out_g(osb, gbc, gt, y, r * 128, col_off + h * 128, tg)

    P = {}
    NB = 2

    def run_prep(b):
        gens = []
        for r in range(b * NB, (b + 1) * NB):
            for h in range(nheads):
                P[(r, h)] = {}
                gens.append(prep(r, h, P[(r, h)]))
        yield from lockstep_g(gens)
    yield from run_prep(0)
    for b in range(NR // NB):
        if b + 1 < NR // NB:
            yield from run_prep(b + 1)
        for r in range(b * NB, (b + 1) * NB):
            yield from lockstep_g([seq(r, h, P.pop((r, h))) for h in range(nheads)])


def _gdn(self, *a, **kw):
    for _ in _gdn_g(self, *a, **kw):
        pass


MixProg.gdn_g = _gdn_g
MixProg.gdn = _gdn
MixProg.gdn = _gdn


def build_gdn(nheads=2):
    nc = new_nc()
    es = ExitStack()
    with es:
        c = Ctx(nc, es)
        di = lambda n, sh: c.dram(n, sh, F32, "ExternalInput")
        cst = di("cst", [5, 128, 128])
        qkvT = di("qkvT", [3, nheads, 128, T]); cw = di("cw", [3, nheads, 128, 4]); gate = di("gate", [T, nheads * 128])
        ba = di("ba", [nheads, 128, 64]); sc = di("sc", [nheads, 2]); ong = di("ong", [128])
        y = c.dram("y", [T, nheads * 128], F32, "ExternalOutput")
        mp = MixProg(c, cst.h)
        mp.gdn(qkvT.h, cw.h, gate.h, ba.h, sc.h, ong.h, y, nheads)
        c.finish()
    return nc, c


def _moba_g(self, qT, kT, v, esel, pbias, triw, y, nheads=2, col_off=0, bigs=None):
    c, nc = self.c, self.nc
    scale = 128 ** -0.5
    NEG = -1e30
    Eb = c.sb([32, 32 * 128], BF16, "mo_esel")
    c.dma("pool", [], [Eb], lambda e: e.dma_start(out=Eb[:, :], in_=esel))
    pbt = c.sb([128, 256], F32, "mo_pb")
    c.dma("sp", [], [pbt], lambda e: e.dma_start(out=pbt[:, :], in_=pbias.partition_broadcast(128)))
    trb = c.sb([128, 4, 512], BF16, "mo_tri")
    c.dma("pool", [], [trb], lambda e: e.dma_start(out=trb[:, :, :], in_=triw.rearrange("j p n -> p j n")))
    zeros = c.sb([128, 32], F32, "mo_z")
    c.dve([], [zeros], lambda: nc.vector.memset(zeros[:, :], 0.0))
    zb = c.sb([128, 512], BF16, "mo_zb")
    c.dve([], [zb], lambda: nc.vector.memset(zb[:, :], 0.0))
    stg = [c.sb([128, 1024], F32, "mo_stg") for _ in range(2)]
    gall = c.sb([128, 32, 16], F32, "mo_gall")
    qb = c.sb([128, T], BF16, "mo_qb")
    kb = c.sb([128, T], BF16, "mo_kb")
    vaug = c.sb([128, 32, 132], BF16, "mo_v")
    kmT = c.sb([128, 16], F32, "mo_km")
    pobank = [self.pb[0], self.pb[0], self.pb[1], self.pb[1]]
    pooff = [0, 256, 0, 256]
    self.prings["mo"] = [[2, 3], 0]
    nps = lambda: self.nps("mo")
    for h in range(nheads):
        c.dma("pool", [], [vaug], lambda e: e.dma_start(
            out=vaug[:, :, 0:128], in_=v[:, h * 128:(h + 1) * 128].rearrange("(r p) d -> p r d", p=128)))
        c.dve([], [vaug], lambda: nc.vector.memset(vaug[:, :, 128:129], 1.0))
        for ch in range(4):
            sg = stg[ch % 2]
            cs = slice(ch * 1024, (ch + 1) * 1024)
            c.dma("sp", [], [sg], lambda e: e.dma_start(out=sg[:, :], in_=kT[h][:, cs]))
            c.dve([sg], [kb], lambda: nc.vector.tensor_copy(kb[:, cs], sg[:, :]))
            c.dve([sg], [kmT], lambda: nc.vector.tensor_reduce(
                out=kmT[:, ch * 4:(ch + 1) * 4], in_=sg[:, :].rearrange("p (n k) -> p n k", k=256), axis=AX.X, op=ALU.add))
            yield
        c.dve([kmT], [kmT], lambda: nc.vector.tensor_scalar(kmT[:, :], kmT[:, :], 1.0 / 256, None, ALU.mult))
        for ch in range(4):
            sg = stg[ch % 2]
            cs = slice(ch * 1024, (ch + 1) * 1024)
            c.dma("sp", [], [sg], lambda e: e.dma_start(out=sg[:, :], in_=qT[h][:, cs]))
            c.act([sg], [qb], lambda: nc.scalar.copy(out=qb[:, cs], in_=sg[:, :]))
            for j8 in range(8):
                qt_ = ch * 8 + j8
                pg = nps()
                c.pe([sg, kmT], [pg], lambda pg=pg: nc.tensor.matmul(
                    pg[:, :16], sg[:, j8 * 128:(j8 + 1) * 128], kmT[:, :], start=True, stop=True))
                c.dve([pg], [gall], lambda pg=pg: nc.vector.tensor_copy(gall[:, qt_, :], pg[:, :16]))
            yield
        def setup_g(qc, biasT):
            for j in range(4):
                qt = 4 * qc + j
                blk = qt // 2
                qcols = slice(qt * 128, (qt + 1) * 128)
                b32 = self.ring("mo_b32", [128, 32], F32, 2)
                mx = self.ring("mo_mx", [128, 16], F32, 2)
                nkc = qt // 4 + 1
                for kc in range(nkc):
                    pm = nps()
                    c.pe([qb, kb], [pm], lambda pm=pm, kc=kc: nc.tensor.matmul(
                        pm[:, :], qb[:, qcols], kb[:, kc * 512:(kc + 1) * 512], start=True, stop=True))
                    c.dve([pm], [mx], lambda pm=pm, kc=kc: nc.vector.reduce_max(mx[:, kc:kc + 1], pm[:, :], axis=AX.X))
                c.dve([mx], [mx], lambda: nc.vector.reduce_max(mx[:, 8:9], mx[:, 0:nkc], axis=AX.X))
                c.dve([], [b32], lambda: nc.vector.memset(b32[:, :], NEG))
                c.dve([mx, zeros], [b32], lambda: nc.vector.tensor_scalar(
                    b32[:, 0:qt + 1], zeros[:, 0:qt + 1], mx[:, 8:9], None, ALU.subtract))
                if blk > 0:
                    selb = self.ring("mo_selb", [128, 16], F32, 2)
                    if blk > 3:
                        gm = self.ring("mo_gm", [128, 16], F32, 2)
                        c.dve([gall, pbt], [gm], lambda: nc.vector.tensor_tensor(
                            gm[:, :], gall[:, qt, :], pbt[:, blk * 16:(blk + 1) * 16], ALU.add))
                        t8 = self.ring("mo_t8", [128, 8], F32, 2)
                        c.dve([gm], [t8], lambda: nc.vector.max(out=t8[:, :], in_=gm[:, :]))
                        c.dve([gm, t8], [selb], lambda: nc.vector.tensor_scalar(
                            selb[:, :], gm[:, :], t8[:, 2:3], None, ALU.is_ge))
                        c.dve([selb], [selb], lambda: nc.vector.tensor_scalar(
                            selb[:, :], selb[:, :], -NEG, NEG, ALU.mult, ALU.add))
                    else:
                        c.dve([], [selb], lambda: nc.vector.memset(selb[:, :], 0.0))
                    b32v = b32[:, 0:2 * blk].rearrange("p (n two) -> p n two", two=2)
                    for e2 in range(2):
                        c.dve([b32, selb], [b32], lambda e2=e2: nc.vector.tensor_tensor(
                            b32v[:, :, e2], b32v[:, :, e2], selb[:, 0:blk], ALU.add))
                pT = nps()
                c.pe([b32, self.K], [pT], lambda pT=pT: nc.tensor.transpose(pT[0:32, 0:128], b32[:, :], self.identf))
                c.act([pT], [biasT], lambda pT=pT: nc.scalar.copy(out=biasT[:, j * 128:(j + 1) * 128], in_=pT[0:32, 0:128]))
                yield

        bT = {0: self.ring("mo_biasT", [32, 512], BF16, 2)}
        yield from setup_g(0, bT[0])
        for qc in range(8):
            biasT = bT.pop(qc)
            nxt = None
            if qc + 1 < 8:
                bT[qc + 1] = self.ring("mo_biasT", [32, 512], BF16, 2)
                nxt = setup_g(qc + 1, bT[qc + 1])
            qcs = slice(qc * 512, (qc + 1) * 512)
            ns = 4 * qc + 4
            for bk in (self.pb[0], self.pb[1]):
                c.pe([zb], [bk], lambda bk=bk: nc.tensor.matmul(bk[:, :], zb[:, 0:128], zb[:, :], start=True, stop=True))
            def score(s):
                ps = nps()
                jd = s - 4 * qc
                c.pe([kb, qb], [ps], lambda: nc.tensor.matmul(
                    ps[:, :], kb[:, s * 128:(s + 1) * 128], qb[:, qcs], start=True, stop=False))
                c.pe([Eb, biasT], [ps], lambda: nc.tensor.matmul(
                    ps[:, :], Eb[:, s * 128:(s + 1) * 128], biasT[:, :], start=False, stop=(jd < 0)))
                if jd >= 0:
                    c.pe([self.Kb, trb], [ps], lambda: nc.tensor.matmul(
                        ps[:, :], self.identb, trb[:, jd, :], start=False, stop=True))
                PT = self.ring("mo_PT", [128, 512], BF16, 3)
                c.act([ps], [PT], lambda: nc.scalar.activation(out=PT[:, :], in_=ps[:, :], func=AF.Exp, scale=scale))
                return PT
            PTn = score(0)
            for s in range(ns):
                PT = PTn
                if s + 1 < ns:
                    PTn = score(s + 1)
                for j in range(4):
                    qt = 4 * qc + j
                    if s <= qt:
                        c.pe([PT, vaug], [pobank[j]], lambda j=j: nc.tensor.matmul(
                            pobank[j][:, pooff[j]:pooff[j] + 129], PT[:, j * 128:(j + 1) * 128], vaug[:, s, 0:129],
                            start=False, stop=(s == qt), skip_group_check=True))
                if nxt is not None and s % 2 == 1:
                    try:
                        next(nxt)
                    except StopIteration:
                        nxt = None
                yield
            if nxt is not None:
                for _ in nxt:
                    yield
            for j in range(4):
                qt = 4 * qc + j
                rc = self.ring("mo_rc", [128, 1], F32, 3)
                ot = self.ring("mo_ot", [128, 128], F32, 3)
                c.dve([pobank[j]], [rc], lambda j=j: nc.vector.reciprocal(rc[:, :], pobank[j][:, pooff[j] + 128:pooff[j] + 129]))
                c.dve([pobank[j], rc], [ot], lambda j=j: nc.vector.tensor_scalar(
                    ot[:, :], pobank[j][:, pooff[j]:pooff[j] + 128], rc[:, 0:1], None, ALU.mult))
                c.dma("sp", [ot], [y], lambda e: e.dma_start(
                    out=y[qt * 128:(qt + 1) * 128, col_off + h * 128:col_off + (h + 1) * 128], in_=ot[:, :]))


def _moba(self, *a, **kw):
    for _ in _moba_g(self, *a, **kw):
        pass


MixProg.moba = _moba
MixProg.moba_g = _moba_g


def moba_consts():
    esel = np.zeros((32, 32, 128), np.float32)
    for s in range(32):
        esel[s, s, :] = 1.0
    pb = np.zeros((16, 16), np.float32)
    for blk in range(16):
        pb[blk, blk:] = -1e30
    i = np.arange(128)
    tri = np.where(i[:, None] <= i[None, :], 0.0, -30000.0).astype(np.float32)
    triw = np.zeros((4, 128, 512), np.float32)
    for j in range(4):
        triw[j, :, j * 128:(j + 1) * 128] = tri
    return esel.reshape(32, 32 * 128), pb.reshape(256), triw


def build_moba(nheads=2):
    nc = new_nc()
    es = ExitStack()
    with es:
        c = Ctx(nc, es)
        di = lambda n, sh: c.dram(n, sh, F32, "ExternalInput")
        cst = di("cst", [5, 128, 128])
        qT = di("mqT", [nheads, 128, T]); kT = di("mkT", [nheads, 128, T]); v = di("mv", [T, nheads * 128])
        esel = di("esel", [32, 32 * 128]); pbias = di("pbias", [256]); triw = di("triw", [4, 128, 512])
        y = c.dram("y", [T, nheads * 128], F32, "ExternalOutput")
        mp = MixProg(c, cst.h)
        mp.moba(qT.h, kT.h, v.h, esel.h, pbias.h, triw.h, y, nheads)
        c.finish()
    return nc, c


def build_mix_even():
    nc = new_nc()
    es = ExitStack()
    with es:
        c = Ctx(nc, es)
        di = lambda n, sh: c.dram(n, sh, F32, "ExternalInput")
        cst = di("cst", [5, 128, 128])
        qT = di("mqT", [2, 128, T]); kT = di("mkT", [2, 128, T]); v = di("mv", [T, 256])
        esel = di("esel", [32, 32 * 128]); pbias = di("pbias", [256]); triw = di("triw", [4, 128, 512])
        qkvT = di("qkvT", [3, 2, 128, T]); cw = di("cw", [3, 2, 128, 4]); gate = di("gate", [T, 256])
        ba = di("ba", [2, 128, 64]); sc = di("sc", [2, 2]); ong = di("ong", [128])
        y = c.dram("y", [T, 512], F32, "ExternalOutput")
        mp = MixProg(c, cst.h)
        bigs = None
        mp.prings["dn"] = [[4, 5, 6, 7], 0]
        lockstep([mp.moba_g(qT.h, kT.h, v.h, esel.h, pbias.h, triw.h, y, 2, 0, None),
                  mp.gdn_g(qkvT.h, cw.h, gate.h, ba.h, sc.h, ong.h, y, 2, 256, bigs, "dn")])
        c.finish()
    return nc, c


_PROGS = {}


def _prog(key, fn):
    if key not in _PROGS:
        _PROGS[key] = fn()[0]
    return _PROGS[key]


def _run(nc, maps):
    maps = [{k: np.ascontiguousarray(v, dtype=np.float32) for k, v in m.items()} for m in maps]
    return run_bass_kernel_spmd(nc, maps, core_ids=list(range(NCORE))).results


def kernel(x, mem, norm_g, mem_norm_g, final_norm_g, ffn_w_in, ffn_w_out, ab_w_in, ab_conv_w, ab_a_log,
           ab_dt_bias, ab_o_norm_g, ab_w_out, c_w_in, c_lb_logits, c_o_norm_g, c_w_out, x_w_q, x_w_kv, x_w_o):
    f = lambda a: np.asarray(a, dtype=np.float32)
    x, mem, norm_g = f(x), f(mem), f(norm_g)
    ident = np.eye(128, dtype=np.float32)
    cst = mix_consts()
    esel, pbias, triw = moba_consts()
    xs = x.reshape(NCORE, TPC, D)
    depth = 4

    def pre_inputs(l):
        kind = "even" if l % 2 == 0 else "odd"
        w_in = f(ab_w_in[l // 2]) if kind == "even" else f(c_w_in[l // 2])
        return kind, dict(g0=norm_g[l, 0], g1=norm_g[l, 1], f0_in=f(ffn_w_in[l, 0]), f0_out=f(ffn_w_out[l, 0]), w_in=w_in)

    def mix_inputs(l):
        w_mo = f(ab_w_out[l // 2]) if l % 2 == 0 else f(c_w_out[l // 2])
        return dict(w_mo=w_mo, mem_g=f(mem_norm_g), w_q=f(x_w_q[l]), w_kv=f(x_w_kv[l]), w_o=f(x_w_o[l]),
                    g2=norm_g[l, 2], g3=norm_g[l, 3], f1_in=f(ffn_w_in[l, 1]), f1_out=f(ffn_w_out[l, 1]))

    kind, pin = pre_inputs(0)
    res = _run(_prog(("tok", False, kind, False), lambda: build_tok(False, kind, False)),
               [dict(pin, x=xs[i], ident=ident) for i in range(NCORE)])
    out = None
    for l in range(depth):
        xcur = [r["xo"] for r in res]
        pf = [np.concatenate([res[b * 4 + q]["pf"] for q in range(4)], axis=1) for b in range(B)]
        pt = [np.concatenate([res[b * 4 + q]["pt"] for q in range(4)], axis=0) for b in range(B)]
        yfull = np.zeros((B, T, D), np.float32)
        if l % 2 == 0:
            e = l // 2
            convw = f(ab_conv_w[e])
            maps = []
            for i in range(NCORE):
                b, hp = i // 4, i % 4
                hds = [2 * hp, 2 * hp + 1]
                P = pf[b]
                m = dict(cst=cst, esel=esel, pbias=pbias, triw=triw)
                m["mqT"] = np.stack([P[h * 128:(h + 1) * 128] for h in hds])
                m["mkT"] = np.stack([P[1024 + h * 128:1024 + (h + 1) * 128] for h in hds])
                m["mv"] = pt[b][:, 2 * hp * 128:(2 * hp + 2) * 128]
                m["qkvT"] = np.stack([np.stack([P[2048 + xi * 1024 + h * 128:2048 + xi * 1024 + (h + 1) * 128] for h in hds])
                                      for xi in range(3)])
                m["cw"] = np.stack([np.stack([convw[:, xi * 1024 + h * 128:xi * 1024 + (h + 1) * 128].T for h in hds])
                                    for xi in range(3)])
                m["gate"] = pt[b][:, 1024 + 2 * hp * 128:1024 + (2 * hp + 2) * 128]
                ba = np.zeros((2, 128, 64), np.float32)
                for hh, h in enumerate(hds):
                    ba[hh, :, :32] = pt[b][:, 2048 + h].reshape(32, 128).T
                    ba[hh, :, 32:] = pt[b][:, 2056 + h].reshape(32, 128).T
                m["ba"] = ba
                m["sc"] = np.stack([f(ab_a_log[e])[hds], f(ab_dt_bias[e])[hds]], axis=1)
                m["ong"] = f(ab_o_norm_g[e])
                maps.append(m)
            rb = _run(_prog("mix_even", build_mix_even), maps)
            for i in range(NCORE):
                b, hp = i // 4, i % 4
                yfull[b][:, 2 * hp * 128:(2 * hp + 2) * 128] = rb[i]["y"][:, :256]
                yfull[b][:, 1024 + 2 * hp * 128:1024 + (2 * hp + 2) * 128] = rb[i]["y"][:, 256:]
        else:
            o = l // 2
            lgt = f(c_lb_logits)
            coef = np.zeros(4, np.float32)
            coef[1:l + 1] = 1.0
            maps = []
            for i in range(NCORE):
                b, hp = i // 4, i % 4
                cs = slice(hp * 512, (hp + 1) * 512)
                m = dict(cst=cst, coef=coef, ong=f(c_o_norm_g[o]))
                m["qT"] = pf[b][cs].reshape(4, 128, T)
                m["f"] = pt[b][:, cs]
                m["iv"] = pt[b][:, 2048 + hp * 512:2048 + (hp + 1) * 512]
                m["gg"] = pt[b][:, 4096 + hp * 512:4096 + (hp + 1) * 512]
                m["lg"] = lgt[:, cs].reshape(4, 4, 128).transpose(1, 2, 0)
                maps.append(m)
            rb = _run(_prog("mix_odd", lambda: build_hgrn2(4)), maps)
            for i in range(NCORE):
                b, hp = i // 4, i % 4
                yfull[b][:, hp * 512:(hp + 1) * 512] = rb[i]["y"]
        ys = yfull.reshape(NCORE, TPC, D)
        mems = [mem[i // 4] for i in range(NCORE)]
        min_ = mix_inputs(l)
        if l + 1 < depth:
            kind, pin = pre_inputs(l + 1)
            res = _run(_prog(("tok", True, kind, False), lambda: build_tok(True, kind, False)),
                       [dict(min_, **pin, x=xcur[i], y=ys[i], mem=mems[i], ident=ident) for i in range(NCORE)])
        else:
            res = _run(_prog(("tok", True, None, True), lambda: build_tok(True, None, True)),
                       [dict(min_, gf=f(final_norm_g), x=xcur[i], y=ys[i], mem=mems[i], ident=ident) for i in range(NCORE)])
            out = np.stack([r["out"] for r in res]).reshape(B, T, D)
    return out.astype(np.float32)
```
